# Optimizing a Trainium2 kernel written in Bass

```python
import jax
import jax.numpy as jnp
from jax import lax
import numpy as np

D_MODEL = 1024
BATCH = 8
SEQ = 2048
DEPTH = 2
DEC_BATCH = 128
DEC_SEQ = 1
PAST_LEN = 16384
PAGE_SIZE = 128

GLA_HEADS = 4
GLA_DK = D_MODEL // 8
GLA_DV = D_MODEL // 4
GLA_RANK = 16
GLA_GATE_NORM = 16.0
GLA_CHUNK = 32
SC_WIDTH = D_MODEL
SC_CONV_W = 3
RG_WIDTH = D_MODEL
RG_BLOCKS = 8
RG_BLOCK = RG_WIDTH // RG_BLOCKS
RG_CONV_W = 4
RG_C = 8.0
N_BRANCH = 3
RMS_EPS = 1e-6

IN_WIDTHS = (
    GLA_HEADS * GLA_DK,
    GLA_HEADS * GLA_DK,
    GLA_HEADS * GLA_DV,
    GLA_HEADS * GLA_DV,
    GLA_RANK,
    SC_WIDTH,
    SC_WIDTH,
    SC_WIDTH,
    SC_WIDTH,
    RG_WIDTH,
    RG_WIDTH,
    N_BRANCH * D_MODEL,
)
N_IN = sum(IN_WIDTHS)

kernel_name = 'hybrid_gla_shortconv_rglru_step'


def rms_norm(x, g):
    xf = x.astype(jnp.float32)
    y = xf * lax.rsqrt(jnp.mean(xf * xf, axis=-1, keepdims=True) + RMS_EPS)
    return (y * g.astype(jnp.float32)).astype(x.dtype)


def split_cols(z):
    offs = np.cumsum(np.array(IN_WIDTHS))[:-1].tolist()
    return jnp.split(z, offs, axis=-1)


def causal_dwconv(u, buf, w):
    W = w.shape[0]
    T = u.shape[1]
    full = jnp.concatenate([buf.astype(u.dtype), u], axis=1)
    y = full[:, 0:T] * w[0]
    for j in range(1, W):
        y = y + full[:, j:j + T] * w[j]
    return y, full[:, T:]


def gla_chunked(q, k, v, log_a, s0):
    B, T, H, _ = q.shape
    C = min(GLA_CHUNK, T)
    n = -(-T // C)
    pad = n * C - T

    def blocks(a):
        a = jnp.pad(a, ((0, 0), (0, pad), (0, 0), (0, 0)))
        return a.reshape(B, n, C, H, a.shape[-1]).transpose(1, 0, 3, 2, 4)

    qb, kb, vb = blocks(q), blocks(k), blocks(v)
    bb = jnp.cumsum(blocks(log_a), axis=3)
    causal = jnp.tril(jnp.ones((C, C), dtype=bool))

    def step(S, inp):
        qc, kc, vc, bc = inp
        qe = qc * jnp.exp(bc)
        ke = kc * jnp.exp(-bc)
        att = jnp.where(causal, jnp.einsum('bhtd,bhsd->bhts', qe, ke), 0.0)
        o = jnp.einsum('bhts,bhsv->bhtv', att, vc) + jnp.einsum('bhtd,bhdv->bhtv', qe, S)
        b_end = bc[:, :, -1:, :]
        kd = kc * jnp.exp(b_end - bc)
        S = jnp.exp(b_end[:, :, 0, :])[..., None] * S + jnp.einsum('bhsd,bhsv->bhdv', kd, vc)
        return S, o

    S, ob = lax.scan(step, s0, (qb, kb, vb, bb))
    o = ob.transpose(1, 0, 3, 2, 4).reshape(B, n * C, H, -1)[:, :T]
    return o, S


def rg_lru(xc, h0, w_a, b_a, w_x, b_x, lam):
    B, T, D = xc.shape
    xb = xc.reshape(B, T, RG_BLOCKS, RG_BLOCK)
    r = jax.nn.sigmoid(jnp.einsum('btni,nij->btnj', xb, w_a).reshape(B, T, D) + b_a)
    i = jax.nn.sigmoid(jnp.einsum('btni,nij->btnj', xb, w_x).reshape(B, T, D) + b_x)
    log_a = -RG_C * r.astype(jnp.float32) * jax.nn.softplus(-lam.astype(jnp.float32))
    a = jnp.exp(log_a)
    bterm = jnp.sqrt(-jnp.expm1(2.0 * log_a)) * (i * xc).astype(jnp.float32)
    bterm = bterm.at[:, 0].add(a[:, 0] * h0.astype(jnp.float32))

    def comb(left, right):
        a1, b1 = left
        a2, b2 = right
        return a1 * a2, a2 * b1 + b2

    _, h = lax.associative_scan(comb, (a, bterm), axis=1)
    return h, h[:, -1]


def mixer_block(x, c, s_gla, s_sc, s_rgc, s_rgh,
                w_ada, b_ada, norm_g, w_in, gla_w_a, gla_b_a, gla_ng, gla_wb,
                sc_cw, sc_wb, rg_cw, rg_cb, rg_wa, rg_ba, rg_wx, rg_bx, rg_lam, rg_wb,
                b_merge, w_out):
    B, T, _ = x.shape
    mod = jax.nn.silu(c) @ w_ada + b_ada
    shift, scale, gate = jnp.split(mod[:, None, :], 3, axis=-1)
    h = rms_norm(x, norm_g) * (1 + scale) + shift
    (zq, zk, zv, zg_gla, zlr, zb, zc, zh, zg_sc, zx, zg_rg, zm) = split_cols(h @ w_in)

    q = zq.reshape(B, T, GLA_HEADS, GLA_DK).astype(jnp.float32) * (GLA_DK ** -0.5)
    k = zk.reshape(B, T, GLA_HEADS, GLA_DK).astype(jnp.float32)
    v = zv.reshape(B, T, GLA_HEADS, GLA_DV).astype(jnp.float32)
    log_a = jax.nn.log_sigmoid((zlr @ gla_w_a + gla_b_a).astype(jnp.float32)) / GLA_GATE_NORM
    log_a = log_a.reshape(B, T, GLA_HEADS, GLA_DK)
    o, new_gla = gla_chunked(q, k, v, log_a, s_gla.astype(jnp.float32))
    o = rms_norm(o, gla_ng).reshape(B, T, GLA_HEADS * GLA_DV).astype(x.dtype)
    y_gla = (o * jax.nn.silu(zg_gla)) @ gla_wb

    uc, new_sc = causal_dwconv(zc * zh, s_sc, sc_cw)
    y_sc = (zb * uc * jax.nn.silu(zg_sc)) @ sc_wb

    xc, new_rgc = causal_dwconv(zx, s_rgc, rg_cw)
    xc = xc + rg_cb
    hr, new_rgh = rg_lru(xc, s_rgh, rg_wa, rg_ba, rg_wx, rg_bx, rg_lam)
    y_rg = (hr.astype(x.dtype) * jax.nn.silu(zg_rg)) @ rg_wb

    g1, g2, g3 = jnp.split(jax.nn.sigmoid(zm + b_merge), 3, axis=-1)
    out = (g1 * y_gla + g2 * y_sc + g3 * y_rg) @ w_out
    return (x + gate * out, new_gla.astype(s_gla.dtype), new_sc.astype(s_sc.dtype),
            new_rgc.astype(s_rgc.dtype), new_rgh.astype(s_rgh.dtype))


def trunk(x, c, s_gla, s_sc, s_rgc, s_rgh, final_gain, layer_params):
    n_gla, n_sc, n_rgc, n_rgh = [], [], [], []
    for l in range(DEPTH):
        x, g, sc, rc, rh = mixer_block(x, c, s_gla[l], s_sc[l], s_rgc[l], s_rgh[l],
                                       *[p[l] for p in layer_params])
        n_gla.append(g)
        n_sc.append(sc)
        n_rgc.append(rc)
        n_rgh.append(rh)
    return (rms_norm(x, final_gain), jnp.stack(n_gla), jnp.stack(n_sc),
            jnp.stack(n_rgc), jnp.stack(n_rgh))


def setup_inputs(seed: int = 0) -> dict:
    key = jax.random.key(seed)
    ks = jax.random.split(key, 32)
    f32 = jnp.float32

    def nrm(k, shape, s):
        return jax.random.normal(k, shape, f32) * s

    D = D_MODEL
    HDK = GLA_HEADS * GLA_DK
    HDV = GLA_HEADS * GLA_DV
    u = jax.random.uniform(ks[27], (DEPTH, RG_WIDTH), f32, 0.9, 0.999)
    s = u ** (1.0 / RG_C)
    rg_lambda = jnp.log(s) - jnp.log1p(-s)
    return {
        'x_prompt': nrm(ks[0], (BATCH, SEQ, D), 1.0),
        'x_sample': nrm(ks[1], (DEC_BATCH, DEC_SEQ, D), 1.0),
        'c_prompt': nrm(ks[2], (BATCH, D), 1.0),
        'c_sample': nrm(ks[3], (DEC_BATCH, D), 1.0),
        'state_gla': nrm(ks[4], (DEPTH, DEC_BATCH, GLA_HEADS, GLA_DK, GLA_DV), 0.5),
        'state_sc_conv': nrm(ks[5], (DEPTH, DEC_BATCH, SC_CONV_W - 1, SC_WIDTH), 1.0),
        'state_rg_conv': nrm(ks[6], (DEPTH, DEC_BATCH, RG_CONV_W - 1, RG_WIDTH), 1.0),
        'state_rg_h': nrm(ks[7], (DEPTH, DEC_BATCH, RG_WIDTH), 1.0),
        'w_ada': nrm(ks[8], (DEPTH, D, 3 * D), 0.5 * D ** -0.5),
        'b_ada': nrm(ks[9], (DEPTH, 3 * D), 0.02),
        'norm_gain': 1.0 + nrm(ks[10], (DEPTH, D), 0.02),
        'w_in': nrm(ks[11], (DEPTH, D, N_IN), D ** -0.5),
        'gla_w_alpha': nrm(ks[12], (DEPTH, GLA_RANK, HDK), GLA_RANK ** -0.5),
        'gla_b_alpha': nrm(ks[13], (DEPTH, HDK), 0.5),
        'gla_norm_gain': 1.0 + nrm(ks[14], (DEPTH, GLA_DV), 0.02),
        'gla_w_branch': nrm(ks[15], (DEPTH, HDV, D), HDV ** -0.5),
        'sc_conv_w': nrm(ks[16], (DEPTH, SC_CONV_W, SC_WIDTH), SC_CONV_W ** -0.5),
        'sc_w_branch': nrm(ks[17], (DEPTH, SC_WIDTH, D), SC_WIDTH ** -0.5),
        'rg_conv_w': nrm(ks[18], (DEPTH, RG_CONV_W, RG_WIDTH), RG_CONV_W ** -0.5),
        'rg_conv_b': nrm(ks[19], (DEPTH, RG_WIDTH), 0.02),
        'rg_w_a': nrm(ks[20], (DEPTH, RG_BLOCKS, RG_BLOCK, RG_BLOCK), RG_BLOCK ** -0.5),
        'rg_b_a': nrm(ks[21], (DEPTH, RG_WIDTH), 0.02),
        'rg_w_x': nrm(ks[22], (DEPTH, RG_BLOCKS, RG_BLOCK, RG_BLOCK), RG_BLOCK ** -0.5),
        'rg_b_x': nrm(ks[23], (DEPTH, RG_WIDTH), 0.02),
        'rg_lambda': rg_lambda,
        'rg_w_branch': nrm(ks[24], (DEPTH, RG_WIDTH, D), RG_WIDTH ** -0.5),
        'b_merge': nrm(ks[25], (DEPTH, N_BRANCH * D), 0.02),
        'w_out': nrm(ks[26], (DEPTH, D, D), D ** -0.5),
        'final_gain': 1.0 + nrm(ks[28], (D,), 0.02),
    }


def reference(x_prompt, x_sample, c_prompt, c_sample, state_gla, state_sc_conv, state_rg_conv,
              state_rg_h, w_ada, b_ada, norm_gain, w_in, gla_w_alpha, gla_b_alpha, gla_norm_gain,
              gla_w_branch, sc_conv_w, sc_w_branch, rg_conv_w, rg_conv_b, rg_w_a, rg_b_a, rg_w_x,
              rg_b_x, rg_lambda, rg_w_branch, b_merge, w_out, final_gain):
    layer_params = (w_ada, b_ada, norm_gain, w_in, gla_w_alpha, gla_b_alpha, gla_norm_gain,
                    gla_w_branch, sc_conv_w, sc_w_branch, rg_conv_w, rg_conv_b, rg_w_a, rg_b_a,
                    rg_w_x, rg_b_x, rg_lambda, rg_w_branch, b_merge, w_out)
    bp = x_prompt.shape[0]
    dt = x_prompt.dtype
    z_gla = jnp.zeros((DEPTH, bp, GLA_HEADS, GLA_DK, GLA_DV), dt)
    z_sc = jnp.zeros((DEPTH, bp, SC_CONV_W - 1, SC_WIDTH), dt)
    z_rgc = jnp.zeros((DEPTH, bp, RG_CONV_W - 1, RG_WIDTH), dt)
    z_rgh = jnp.zeros((DEPTH, bp, RG_WIDTH), dt)
    y_prompt, gla_p, sc_p, rgc_p, rgh_p = trunk(x_prompt, c_prompt, z_gla, z_sc, z_rgc, z_rgh,
                                                final_gain, layer_params)
    y_sample, gla_s, sc_s, rgc_s, rgh_s = trunk(x_sample, c_sample, state_gla, state_sc_conv,
                                                state_rg_conv, state_rg_h, final_gain, layer_params)
    return (y_prompt, y_sample, gla_p, sc_p, rgc_p, rgh_p, gla_s, sc_s, rgc_s, rgh_s)
```

```python
from contextlib import ExitStack
import os
import numpy as np
import concourse.bass as bass
import concourse.mybir as mybir
from concourse.bass_utils import run_bass_kernel_spmd

F32 = mybir.dt.float32
BF16 = mybir.dt.bfloat16
AF = mybir.ActivationFunctionType
ALU = mybir.AluOpType

D = 1024
L = 2
TT = 512
NT = 4
NS = 16
NV = 160
N_IN = 12304
OQ, OK_, OV, OGG, OLR = 0, 512, 1024, 2048, 3072
OB, OC, OH, OGS, OX, OGR, OM = 3088, 4112, 5136, 6160, 7184, 8208, 9232
VNG, VBADA, VGNG, VSCW, VRGW, VRGCB, VRGBA, VRGBX, VLAM, VBM, VFG = 0, 8, 32, 34, 58, 90, 98, 106, 114, 122, 146
EPS = 1e-6


class Tile:
    __slots__ = ("name", "w", "r")

    def __init__(self, name):
        self.name = name
        self.w = None
        self.r = []


class Buf:
    __slots__ = ("a", "T")

    def __init__(self, a, T):
        self.a = a
        self.T = T


class Prog:
    ENGS = ("pe", "act", "dve", "pool", "sp")

    def __init__(self, nc):
        self.nc = nc
        self.streams = {e: [] for e in self.ENGS}
        self.dma_cnt = {}
        self.sig = {e: set() for e in self.ENGS}
        self.window = {"act": 3, "dve": 3, "pool": 1 << 30}

    def _collect(self, eng, idx, reads, writes):
        deps = set()
        for t in reads:
            if t.w is not None:
                deps.add(t.w)
        for t in writes:
            if t.w is not None:
                deps.add(t.w)
            deps.update(t.r)
        out = []
        for d in deps:
            if d[0] == "e" and d[1] == eng:
                if eng in ("pe", "sp"):
                    continue
                if idx - d[2] > self.window[eng]:
                    continue
            out.append(d)
        for d in out:
            if d[0] == "e":
                self.sig[d[1]].add(d[2])
        return out

    def _mark(self, me, reads, writes):
        for t in reads:
            t.r.append(me)
            if len(t.r) > 64:
                t.r = t.r[-48:]
        for t in writes:
            t.w = me
            t.r = []

    def op(self, eng, fn, reads=(), writes=()):
        idx = len(self.streams[eng])
        deps = self._collect(eng, idx, reads, writes)
        self.streams[eng].append((fn, deps, None))
        self._mark(("e", eng, idx), reads, writes)

    def dma(self, eng, fns, key, reads=(), writes=()):
        idx = len(self.streams[eng])
        deps = self._collect(eng, idx, reads, writes)
        cnt = self.dma_cnt.get(key, 0)
        for i, fn in enumerate(fns):
            cnt += 16
            self.streams[eng].append((fn, deps if i == 0 else [], key))
        self.dma_cnt[key] = cnt
        self._mark(("s", key, cnt), reads, writes)

    def wait_all(self, eng, tiles):
        idx = len(self.streams[eng])
        deps = self._collect(eng, idx, (), tiles)
        self.streams[eng].append((None, deps, None))

    def emit(self, stack):
        nc = self.nc
        esem = {e: stack.enter_context(nc.semaphore("S_" + e)) for e in self.ENGS if e != "sp"}
        dsem = {k: stack.enter_context(nc.semaphore("D_" + str(k))) for k in self.dma_cnt}
        cnt_at = {}
        for e in self.ENGS:
            c = 0
            m = {}
            for i in range(len(self.streams[e])):
                if i in self.sig[e]:
                    c += 1
                    m[i] = c
            cnt_at[e] = m
        block = stack.enter_context(nc.Block())

        def run(eng_name):
            def body(eng):
                waited = {}
                for i, (fn, deps, key) in enumerate(self.streams[eng_name]):
                    for d in deps:
                        if d[0] == "e":
                            sem, val, k = esem[d[1]], cnt_at[d[1]][d[2]], ("e", d[1])
                        else:
                            sem, val, k = dsem[d[1]], d[2], ("s", d[1])
                        if waited.get(k, 0) >= val:
                            continue
                        waited[k] = val
                        eng.wait_ge(sem, val)
                    if fn is None:
                        continue
                    ins = fn(eng)
                    if key is not None:
                        ins.then_inc(dsem[key], 16)
                    elif i in self.sig[eng_name]:
                        ins.then_inc(esem[eng_name], 1)
            return body

        block.tensor(run("pe"))
        block.scalar(run("act"))
        block.vector(run("dve"))
        block.gpsimd(run("pool"))
        block.sync(run("sp"))


class Ring:
    def __init__(self, bufs):
        self.bufs = bufs
        self.i = 0

    def next(self):
        b = self.bufs[self.i % len(self.bufs)]
        self.i += 1
        return b


def build_program():
    nc = bass.Bass("TRN2", target_bir_lowering=False)

    def din(name, shape):
        return nc.dram_tensor(name, list(shape), F32, kind="ExternalInput").ap()

    def dout(name, shape):
        return nc.dram_tensor(name, list(shape), F32, kind="ExternalOutput").ap()

    xT = din("xT", [128, 8, NT * TT])
    xsT = din("xsT", [128, 8, NS])
    cT = din("cT", [128, 8, 1 + NS])
    sgla = din("sgla", [L, NS, 4, 128, 256])
    sscf = din("sscf", [L, 128, 8, 2, NS])
    srgcf = din("srgcf", [L, 128, 8, 3, NS])
    srghf = din("srghf", [L, 128, 8, NS])
    w_ada = din("w_ada", [L, D, 3 * D])
    w_in = din("w_in", [L, D, N_IN])
    wbr = [din("wgla", [L, D, D]), din("wsc", [L, D, D]), din("wrg", [L, D, D])]
    w_out = din("w_out", [L, D, D])
    rgwa_d = din("rgwa", [L, 8, 128, 128])
    rgwx_d = din("rgwx", [L, 8, 128, 128])
    gwa_d = din("gwa", [L, 17, 512])
    lvec_d = din("lvec", [L, 128, NV])
    consts_d = din("consts", [128, 5 * 128])

    yT = dout("yT", [128, 8, NT * TT])
    ysT = dout("ysT", [128, 8, NS])
    ogla_p = dout("ogla_p", [L, 4, 128, 256])
    osc_p = dout("osc_p", [L, 128, 8, 2])
    orgc_p = dout("orgc_p", [L, 128, 8, 3])
    orgh_p = dout("orgh_p", [L, 128, 8])
    ogla_s = dout("ogla_s", [L, NS, 4, 128, 256])
    osc_s = dout("osc_s", [L, 128, 8, 2, NS])
    orgc_s = dout("orgc_s", [L, 128, 8, 3, NS])
    orgh_s = dout("orgh_s", [L, 128, 8, NS])

    with ExitStack() as st:
        P = Prog(nc)
        STAGE = int(os.environ.get("KSTAGE", "9"))

        def sb(name, shape, dt=F32):
            t = st.enter_context(nc.sbuf_tensor("s_" + name, list(shape), dt))
            return Buf(t[:], Tile(name))

        def ps(name, shape):
            t = st.enter_context(nc.psum_tensor("p_" + name, list(shape), F32))
            return Buf(t[:], Tile(name))

        def MM(out, lhsT, rhs, start, stop, R, W):
            P.op("pe", lambda e: e.matmul(out, lhsT=lhsT, rhs=rhs, start=start, stop=stop), R, W)

        def ACT(out, in_, func, R, W, bias=None, scale=None):
            kw = {}
            if bias is not None:
                kw["bias"] = bias
            if scale is not None:
                kw["scale"] = scale
            P.op("act", lambda e: e.activation(out=out, in_=in_, func=func, **kw), R, W)

        def ACPY(out, in_, R, W):
            P.op("act", lambda e: e.copy(out=out, in_=in_), R, W)

        def VCPY(out, in_, R, W):
            P.op("dve", lambda e: e.tensor_copy(out=out, in_=in_), R, W)

        def TTo(out, a, b, op, R, W):
            P.op("dve", lambda e: e.tensor_tensor(out=out, in0=a, in1=b, op=op), R, W)

        def TS(out, a, s1, s2, op0, op1, R, W):
            if op1 is None:
                P.op("dve", lambda e: e.tensor_scalar(out=out, in0=a, scalar1=s1, scalar2=None, op0=op0), R, W)
            else:
                P.op("dve", lambda e: e.tensor_scalar(out=out, in0=a, scalar1=s1, scalar2=s2, op0=op0, op1=op1), R, W)

        def STT(out, a, s, b, op0, op1, R, W):
            P.op("dve", lambda e: e.scalar_tensor_tensor(out=out, in0=a, scalar=s, in1=b, op0=op0, op1=op1), R, W)

        def PTS(out, a, s1, s2, op0, op1, R, W):
            P.op("pool", lambda e: e.tensor_scalar(out=out, in0=a, scalar1=s1, scalar2=s2, op0=op0, op1=op1), R, W)

        def PSTT(out, a, s, b, op0, op1, R, W):
            P.op("pool", lambda e: e.scalar_tensor_tensor(out=out, in0=a, scalar=s, in1=b, op0=op0, op1=op1), R, W)

        def PCPY(out, in_, R, W):
            P.op("pool", lambda e: e.tensor_copy(out=out, in_=in_), R, W)

        def DMA(eng, key, pairs, R, W):
            fns = []
            for (o, i) in pairs:
                fns.append((lambda o, i: (lambda e: e.dma_start(out=o, in_=i)))(o, i))
            P.dma(eng, fns, key, R, W)

        MUL, ADD, MIN = ALU.mult, ALU.add, ALU.min

        consts = sb("consts", [128, 5 * 128])
        ident = consts.a[:, 0:128]
        onesf = consts.a[:, 128:256]
        triNeg = consts.a[:, 256:384]
        triUpNeg = consts.a[:, 384:512]
        maskT = consts.a[:, 512:640]
        onesb = sb("onesb", [128, 128], BF16)
        lvec = [sb(f"lvec{l}", [128, NV]) for l in range(L)]
        cneg = [sb(f"cneg{l}", [128, 8]) for l in range(L)]
        gwa = [sb(f"gwa{l}", [17, 512]) for l in range(L)]
        rgwa = [sb(f"rgwa{l}", [128, 8, 128], BF16) for l in range(L)]
        rgwx = [sb(f"rgwx{l}", [128, 8, 128], BF16) for l in range(L)]
        modT = [sb(f"modT{l}", [128, 24, 1 + NS]) for l in range(L)]
        Ap = [sb(f"Ap{l}", [128, 8]) for l in range(L)]
        As = [sb(f"As{l}", [128, 8, NS]) for l in range(L)]
        cs = sb("cs", [128, 8, 1 + NS])
        scb = sb("scb", [128, 8, 1 + NS], BF16)

        wslots = Ring([sb(f"ws{i}", [128, 8, 512], BF16) for i in range(3)])
        xt = Ring([sb(f"xt{i}", [128, 8, TT]) for i in range(2)])
        hT = sb("hT", [128, 8, TT], BF16)
        pbr0 = sb("pbr0", [128, 8, TT], BF16)
        raw1 = sb("raw1", [128, 4 * 512])
        raw2 = sb("raw2", [128, 4 * 512])
        pbr = [pbr0,
               Buf(raw1.a.bitcast(BF16).rearrange("p (k n) -> p k n", k=8), raw1.T),
               Buf(raw2.a.bitcast(BF16).rearrange("p (k n) -> p k n", k=8), raw2.T)]
        mg = sb("mg", [128, 8, TT], BF16)
        sqb = sb("sqb", [128, 4, TT], BF16)
        rstd = sb("rstd", [128, TT])
        wk = Ring([sb(f"wk{i}", [128, TT]) for i in range(4)])
        xcd = Ring([sb(f"xcd{i}", [128, TT]) for i in range(2)])
        gbuf = [sb(f"gb{i}", [128, TT]) for i in range(3)]
        zlr = sb("zlr", [17, TT])
        sp_tok = Buf(raw1.a.rearrange("p (c n) -> p c n", c=4), raw1.T)
        ekd = Buf(raw2.a.rearrange("p (c n) -> p c n", c=4), raw2.T)
        eb = [sb(f"eb{h}", [128, 4]) for h in range(4)]
        qk = sb("qk", [128, 8, TT], BF16)
        qe = [Buf(qk.a[:, h, :], qk.T) for h in range(4)]
        ke = [Buf(qk.a[:, 4 + h, :], qk.T) for h in range(4)]
        hvec = [sb(f"hvec{l}", [128, NV]) for l in range(L)]
        hc = [sb(f"hc{l}", [128, 8]) for l in range(L)]
        gqp = [sb(f"gqp{l}", [128, 8]) for l in range(L)]
        gqs = [sb(f"gqs{l}", [128, 8, NS]) for l in range(L)]
        vtok = [Buf(None, mg.T) for h in range(4)]

        def vt(h, c):
            return mg.a[:, 2 * h + c // 2, (c % 2) * 256:(c % 2) * 256 + 256]
        kdtok = [sb(f"kdtok{h}", [128, 4, 128], BF16) for h in range(4)]
        attm = Ring([sb(f"attm{i}", [128, 128], BF16) for i in range(4)])
        oT = [sb(f"oT{h}", [128, 2, TT]) for h in range(4)]
        S = [[sb(f"S{l}_{h}", [128, 256]) for h in range(4)] for l in range(L)]
        Sb = [[sb(f"Sb{l}_{h}", [128, 256], BF16) for h in range(4)] for l in range(L)]
        csc = [sb(f"csc{l}", [128, 8, 2]) for l in range(L)]
        crg = [sb(f"crg{l}", [128, 8, 3]) for l in range(L)]
        chh = [sb(f"chh{l}", [128, 8]) for l in range(L)]
        ub = Ring([sb(f"ub{i}", [128, TT + 3]) for i in range(2)])
        xcb = sb("xcb", [128, TT], BF16)
        xs = sb("xs", [128, 8, NS])
        ssc = sb("ssc", [128, 8, 2, NS])
        srgc = sb("srgc", [128, 8, 3, NS])
        srgh = sb("srgh", [128, 8, NS])
        nsc = sb("nsc", [128, 8, 2, NS])
        nrgc = sb("nrgc", [128, 8, 3, NS])
        nrgh = sb("nrgh", [128, 8, NS])
        aTs = sb("aTs", [128, 4, NS])
        qss = sb("qss", [128, 4, NS], BF16)
        kvs = sb("kvs", [NS, 4, 384], BF16)
        km4 = Ring([Buf(sqb.a[0:NS, i, :].rearrange("p (h d) -> p h d", h=4), Tile(f"km4_{i}")) for i in range(2)])
        Sin = [Buf(oT[i].a.rearrange("p a (b c) -> p (a b) c", b=2), oT[i].T) for i in range(2)]
        Snw = [Buf(oT[i].a.rearrange("p a (b c) -> p (a b) c", b=2), oT[i].T) for i in range(2, 4)]
        snbq = [Buf(qk.a[:, 2 * i:2 * i + 2, :].rearrange("p k (a v) -> p (k a) v", a=2), Tile(f"snbq{i}")) for i in range(2)]
        sgs = sb("sgs", [128, 8, NS])
        snb4 = Ring([Buf(ub.bufs[i].a[:, 0:512].bitcast(BF16).rearrange("p (h v) -> p h v", h=4), ub.bufs[i].T) for i in range(2)])
        oTs = sb("oTs", [128, 8, NS])
        hT_s = sb("hT_s", [128, 8, NS], BF16)
        pbr_s = [sb(f"pbr_s{i}", [128, 8, NS], BF16) for i in range(3)]
        mg_s = sb("mg_s", [128, 8, NS], BF16)
        rsb_s = [sb(f"rsb_s{i}", [128, NS]) for i in range(4)]
        gb_s = [sb(f"gb_s{i}", [128, NS]) for i in range(3)]
        wk_s = Ring([sb(f"wk_s{i}", [128, NS]) for i in range(4)])
        xcd_s = Ring([sb(f"xcd_s{i}", [128, NS]) for i in range(2)])
        xcb_s = sb("xcb_s", [128, NS], BF16)

        zr = Ring([ps(f"z{i}", [128, 512]) for i in range(8)])

        class _SmView:
            def next(self):
                b = zr.next()
                return Buf(b.a[:, 0:256], b.T)
        smr = _SmView()

        def wview(w, l):
            return w[l].rearrange("(k p) n -> p k n", p=128)

        NGRP = 43
        wscr = nc.dram_tensor("wscr", [L * NGRP, 128, 8 * 512], BF16).ap()
        wscr_T = [Tile(f"wscr{g}") for g in range(L * NGRP)]

        def wload(entry):
            pieces, gid, mode = entry
            slot = wslots.next()
            ntot = sum(n for (_, _, n) in pieces)
            flat = slot.a.rearrange("p k n -> p (k n)")[:, 0:8 * ntot]
            sv = Buf(flat.rearrange("p (k n) -> p k n", k=8), slot.T)
            if mode == "ld":
                DMA("sp", slot.T.name, [(flat, wscr[gid][:, 0:8 * ntot])], [wscr_T[gid]], [slot.T])
                return sv
            pairs = [(sv.a[:, :, d0:d0 + n], src) for (src, d0, n) in pieces]
            DMA("pool", slot.T.name, pairs, [], [slot.T])
            if mode == "cast+st":
                DMA("sp", "wst_" + slot.T.name, [(wscr[gid][:, 0:8 * ntot], flat)], [slot.T], [wscr_T[gid]])
            return sv

        class WQ:
            def __init__(self):
                self.plan = []
                self.issued = []
                self.pos = 0

            def add(self, pieces):
                self.plan.append([pieces, None, "cast"])

            def get(self):
                while len(self.issued) < min(len(self.plan), self.pos + 3):
                    self.issued.append(wload(self.plan[len(self.issued)]))
                s = self.issued[self.pos]
                self.pos += 1
                return s

        wq = WQ()

        def plan_layer(l):
            wi = wview(w_in, l)
            wq.add([(wi[:, :, OLR:OLR + 16], 0, 16)])
            for h in range(4):
                wq.add([(wi[:, :, OQ + 128 * h:OQ + 128 * h + 128], 0, 128),
                        (wi[:, :, OK_ + 128 * h:OK_ + 128 * h + 128], 128, 128),
                        (wi[:, :, OV + 256 * h:OV + 256 * h + 256], 256, 256)])
            for h in range(4):
                wq.add([(wi[:, :, OGG + 256 * h:OGG + 256 * h + 256], 0, 256)])
            for j in range(8):
                wq.add([(wi[:, :, o + 128 * j:o + 128 * j + 128], 128 * q, 128) for q, o in enumerate((OX, OC, OH, OB))])
                wq.add([(wi[:, :, o + 128 * j:o + 128 * j + 128], 128 * q, 128) for q, o in enumerate((OGS, OGR))])
            for i in range(8):
                wq.add([(wi[:, :, OM + 1024 * br + 128 * i:OM + 1024 * br + 128 * i + 128], 128 * br, 128) for br in range(3)])
                wq.add([(wview(wbr[br], l)[:, :, 128 * i:128 * i + 128], 128 * br, 128) for br in range(3)])
            wo = wview(w_out, l)
            for g in range(2):
                wq.add([(wo[:, :, 512 * g:512 * g + 512], 0, 512)])

        for l in range(L):
            wa = wview(w_ada, l)
            for g in range(6):
                wq.add([(wa[:, :, 512 * g:512 * g + 512], 0, 512)])
        for _pass in range(NT):
            for l in range(L):
                n0 = len(wq.plan)
                plan_layer(l)
                assert len(wq.plan) - n0 == NGRP
                for g in range(NGRP):
                    wq.plan[n0 + g][1] = l * NGRP + g
                    wq.plan[n0 + g][2] = "cast+st" if _pass == 0 else "ld"

        DMA("sp", "consts", [(consts.a, consts_d[:, :])], [], [consts.T])
        for l in range(L):
            DMA("sp", f"lvec{l}", [(lvec[l].a, lvec_d[l])], [], [lvec[l].T])
            DMA("sp", f"gwa{l}", [(gwa[l].a, gwa_d[l])], [], [gwa[l].T])
            DMA("pool", f"rgwa{l}", [(rgwa[l].a, rgwa_d[l].rearrange("n i j -> i n j"))], [], [rgwa[l].T])
            DMA("pool", f"rgwx{l}", [(rgwx[l].a, rgwx_d[l].rearrange("n i j -> i n j"))], [], [rgwx[l].T])
        DMA("sp", "cs", [(cs.a, cT[:, :, :])], [], [cs.T])
        VCPY(onesb.a, onesf, [consts.T], [onesb.T])
        P.op("dve", lambda e: e.memset(zlr.a, 1.0), [], [zlr.T])
        for l in range(L if STAGE >= -2 else 0):
            for h in range(4):
                P.op("dve", (lambda a: (lambda e: e.memset(a, 0.0)))(S[l][h].a), [], [S[l][h].T])
                P.op("dve", (lambda a: (lambda e: e.memset(a, 0.0)))(Sb[l][h].a), [], [Sb[l][h].T])
            P.op("dve", (lambda a: (lambda e: e.memset(a, 0.0)))(csc[l].a), [], [csc[l].T])
            P.op("dve", (lambda a: (lambda e: e.memset(a, 0.0)))(crg[l].a), [], [crg[l].T])
            P.op("dve", (lambda a: (lambda e: e.memset(a, 0.0)))(chh[l].a), [], [chh[l].T])
        ACT(scb.a, cs.a, AF.Silu, [cs.T], [scb.T])
        for l in range(L if STAGE >= -1 else 0):
            lv = lvec[l]
            ACT(cneg[l].a, lv.a[:, VLAM:VLAM + 8], AF.Exp, [lv.T], [cneg[l].T], scale=-1.0)
            ACT(cneg[l].a, cneg[l].a, AF.Ln, [cneg[l].T], [cneg[l].T], bias=1.0)
            TS(cneg[l].a, cneg[l].a, -8.0, None, MUL, None, [cneg[l].T], [cneg[l].T])
            TS(hc[l].a, cneg[l].a, 0.5, None, MUL, None, [cneg[l].T], [hc[l].T])
            TS(hvec[l].a, lv.a, 0.5, None, MUL, None, [lv.T], [hvec[l].T])
            for g in range(6):
                slot = wq.get()
                for cc in range(4):
                    ch = g * 4 + cc
                    z = zr.next()
                    for k in range(8):
                        MM(z.a[:, 0:1 + NS], slot.a[:, k, cc * 128:cc * 128 + 128], scb.a[:, k, :], k == 0, k == 7,
                           [slot.T, scb.T], [z.T])
                    ACT(modT[l].a[:, ch, :], z.a[:, 0:1 + NS], AF.Identity, [z.T, lv.T], [modT[l].T],
                        bias=lv.a[:, VBADA + ch:VBADA + ch + 1])
            TS(Ap[l].a, modT[l].a[:, 8:16, 0], 1.0, None, ADD, None, [modT[l].T], [Ap[l].T])
            TTo(Ap[l].a, Ap[l].a, lv.a[:, VNG:VNG + 8], MUL, [Ap[l].T, lv.T], [Ap[l].T])
            for k in range(8):
                TS(As[l].a[:, k, :], modT[l].a[:, 8 + k, 1:1 + NS], 1.0, lv.a[:, VNG + k:VNG + k + 1], ADD, MUL,
                   [modT[l].T, lv.T], [As[l].T])
            TS(gqp[l].a, modT[l].a[:, 16:24, 0], 0.25, None, MUL, None, [modT[l].T], [gqp[l].T])
            TS(gqs[l].a, modT[l].a[:, 16:24, 1:1 + NS], 0.25, None, MUL, None, [modT[l].T], [gqs[l].T])

        def proj(slot, c0, M, rhs, N, extra_R=()):
            z = zr.next()
            for k in range(8):
                MM(z.a[0:M, 0:N], slot.a[:, k, c0:c0 + M], rhs.a[:, k, 0:N], k == 0, k == 7,
                   [slot.T, rhs.T] + list(extra_R), [z.T])
            return z

        def rms_stats(src_fn, nk, N, scale, Rsrc, rstd=rstd):
            z = zr.next()
            for k0 in range(0, nk, 4):
                for k in range(k0, min(nk, k0 + 4)):
                    ACT(sqb.a[:, k % 4, 0:N], src_fn(k), AF.Square, Rsrc, [sqb.T])
                for k in range(k0, min(nk, k0 + 4)):
                    MM(z.a[:, 0:N], onesb.a, sqb.a[:, k % 4, 0:N], k == 0, k == nk - 1, [onesb.T, sqb.T], [z.T])
            ACT(rstd.a[:, 0:N], z.a[:, 0:N], AF.Ln, [z.T], [rstd.T], bias=EPS, scale=scale)
            ACT(rstd.a[:, 0:N], rstd.a[:, 0:N], AF.Exp, [rstd.T], [rstd.T], scale=-0.5)

        class Ctx:
            pass

        def block(l, cx):
            x, N, samp, tile_idx = cx.x, cx.N, cx.samp, cx.tile_idx
            hT, pbr, mg, gbuf, wk, xcd, xcb = cx.hT, cx.pbr, cx.mg, cx.gb, cx.wk, cx.xcd, cx.xcb
            lv = lvec[l]
            md = modT[l]
            last_tile = (not samp) and tile_idx == NT - 1
            PB = int(os.environ.get("KPB", "99")) if not samp else 99
            rms_stats(lambda k: x.a[:, k, 0:N], 8, N, 1.0 / D, [x.T])
            for k in range(8):
                t = wk.next()
                if not samp:
                    STT(t.a[:, 0:N], x.a[:, k, 0:N], Ap[l].a[:, k:k + 1], rstd.a[:, 0:N], MUL, MUL,
                        [x.T, Ap[l].T, rstd.T], [t.T])
                    ACT(hT.a[:, k, 0:N], t.a[:, 0:N], AF.Identity, [t.T, md.T], [hT.T], bias=md.a[:, k, 0:1])
                else:
                    TTo(t.a[:, 0:N], x.a[:, k, 0:N], rstd.a[:, 0:N], MUL, [x.T, rstd.T], [t.T])
                    TTo(t.a[:, 0:N], t.a[:, 0:N], As[l].a[:, k, :], MUL, [t.T, As[l].T], [t.T])
                    TTo(hT.a[:, k, 0:N], t.a[:, 0:N], md.a[:, k, 1:1 + NS], ADD, [t.T, md.T], [hT.T])

            if PB <= 1:
                return
            slot = yield
            z = proj(slot, 0, 16, hT, N)
            ACPY(zlr.a[0:16, 0:N], z.a[0:16, 0:N], [z.T], [zlr.T])
            if not samp:
                for c in range(4):
                    z = zr.next()
                    MM(z.a[:, :], zlr.a[0:17, c * 128:c * 128 + 128], gwa[l].a[0:17, :], True, True, [zlr.T, gwa[l].T], [z.T])
                    ACT(sp_tok.a[:, c, :], z.a[:, :], AF.Exp, [z.T], [sp_tok.T], scale=-1.0)
                    ACT(sp_tok.a[:, c, :], sp_tok.a[:, c, :], AF.Ln, [sp_tok.T], [sp_tok.T], bias=1.0)
                for c in range(4):
                    z = zr.next()
                    MM(z.a[:, :], triUpNeg, sp_tok.a[:, c, :], True, True, [consts.T, sp_tok.T], [z.T])
                    ACT(ekd.a[:, c, :], z.a[:, :], AF.Exp, [z.T], [ekd.T])
            else:
                z = zr.next()
                for h in range(4):
                    MM(z.a[:, h * NS:(h + 1) * NS], gwa[l].a[0:17, h * 128:h * 128 + 128], zlr.a[0:17, 0:N], True, True,
                       [zlr.T, gwa[l].T], [z.T])
                av = aTs.a.rearrange("p h s -> p (h s)")
                ACT(av, z.a[:, 0:4 * NS], AF.Exp, [z.T], [aTs.T], scale=-1.0)
                ACT(av, av, AF.Ln, [aTs.T], [aTs.T], bias=1.0)
                ACT(av, av, AF.Exp, [aTs.T], [aTs.T], scale=-1.0 / 16.0)

            if PB <= 2:
                return
            for h in range(4):
                slot = yield
                if not samp:
                    zb = zr.next()
                    for c in range(4):
                        MM(zb.a[:, c * 128:c * 128 + 128], sp_tok.a[:, c, h * 128:h * 128 + 128], triNeg, True, True,
                           [sp_tok.T, consts.T], [zb.T])
                    E2 = wk.next()
                    E1 = wk.next()
                    ACT(E1.a, zb.a, AF.Exp, [zb.T], [E1.T])
                    ACT(E2.a, zb.a, AF.Exp, [zb.T], [E2.T], scale=-1.0)
                    ACPY(eb[h].a, E1.a.rearrange("p (c n) -> p c n", c=4)[:, :, 127], [E1.T], [eb[h].T])
                    KA = int(os.environ.get("KA", "9"))
                    if KA <= 1:
                        continue
                    zq = proj(slot, 0, 128, hT, N)
                    STT(qe[h].a, zq.a, float(128 ** -0.5), E1.a, MUL, MUL, [zq.T, E1.T], [qe[h].T])
                    zk = proj(slot, 128, 128, hT, N)
                    TTo(ke[h].a, zk.a, E2.a, MUL, [zk.T, E2.T], [ke[h].T])
                    if KA <= 2:
                        continue
                    for c in range(4):
                        z = zr.next()
                        for k in range(8):
                            MM(z.a[:, 0:384], hT.a[:, k, c * 128:c * 128 + 128], slot.a[:, k, 128:512], k == 0, k == 7,
                               [hT.T, slot.T], [z.T])
                        if KA <= 3:
                            continue
                        ACPY(vt(h, c), z.a[:, 128:384], [z.T], [vtok[h].T])
                        if KA <= 4:
                            continue
                        TTo(kdtok[h].a[:, c, :], z.a[:, 0:128], ekd.a[:, c, h * 128:h * 128 + 128], MUL, [z.T, ekd.T, vtok[h].T], [kdtok[h].T])
                else:
                    zq = proj(slot, 0, 128, hT, N)
                    TS(qss.a[:, h, :], zq.a[:, 0:N], float(128 ** -0.5), None, MUL, None, [zq.T], [qss.T])
                    z = zr.next()
                    for k in range(8):
                        MM(z.a[0:NS, 0:384], hT.a[:, k, 0:N], slot.a[:, k, 128:512], k == 0, k == 7, [hT.T, slot.T], [z.T])
                    ACPY(kvs.a[:, h, :], z.a[0:NS, 0:384], [z.T], [kvs.T])

            if PB <= 3:
                return
            if not samp:
                for c in range(4):
                    cs_ = slice(c * 128, c * 128 + 128)
                    sa4 = zr.next()
                    for h in range(4):
                        MM(sa4.a[:, h * 128:h * 128 + 128], ke[h].a[:, cs_], qe[h].a[:, cs_], True, True, [qk.T], [sa4.T])
                    ams = []
                    for h in range(4):
                        am = attm.next()
                        TTo(am.a, sa4.a[:, h * 128:h * 128 + 128], maskT, MUL, [sa4.T, consts.T], [am.T])
                        ams.append(am)
                    sos = [zr.next(), zr.next()]
                    for h in range(4):
                        so = sos[h // 2]
                        for vc in range(2):
                            col = (h % 2) * 256 + vc * 128
                            MM(so.a[:, col:col + 128], vt(h, c)[:, vc * 128:vc * 128 + 128], ams[h].a, True, False,
                               [vtok[h].T, ams[h].T], [so.T])
                            MM(so.a[:, col:col + 128], Sb[l][h].a[:, vc * 128:vc * 128 + 128], qe[h].a[:, cs_], False, True,
                               [Sb[l][h].T, qk.T], [so.T])
                    sus = [zr.next(), zr.next()]
                    for h in range(4):
                        su = sus[h // 2]
                        MM(su.a[:, (h % 2) * 256:(h % 2) * 256 + 256], kdtok[h].a[:, c, :], vt(h, c), True, True,
                           [kdtok[h].T, vtok[h].T], [su.T])
                    for h in range(4):
                        so = sos[h // 2]
                        ACPY(oT[h].a[:, :, cs_], so.a[:, (h % 2) * 256:(h % 2) * 256 + 256].rearrange("p (a b) -> p a b", a=2),
                             [so.T], [oT[h].T])
                    for h in range(4):
                        su = sus[h // 2]
                        STT(S[l][h].a, S[l][h].a, eb[h].a[:, c:c + 1], su.a[:, (h % 2) * 256:(h % 2) * 256 + 256], MUL, ADD,
                            [S[l][h].T, eb[h].T, su.T], [S[l][h].T])
                        ACPY(Sb[l][h].a, S[l][h].a, [S[l][h].T], [Sb[l][h].T])
                if last_tile:
                    for h in range(4):
                        DMA("sp", f"S{l}_{h}", [(ogla_p[l, h], S[l][h].a)], [S[l][h].T], [])
            else:
                pass

            def s_A(s_):
                si = Sin[s_ % 2]
                DMA("sp", "ld_" + si.T.name + "s", [(si.a, sgla[l, s_].rearrange("h d v -> d h v"))], [], [si.T])
                km = km4.bufs[s_ % 2]
                TS(km.a, kvs.a[:, :, 0:128], ident[0:NS, s_:s_ + 1], None, MUL, None, [kvs.T, consts.T, sqb.T], [km.T])

            def s_B(s_):
                si, sn, km, sbf = Sin[s_ % 2], Snw[s_ % 2], km4.bufs[s_ % 2], snbq[s_ % 2]
                sus = [zr.next(), zr.next()]
                for h in range(4):
                    su = sus[h // 2]
                    MM(su.a[:, (h % 2) * 256:(h % 2) * 256 + 256], km.a[:, h, :], kvs.a[:, h, 128:384], True, True,
                       [km.T, kvs.T, sqb.T], [su.T])
                for h in range(4):
                    su = sus[h // 2]
                    STT(sn.a[:, h, :], si.a[:, h, :], aTs.a[:, h, s_:s_ + 1], su.a[:, (h % 2) * 256:(h % 2) * 256 + 256], MUL, ADD,
                        [si.T, aTs.T, su.T], [sn.T])
                ACPY(sbf.a, sn.a, [sn.T, qk.T], [sbf.T])
                DMA("sp", "st_" + sn.T.name + "s", [(ogla_s[l, s_].rearrange("h d v -> d h v"), sn.a)], [sn.T], [])

            def s_C(s_):
                sbf = snbq[s_ % 2]
                so = zr.next()
                for h in range(4):
                    for vc in range(2):
                        MM(so.a[:, 2 * h + vc:2 * h + vc + 1], sbf.a[:, h, vc * 128:vc * 128 + 128], qss.a[:, h, s_:s_ + 1], True, True,
                           [sbf.T, qss.T, qk.T], [so.T])
                ACPY(oTs.a[:, :, s_:s_ + 1], so.a[:, 0:8].rearrange("p (a b) -> p a b", b=1), [so.T], [oTs.T])

            def sstep(n):
                if n == 0:
                    ACPY(qk.a[0:1, 0, 0:1], qk.a[0:1, 0, 0:1], [], [qk.T])
                    ACPY(sqb.a[0:1, 3, 0:1], sqb.a[0:1, 3, 0:1], [], [sqb.T])
                    s_A(0)
                if n + 1 < NS:
                    s_A(n + 1)
                if 0 <= n - 1 < NS:
                    s_C(n - 1)
                if n < NS:
                    s_B(n)

            if PB <= 4:
                return
            rsb = cx.rsb
            for h in range(4):
                if not samp:
                    rms_stats(lambda vc: oT[h].a[:, vc, :], 2, N, 1.0 / 256, [oT[h].T], rsb[h])
            for h in range(4):
                slot = yield
                for vc in range(2):
                    zg = proj(slot, vc * 128, 128, hT, N)
                    if samp:
                        ACT(sgs.a[:, 2 * h + vc, :], zg.a[:, 0:N], AF.Tanh, [zg.T], [sgs.T], scale=0.5)
                        STT(sgs.a[:, 2 * h + vc, :], sgs.a[:, 2 * h + vc, :], 1.0, zg.a[:, 0:N], ADD, MUL, [sgs.T, zg.T], [sgs.T])
                        continue
                    sg = wk.next()
                    ACT(sg.a[:, 0:N], zg.a[:, 0:N], AF.Tanh, [zg.T], [sg.T], scale=0.5)
                    STT(sg.a[:, 0:N], sg.a[:, 0:N], 1.0, zg.a[:, 0:N], ADD, MUL, [sg.T, zg.T], [sg.T])
                    t1 = wk.next()
                    osrc = oT[h].a[:, vc, :] if not samp else oTs.a[:, 2 * h + vc, :]
                    oTt = oT[h].T if not samp else oTs.T
                    STT(t1.a[:, 0:N], osrc, lv.a[:, VGNG + vc:VGNG + vc + 1], rsb[h].a[:, 0:N], MUL, MUL,
                        [oTt, lv.T, rsb[h].T], [t1.T])
                    TTo(pbr[0].a[:, 2 * h + vc, 0:N], t1.a[:, 0:N], sg.a[:, 0:N], MUL, [t1.T, sg.T], [pbr[0].T])

            if PB <= 5:
                return
            hv = hvec[l]
            def rg2_tail(j, zrr, zi, zg, xc):
                a_ = wk.next()
                ACT(a_.a[:, 0:N], zrr.a[:, 0:N], AF.Tanh, [zrr.T, hv.T], [a_.T], bias=hv.a[:, VRGBA + j:VRGBA + j + 1], scale=0.5)
                ig = wk.next()
                ACT(ig.a[:, 0:N], zi.a[:, 0:N], AF.Tanh, [zi.T, hv.T], [ig.T], bias=hv.a[:, VRGBX + j:VRGBX + j + 1], scale=0.5)
                sg = wk.next()
                ACT(sg.a[:, 0:N], zg.a[:, 0:N], AF.Tanh, [zg.T], [sg.T], scale=0.5)
                ACT(a_.a[:, 0:N], a_.a[:, 0:N], AF.Exp, [a_.T, hc[l].T], [a_.T], bias=hc[l].a[:, j:j + 1], scale=hc[l].a[:, j:j + 1])
                sq_ = wk.next()
                STT(sq_.a[:, 0:N], a_.a[:, 0:N], 0.9999999, a_.a[:, 0:N], MIN, MUL, [a_.T], [sq_.T])
                ACT(sq_.a[:, 0:N], sq_.a[:, 0:N], AF.Sqrt, [sq_.T], [sq_.T], bias=0.25, scale=-0.25)
                STT(ig.a[:, 0:N], ig.a[:, 0:N], 1.0, xc.a[:, 0:N], ADD, MUL, [ig.T, xc.T], [ig.T])
                STT(sg.a[:, 0:N], sg.a[:, 0:N], 1.0, zg.a[:, 0:N], ADD, MUL, [sg.T, zg.T], [sg.T])
                TTo(ig.a[:, 0:N], ig.a[:, 0:N], sq_.a[:, 0:N], MUL, [ig.T, sq_.T], [ig.T])
                if not samp:
                    P.op("dve", (lambda o, d0, d1, ini: (lambda e: e.tensor_tensor_scan(out=o, data0=d0, data1=d1, initial=ini,
                                                                                        op0=MUL, op1=ADD)))(
                        xc.a[:, 0:N], a_.a[:, 0:N], ig.a[:, 0:N], chh[l].a[:, j:j + 1]),
                        [a_.T, ig.T, chh[l].T], [xc.T])
                    VCPY(chh[l].a[:, j:j + 1], xc.a[:, N - 1:N], [xc.T], [chh[l].T])
                    hsrc = xc.a[:, 0:N]
                    hT_ = xc.T
                else:
                    TTo(a_.a[:, 0:N], a_.a[:, 0:N], srgh.a[:, j, :], MUL, [a_.T, srgh.T], [a_.T])
                    TTo(nrgh.a[:, j, :], a_.a[:, 0:N], ig.a[:, 0:N], ADD, [a_.T, ig.T], [nrgh.T])
                    hsrc = nrgh.a[:, j, :]
                    hT_ = nrgh.T
                TTo(pbr[2].a[:, j, 0:N], hsrc, sg.a[:, 0:N], MUL, [hT_, sg.T], [pbr[2].T])

            pending = None
            for j in range(8):
                slotA = yield
                if samp:
                    sstep(2 * j)
                cw = [lv.a[:, VRGW + 4 * j + r:VRGW + 4 * j + r + 1] for r in range(4)]
                cb = lv.a[:, VRGCB + j:VRGCB + j + 1]
                zx = proj(slotA, 0, 128, hT, N)
                xc = xcd.next()
                if not samp:
                    ur = ub.next()
                    ACPY(ur.a[:, 3:3 + N], zx.a[:, 0:N], [zx.T], [ur.T])
                    ACPY(ur.a[:, 0:3], crg[l].a[:, j, :], [crg[l].T], [ur.T])
                    ACPY(crg[l].a[:, j, :], ur.a[:, N:N + 3], [ur.T], [crg[l].T])
                    ACT(xc.a[:, 0:N], ur.a[:, 0:N], AF.Identity, [ur.T, lv.T], [xc.T], bias=cb, scale=cw[0])
                    for r in range(1, 4):
                        STT(xc.a[:, 0:N], ur.a[:, r:r + N], cw[r], xc.a[:, 0:N], MUL, ADD, [ur.T, lv.T, xc.T], [xc.T])
                else:
                    ACPY(nrgc.a[:, j, 2, :], zx.a[:, 0:N], [zx.T], [nrgc.T])
                    VCPY(nrgc.a[:, j, 0:2, :], srgc.a[:, j, 1:3, :], [srgc.T], [nrgc.T])
                    TS(xc.a[:, 0:N], srgc.a[:, j, 0, :], cw[0], cb, MUL, ADD, [srgc.T, lv.T], [xc.T])
                    for r in range(1, 3):
                        STT(xc.a[:, 0:N], srgc.a[:, j, r, :], cw[r], xc.a[:, 0:N], MUL, ADD, [srgc.T, lv.T, xc.T], [xc.T])
                    STT(xc.a[:, 0:N], nrgc.a[:, j, 2, :], cw[3], xc.a[:, 0:N], MUL, ADD, [nrgc.T, lv.T, xc.T], [xc.T])
                ACPY(xcb.a[:, 0:N], xc.a[:, 0:N], [xc.T], [xcb.T])
                if pending is not None:
                    rg2_tail(*pending)
                    pending = None

                slot = slotA
                w0 = lv.a[:, VSCW + 3 * j + 0:VSCW + 3 * j + 1]
                w1 = lv.a[:, VSCW + 3 * j + 1:VSCW + 3 * j + 2]
                w2 = lv.a[:, VSCW + 3 * j + 2:VSCW + 3 * j + 3]
                zc = proj(slot, 128, 128, hT, N)
                tcb = wk.next()
                ACPY(tcb.a[:, 0:N], zc.a[:, 0:N], [zc.T], [tcb.T])
                zh = proj(slot, 256, 128, hT, N)
                uc = wk.next()
                if not samp:
                    u = ub.next()
                    TTo(u.a[:, 2:2 + N], zh.a[:, 0:N], tcb.a[:, 0:N], MUL, [zh.T, tcb.T], [u.T])
                    ACPY(u.a[:, 0:2], csc[l].a[:, j, :], [csc[l].T], [u.T])
                    ACPY(csc[l].a[:, j, :], u.a[:, N:N + 2], [u.T], [csc[l].T])
                    ACT(uc.a[:, 0:N], u.a[:, 0:N], AF.Identity, [u.T, lv.T], [uc.T], scale=w0)
                    STT(uc.a[:, 0:N], u.a[:, 1:1 + N], w1, uc.a[:, 0:N], MUL, ADD, [u.T, lv.T, uc.T], [uc.T])
                    STT(uc.a[:, 0:N], u.a[:, 2:2 + N], w2, uc.a[:, 0:N], MUL, ADD, [u.T, lv.T, uc.T], [uc.T])
                else:
                    TTo(nsc.a[:, j, 1, :], zh.a[:, 0:N], tcb.a[:, 0:N], MUL, [zh.T, tcb.T], [nsc.T])
                    VCPY(nsc.a[:, j, 0, :], ssc.a[:, j, 1, :], [ssc.T], [nsc.T])
                    TS(uc.a[:, 0:N], ssc.a[:, j, 0, :], w0, None, MUL, None, [ssc.T, lv.T], [uc.T])
                    STT(uc.a[:, 0:N], ssc.a[:, j, 1, :], w1, uc.a[:, 0:N], MUL, ADD, [ssc.T, lv.T, uc.T], [uc.T])
                    STT(uc.a[:, 0:N], nsc.a[:, j, 1, :], w2, uc.a[:, 0:N], MUL, ADD, [nsc.T, lv.T, uc.T], [uc.T])
                zb_ = proj(slot, 384, 128, hT, N)
                TTo(uc.a[:, 0:N], zb_.a[:, 0:N], uc.a[:, 0:N], MUL, [zb_.T, uc.T], [uc.T])
                slotB = yield
                if samp:
                    sstep(2 * j + 1)
                zg = proj(slotB, 0, 128, hT, N)
                sg = wk.next()
                ACT(sg.a[:, 0:N], zg.a[:, 0:N], AF.Tanh, [zg.T], [sg.T], scale=0.5)
                STT(sg.a[:, 0:N], sg.a[:, 0:N], 1.0, zg.a[:, 0:N], ADD, MUL, [sg.T, zg.T], [sg.T])
                TTo(pbr[1].a[:, j, 0:N], uc.a[:, 0:N], sg.a[:, 0:N], MUL, [uc.T, sg.T], [pbr[1].T])

                slot = slotB
                zrr = zr.next()
                MM(zrr.a[:, 0:N], rgwa[l].a[:, j, :], xcb.a[:, 0:N], True, True, [rgwa[l].T, xcb.T], [zrr.T])
                zi = zr.next()
                MM(zi.a[:, 0:N], rgwx[l].a[:, j, :], xcb.a[:, 0:N], True, True, [rgwx[l].T, xcb.T], [zi.T])
                zg = proj(slot, 128, 128, hT, N)
                if samp or not cx.defer:
                    rg2_tail(j, zrr, zi, zg, xc)
                else:
                    pending = (j, zrr, zi, zg, xc)
            if pending is not None:
                rg2_tail(*pending)
                pending = None
            if samp:
                sstep(NS)
                for h in range(4):
                    rms_stats(lambda vc: oTs.a[:, 2 * h + vc, :], 2, N, 1.0 / 256, [oTs.T], rsb[h])
                for h in range(4):
                    for vc in range(2):
                        t1 = wk.next()
                        STT(t1.a[:, 0:N], oTs.a[:, 2 * h + vc, :], lv.a[:, VGNG + vc:VGNG + vc + 1], rsb[h].a[:, 0:N], MUL, MUL,
                            [oTs.T, lv.T, rsb[h].T], [t1.T])
                        TTo(pbr[0].a[:, 2 * h + vc, 0:N], t1.a[:, 0:N], sgs.a[:, 2 * h + vc, :], MUL, [t1.T, sgs.T], [pbr[0].T])
            if last_tile:
                DMA("sp", f"csc{l}", [(osc_p[l], csc[l].a)], [csc[l].T], [])
                DMA("sp", f"crg{l}", [(orgc_p[l], crg[l].a)], [crg[l].T], [])
                DMA("sp", f"chh{l}", [(orgh_p[l], chh[l].a)], [chh[l].T], [])
            if samp:
                DMA("sp", "nsc", [(osc_s[l], nsc.a)], [nsc.T], [])
                DMA("sp", "nrgc", [(orgc_s[l], nrgc.a)], [nrgc.T], [])
                DMA("sp", "nrgh", [(orgh_s[l], nrgh.a)], [nrgh.T], [])

            if PB <= 7:
                return
            for i in range(8):
                s1 = yield
                for br in range(3):
                    zm = proj(s1, 128 * br, 128, hT, N)
                    ACT(gbuf[br].a[:, 0:N], zm.a[:, 0:N], AF.Tanh, [zm.T, hv.T], [gbuf[br].T],
                        bias=hv.a[:, VBM + 8 * br + i:VBM + 8 * br + i + 1], scale=0.5)
                acc = wk.next()
                s2 = yield
                for br in range(3):
                    zy = proj(s2, 128 * br, 128, pbr[br], N)
                    if br == 0:
                        STT(acc.a[:, 0:N], gbuf[0].a[:, 0:N], 1.0, zy.a[:, 0:N], ADD, MUL, [zy.T, gbuf[0].T], [acc.T])
                    else:
                        STT(gbuf[br].a[:, 0:N], gbuf[br].a[:, 0:N], 1.0, zy.a[:, 0:N], ADD, MUL, [zy.T, gbuf[br].T], [gbuf[br].T])
                        if br == 1:
                            TTo(acc.a[:, 0:N], acc.a[:, 0:N], gbuf[1].a[:, 0:N], ADD, [acc.T, gbuf[1].T], [acc.T])
                        else:
                            TTo(mg.a[:, i, 0:N], acc.a[:, 0:N], gbuf[2].a[:, 0:N], ADD, [acc.T, gbuf[2].T], [mg.T])

            for i in range(8):
                if i % 4 == 0:
                    so_ = yield
                zo = proj(so_, (i % 4) * 128, 128, mg, N)
                if not samp:
                    STT(x.a[:, i, 0:N], zo.a[:, 0:N], gqp[l].a[:, i:i + 1], x.a[:, i, 0:N], MUL, ADD, [zo.T, gqp[l].T, x.T], [x.T])
                else:
                    t = wk.next()
                    TTo(t.a[:, 0:N], zo.a[:, 0:N], gqs[l].a[:, i, :], MUL, [zo.T, gqs[l].T], [t.T])
                    TTo(x.a[:, i, 0:N], x.a[:, i, 0:N], t.a[:, 0:N], ADD, [x.T, t.T], [x.T])

        def final_norm(x, N, dst, dst_dram, key):
            rms_stats(lambda k: x.a[:, k, 0:N], 8, N, 1.0 / D, [x.T])
            for k in range(8):
                STT(dst.a[:, k, 0:N], x.a[:, k, 0:N], lvec[0].a[:, VFG + k:VFG + k + 1], rstd.a[:, 0:N], MUL, MUL,
                    [x.T, lvec[0].T, rstd.T], [dst.T])
            DMA("sp", key, [(dst_dram, dst.a[:, :, 0:N])], [dst.T], [])

        def run_blocks(l, ctxs):
            gens = [block(l, c) for c in ctxs]
            active = []
            for g in gens:
                try:
                    next(g)
                    active.append(g)
                except StopIteration:
                    pass
            while active:
                slot = wq.get()
                nxt = []
                for g in active:
                    try:
                        g.send(slot)
                        nxt.append(g)
                    except StopIteration:
                        pass
                active = nxt

        cs_ctx = Ctx()
        cs_ctx.x, cs_ctx.N, cs_ctx.samp, cs_ctx.tile_idx = xs, NS, True, 0
        cs_ctx.hT, cs_ctx.pbr, cs_ctx.mg, cs_ctx.gb, cs_ctx.wk, cs_ctx.xcd, cs_ctx.xcb, cs_ctx.rsb = (
            hT_s, pbr_s, mg_s, gb_s, wk_s, xcd_s, xcb_s, rsb_s)

        def prompt_ctx(xbuf, tt):
            c = Ctx()
            c.x, c.N, c.samp, c.tile_idx = xbuf, TT, False, tt
            c.hT, c.pbr, c.mg, c.gb, c.wk, c.xcd, c.xcb, c.rsb = hT, pbr, mg, gbuf, wk, xcd, xcb, [rstd, gbuf[0], gbuf[1], gbuf[2]]
            return c

        DMA("sp", "xs", [(xs.a, xsT[:, :, :])], [], [xs.T])
        xcur = xt.next()
        DMA("sp", xcur.T.name, [(xcur.a, xT[:, :, 0:TT])], [], [xcur.T])
        for tt in range(NT):
            if tt + 1 < NT:
                xnext = xt.next()
                DMA("sp", xnext.T.name, [(xnext.a, xT[:, :, (tt + 1) * TT:(tt + 2) * TT])], [], [xnext.T])
            for l in range(L):
                ctxs = [prompt_ctx(xcur, tt)]
                if tt == 0:
                    DMA("sp", "ssc", [(ssc.a, sscf[l])], [], [ssc.T])
                    DMA("sp", "srgc", [(srgc.a, srgcf[l])], [], [srgc.T])
                    DMA("sp", "srgh", [(srgh.a, srghf[l])], [], [srgh.T])
                    ctxs.append(cs_ctx)
                for c_ in ctxs:
                    c_.defer = (len(ctxs) == 1)
                run_blocks(l, ctxs)
            final_norm(xcur, TT, xcur, yT[:, :, tt * TT:(tt + 1) * TT], xcur.T.name)
            if tt == 0:
                final_norm(xs, NS, xs, ysT[:, :, :], "xs")
            if tt + 1 < NT:
                xcur = xnext
        assert STAGE < 9 or wq.pos == len(wq.plan), (wq.pos, len(wq.plan))
        outs = [b.T for b in xt.bufs] + [xs.T, nsc.T, nrgc.T, nrgh.T] + [b.T for b in Snw]
        for l in range(L):
            outs += [csc[l].T, crg[l].T, chh[l].T] + [S[l][h].T for h in range(4)]
        P.wait_all("sp", outs)
        P.emit(st)
    return nc


_CACHE = {}


def _fm(v):
    return np.ascontiguousarray(np.swapaxes(v.reshape(v.shape[:-1] + (8, 128)), -1, -2))


def kernel(x_prompt, x_sample, c_prompt, c_sample, state_gla, state_sc_conv, state_rg_conv, state_rg_h,
           w_ada, b_ada, norm_gain, w_in, gla_w_alpha, gla_b_alpha, gla_norm_gain, gla_w_branch, sc_conv_w,
           sc_w_branch, rg_conv_w, rg_conv_b, rg_w_a, rg_b_a, rg_w_x, rg_b_x, rg_lambda, rg_w_branch,
           b_merge, w_out, final_gain):
    f = lambda a: np.ascontiguousarray(np.asarray(a, dtype=np.float32))
    x_prompt, x_sample, c_prompt, c_sample = f(x_prompt), f(x_sample), f(c_prompt), f(c_sample)
    state_gla, state_sc_conv, state_rg_conv, state_rg_h = f(state_gla), f(state_sc_conv), f(state_rg_conv), f(state_rg_h)
    n = 8
    if "nc" not in _CACHE:
        _CACHE["nc"] = build_program()
    nc = _CACHE["nc"]
    lvec = np.zeros((L, 128, NV), np.float32)
    for l in range(L):
        lvec[l, :, VNG:VNG + 8] = _fm(f(norm_gain)[l])
        lvec[l, :, VBADA:VBADA + 24] = f(b_ada)[l].reshape(24, 128).T
        lvec[l, :, VGNG:VGNG + 2] = f(gla_norm_gain)[l].reshape(2, 128).T
        lvec[l, :, VSCW:VSCW + 24] = f(sc_conv_w)[l].reshape(3, 8, 128).transpose(2, 1, 0).reshape(128, 24)
        lvec[l, :, VRGW:VRGW + 32] = f(rg_conv_w)[l].reshape(4, 8, 128).transpose(2, 1, 0).reshape(128, 32)
        lvec[l, :, VRGCB:VRGCB + 8] = _fm(f(rg_conv_b)[l])
        lvec[l, :, VRGBA:VRGBA + 8] = _fm(f(rg_b_a)[l])
        lvec[l, :, VRGBX:VRGBX + 8] = _fm(f(rg_b_x)[l])
        lvec[l, :, VLAM:VLAM + 8] = _fm(f(rg_lambda)[l])
        lvec[l, :, VBM:VBM + 24] = f(b_merge)[l].reshape(24, 128).T
        lvec[l, :, VFG:VFG + 8] = _fm(f(final_gain))
    gwa = np.concatenate([f(gla_w_alpha), f(gla_b_alpha)[:, None, :]], axis=1)
    s_idx = np.arange(128)[:, None]
    t_idx = np.arange(128)[None, :]
    consts = np.concatenate([
        np.eye(128, dtype=np.float32),
        np.ones((128, 128), np.float32),
        np.where(s_idx <= t_idx, -1.0 / 16.0, 0.0).astype(np.float32),
        np.where(s_idx > t_idx, -1.0 / 16.0, 0.0).astype(np.float32),
        np.where(s_idx <= t_idx, 1.0, 0.0).astype(np.float32),
    ], axis=1)
    shared = {
        "w_ada": f(w_ada), "w_in": f(w_in), "wgla": f(gla_w_branch), "wsc": f(sc_w_branch), "wrg": f(rg_w_branch),
        "w_out": f(w_out), "rgwa": f(rg_w_a), "rgwx": f(rg_w_x), "gwa": np.ascontiguousarray(gwa),
        "lvec": lvec, "consts": np.ascontiguousarray(consts),
    }
    in_maps = []
    for c in range(n):
        sl = slice(c * NS, (c + 1) * NS)
        m = dict(shared)
        m["xT"] = np.ascontiguousarray(x_prompt[c].reshape(2048, 8, 128).transpose(2, 1, 0))
        m["xsT"] = np.ascontiguousarray(x_sample[sl, 0, :].reshape(NS, 8, 128).transpose(2, 1, 0))
        cc = np.concatenate([c_prompt[c:c + 1], c_sample[sl]], axis=0)
        m["cT"] = np.ascontiguousarray(cc.reshape(1 + NS, 8, 128).transpose(2, 1, 0))
        m["sgla"] = np.ascontiguousarray(state_gla[:, sl])
        m["sscf"] = np.ascontiguousarray(state_sc_conv[:, sl].reshape(L, NS, 2, 8, 128).transpose(0, 4, 3, 2, 1))
        m["srgcf"] = np.ascontiguousarray(state_rg_conv[:, sl].reshape(L, NS, 3, 8, 128).transpose(0, 4, 3, 2, 1))
        m["srghf"] = np.ascontiguousarray(state_rg_h[:, sl].reshape(L, NS, 8, 128).transpose(0, 3, 2, 1))
        in_maps.append(m)
    if os.environ.get("KCORES"):
        n1 = int(os.environ["KCORES"])
        res = run_bass_kernel_spmd(nc, in_maps[:n1], core_ids=list(range(n1)), trace=bool(os.environ.get("KTRACE")))
        return res
    res = run_bass_kernel_spmd(nc, in_maps, core_ids=list(range(n)))
    R = res.results
    B = 8
    y_prompt = np.stack([R[c]["yT"].transpose(2, 1, 0).reshape(2048, D) for c in range(n)], axis=0)
    y_sample = np.concatenate([R[c]["ysT"].transpose(2, 1, 0).reshape(NS, 1, D) for c in range(n)], axis=0)
    gla_p = np.stack([R[c]["ogla_p"] for c in range(n)], axis=1)
    sc_p = np.stack([R[c]["osc_p"].transpose(0, 3, 2, 1).reshape(L, 2, D) for c in range(n)], axis=1)
    rgc_p = np.stack([R[c]["orgc_p"].transpose(0, 3, 2, 1).reshape(L, 3, D) for c in range(n)], axis=1)
    rgh_p = np.stack([R[c]["orgh_p"].transpose(0, 2, 1).reshape(L, D) for c in range(n)], axis=1)
    gla_s = np.concatenate([R[c]["ogla_s"] for c in range(n)], axis=1)
    sc_s = np.concatenate([R[c]["osc_s"].transpose(0, 4, 3, 2, 1).reshape(L, NS, 2, D) for c in range(n)], axis=1)
    rgc_s = np.concatenate([R[c]["orgc_s"].transpose(0, 4, 3, 2, 1).reshape(L, NS, 3, D) for c in range(n)], axis=1)
    rgh_s = np.concatenate([R[c]["orgh_s"].transpose(0, 3, 2, 1).reshape(L, NS, D) for c in range(n)], axis=1)
    outs = (y_prompt, y_sample, gla_p, sc_p, rgc_p, rgh_p, gla_s, sc_s, rgc_s, rgh_s)
    return tuple(np.ascontiguousarray(o.astype(np.float32)) for o in outs)
```

```python
from contextlib import ExitStack
import os
import numpy as np
import concourse.bass as bass
import concourse.mybir as mybir
from concourse.bass_utils import run_bass_kernel_spmd

F32 = mybir.dt.float32
BF16 = mybir.dt.bfloat16
AF = mybir.ActivationFunctionType
ALU = mybir.AluOpType

D = 1024
L = 2
TT = 512
NT = 4
NS = 16
NV = 160
N_IN = 12304
OQ, OK_, OV, OGG, OLR = 0, 512, 1024, 2048, 3072
OB, OC, OH, OGS, OX, OGR, OM = 3088, 4112, 5136, 6160, 7184, 8208, 9232
VNG, VBADA, VGNG, VSCW, VRGW, VRGCB, VRGBA, VRGBX, VLAM, VBM, VFG = 0, 8, 32, 34, 58, 90, 98, 106, 114, 122, 146
EPS = 1e-6


class Tile:
    __slots__ = ("name", "w", "r")

    def __init__(self, name):
        self.name = name
        self.w = None
        self.r = []


class Buf:
    __slots__ = ("a", "T")

    def __init__(self, a, T):
        self.a = a
        self.T = T


class Prog:
    ENGS = ("pe", "act", "dve", "pool", "sp")

    def __init__(self, nc):
        self.nc = nc
        self.streams = {e: [] for e in self.ENGS}
        self.dma_cnt = {}
        self.sig = {e: set() for e in self.ENGS}
        self.window = {"act": 3, "dve": 3, "pool": 1 << 30}

    def _collect(self, eng, idx, reads, writes):
        deps = set()
        for t in reads:
            if t.w is not None:
                deps.add(t.w)
        for t in writes:
            if t.w is not None:
                deps.add(t.w)
            deps.update(t.r)
        out = []
        for d in deps:
            if d[0] == "e" and d[1] == eng:
                if eng in ("pe", "sp"):
                    continue
                if idx - d[2] > self.window[eng]:
                    continue
            out.append(d)
        for d in out:
            if d[0] == "e":
                self.sig[d[1]].add(d[2])
        return out

    def _mark(self, me, reads, writes):
        for t in reads:
            t.r.append(me)
            if len(t.r) > 64:
                t.r = t.r[-48:]
        for t in writes:
            t.w = me
            t.r = []

    def op(self, eng, fn, reads=(), writes=()):
        idx = len(self.streams[eng])
        deps = self._collect(eng, idx, reads, writes)
        self.streams[eng].append((fn, deps, None))
        self._mark(("e", eng, idx), reads, writes)

    def dma(self, eng, fns, key, reads=(), writes=()):
        idx = len(self.streams[eng])
        deps = self._collect(eng, idx, reads, writes)
        cnt = self.dma_cnt.get(key, 0)
        for i, fn in enumerate(fns):
            cnt += 16
            self.streams[eng].append((fn, deps if i == 0 else [], key))
        self.dma_cnt[key] = cnt
        self._mark(("s", key, cnt), reads, writes)

    def wait_all(self, eng, tiles):
        idx = len(self.streams[eng])
        deps = self._collect(eng, idx, (), tiles)
        self.streams[eng].append((None, deps, None))

    def emit(self, stack):
        nc = self.nc
        esem = {e: stack.enter_context(nc.semaphore("S_" + e)) for e in self.ENGS if e != "sp"}
        dsem = {k: stack.enter_context(nc.semaphore("D_" + str(k))) for k in self.dma_cnt}
        cnt_at = {}
        for e in self.ENGS:
            c = 0
            m = {}
            for i in range(len(self.streams[e])):
                if i in self.sig[e]:
                    c += 1
                    m[i] = c
            cnt_at[e] = m
        block = stack.enter_context(nc.Block())

        def run(eng_name):
            def body(eng):
                waited = {}
                for i, (fn, deps, key) in enumerate(self.streams[eng_name]):
                    for d in deps:
                        if d[0] == "e":
                            sem, val, k = esem[d[1]], cnt_at[d[1]][d[2]], ("e", d[1])
                        else:
                            sem, val, k = dsem[d[1]], d[2], ("s", d[1])
                        if waited.get(k, 0) >= val:
                            continue
                        waited[k] = val
                        eng.wait_ge(sem, val)
                    if fn is None:
                        continue
                    ins = fn(eng)
                    if key is not None:
                        ins.then_inc(dsem[key], 16)
                    elif i in self.sig[eng_name]:
                        ins.then_inc(esem[eng_name], 1)
            return body

        block.tensor(run("pe"))
        block.scalar(run("act"))
        block.vector(run("dve"))
        block.gpsimd(run("pool"))
        block.sync(run("sp"))


class Ring:
    def __init__(self, bufs):
        self.bufs = bufs
        self.i = 0

    def next(self):
        b = self.bufs[self.i % len(self.bufs)]
        self.i += 1
        return b


def build_program():
    nc = bass.Bass("TRN2", target_bir_lowering=False)

    def din(name, shape):
        return nc.dram_tensor(name, list(shape), F32, kind="ExternalInput").ap()

    def dout(name, shape):
        return nc.dram_tensor(name, list(shape), F32, kind="ExternalOutput").ap()

    xT = din("xT", [128, 8, NT * TT])
    xsT = din("xsT", [128, 8, NS])
    cT = din("cT", [128, 8, 1 + NS])
    sgla = din("sgla", [L, NS, 4, 128, 256])
    sscf = din("sscf", [L, 128, 8, 2, NS])
    srgcf = din("srgcf", [L, 128, 8, 3, NS])
    srghf = din("srghf", [L, 128, 8, NS])
    w_ada = din("w_ada", [L, D, 3 * D])
    w_in = din("w_in", [L, D, N_IN])
    wbr = [din("wgla", [L, D, D]), din("wsc", [L, D, D]), din("wrg", [L, D, D])]
    w_out = din("w_out", [L, D, D])
    rgwa_d = din("rgwa", [L, 8, 128, 128])
    rgwx_d = din("rgwx", [L, 8, 128, 128])
    gwa_d = din("gwa", [L, 17, 512])
    lvec_d = din("lvec", [L, 128, NV])
    consts_d = din("consts", [128, 5 * 128])

    yT = dout("yT", [128, 8, NT * TT])
    ysT = dout("ysT", [128, 8, NS])
    ogla_p = dout("ogla_p", [L, 4, 128, 256])
    osc_p = dout("osc_p", [L, 128, 8, 2])
    orgc_p = dout("orgc_p", [L, 128, 8, 3])
    orgh_p = dout("orgh_p", [L, 128, 8])
    ogla_s = dout("ogla_s", [L, NS, 4, 128, 256])
    osc_s = dout("osc_s", [L, 128, 8, 2, NS])
    orgc_s = dout("orgc_s", [L, 128, 8, 3, NS])
    orgh_s = dout("orgh_s", [L, 128, 8, NS])

    with ExitStack() as st:
        P = Prog(nc)
        STAGE = int(os.environ.get("KSTAGE", "9"))

        def sb(name, shape, dt=F32):
            t = st.enter_context(nc.sbuf_tensor("s_" + name, list(shape), dt))
            return Buf(t[:], Tile(name))

        def ps(name, shape):
            t = st.enter_context(nc.psum_tensor("p_" + name, list(shape), F32))
            return Buf(t[:], Tile(name))

        def MM(out, lhsT, rhs, start, stop, R, W):
            P.op("pe", lambda e: e.matmul(out, lhsT=lhsT, rhs=rhs, start=start, stop=stop), R, W)

        def ACT(out, in_, func, R, W, bias=None, scale=None):
            kw = {}
            if bias is not None:
                kw["bias"] = bias
            if scale is not None:
                kw["scale"] = scale
            P.op("act", lambda e: e.activation(out=out, in_=in_, func=func, **kw), R, W)

        def ACPY(out, in_, R, W):
            P.op("act", lambda e: e.copy(out=out, in_=in_), R, W)

        def VCPY(out, in_, R, W):
            P.op("dve", lambda e: e.tensor_copy(out=out, in_=in_), R, W)

        def TTo(out, a, b, op, R, W):
            P.op("dve", lambda e: e.tensor_tensor(out=out, in0=a, in1=b, op=op), R, W)

        def TS(out, a, s1, s2, op0, op1, R, W):
            if op1 is None:
                P.op("dve", lambda e: e.tensor_scalar(out=out, in0=a, scalar1=s1, scalar2=None, op0=op0), R, W)
            else:
                P.op("dve", lambda e: e.tensor_scalar(out=out, in0=a, scalar1=s1, scalar2=s2, op0=op0, op1=op1), R, W)

        def STT(out, a, s, b, op0, op1, R, W):
            P.op("dve", lambda e: e.scalar_tensor_tensor(out=out, in0=a, scalar=s, in1=b, op0=op0, op1=op1), R, W)

        def PTS(out, a, s1, s2, op0, op1, R, W):
            P.op("pool", lambda e: e.tensor_scalar(out=out, in0=a, scalar1=s1, scalar2=s2, op0=op0, op1=op1), R, W)

        def PSTT(out, a, s, b, op0, op1, R, W):
            P.op("pool", lambda e: e.scalar_tensor_tensor(out=out, in0=a, scalar=s, in1=b, op0=op0, op1=op1), R, W)

        def PCPY(out, in_, R, W):
            P.op("pool", lambda e: e.tensor_copy(out=out, in_=in_), R, W)

        def DMA(eng, key, pairs, R, W):
            fns = []
            for (o, i) in pairs:
                fns.append((lambda o, i: (lambda e: e.dma_start(out=o, in_=i)))(o, i))
            P.dma(eng, fns, key, R, W)

        MUL, ADD, MIN = ALU.mult, ALU.add, ALU.min

        consts = sb("consts", [128, 5 * 128])
        ident = consts.a[:, 0:128]
        onesf = consts.a[:, 128:256]
        triNeg = consts.a[:, 256:384]
        triUpNeg = consts.a[:, 384:512]
        maskT = consts.a[:, 512:640]
        onesb = sb("onesb", [128, 128], BF16)
        lvec = [sb(f"lvec{l}", [128, NV]) for l in range(L)]
        cneg = [sb(f"cneg{l}", [128, 8]) for l in range(L)]
        gwa = [sb(f"gwa{l}", [17, 512]) for l in range(L)]
        rgwa = [sb(f"rgwa{l}", [128, 8, 128], BF16) for l in range(L)]
        rgwx = [sb(f"rgwx{l}", [128, 8, 128], BF16) for l in range(L)]
        modT = [sb(f"modT{l}", [128, 24, 1 + NS]) for l in range(L)]
        Ap = [sb(f"Ap{l}", [128, 8]) for l in range(L)]
        As = [sb(f"As{l}", [128, 8, NS]) for l in range(L)]
        cs = sb("cs", [128, 8, 1 + NS])
        scb = sb("scb", [128, 8, 1 + NS], BF16)

        wslots = Ring([sb(f"ws{i}", [128, 8, 512], BF16) for i in range(3)])
        xt = Ring([sb(f"xt{i}", [128, 8, TT]) for i in range(2)])
        hT = sb("hT", [128, 8, TT], BF16)
        pbr0 = sb("pbr0", [128, 8, TT], BF16)
        raw1 = sb("raw1", [128, 4 * 512])
        raw2 = sb("raw2", [128, 4 * 512])
        pbr = [pbr0,
               Buf(raw1.a.bitcast(BF16).rearrange("p (k n) -> p k n", k=8), raw1.T),
               Buf(raw2.a.bitcast(BF16).rearrange("p (k n) -> p k n", k=8), raw2.T)]
        mg = sb("mg", [128, 8, TT], BF16)
        sqb = sb("sqb", [128, 4, TT], BF16)
        rstd = sb("rstd", [128, TT])
        wk = Ring([sb(f"wk{i}", [128, TT]) for i in range(4)])
        xcd = Ring([sb(f"xcd{i}", [128, TT]) for i in range(2)])
        gbuf = [sb(f"gb{i}", [128, TT]) for i in range(3)]
        zlr = sb("zlr", [17, TT])
        sp_tok = Buf(raw1.a.rearrange("p (c n) -> p c n", c=4), raw1.T)
        ekd = Buf(raw2.a.rearrange("p (c n) -> p c n", c=4), raw2.T)
        eb = [sb(f"eb{h}", [128, 4]) for h in range(4)]
        qk = sb("qk", [128, 8, TT], BF16)
        qe = [Buf(qk.a[:, h, :], qk.T) for h in range(4)]
        ke = [Buf(qk.a[:, 4 + h, :], qk.T) for h in range(4)]
        hvec = [sb(f"hvec{l}", [128, NV]) for l in range(L)]
        hc = [sb(f"hc{l}", [128, 8]) for l in range(L)]
        gqp = [sb(f"gqp{l}", [128, 8]) for l in range(L)]
        gqs = [sb(f"gqs{l}", [128, 8, NS]) for l in range(L)]
        vtok = [Buf(None, mg.T) for h in range(4)]

        def vt(h, c):
            return mg.a[:, 2 * h + c // 2, (c % 2) * 256:(c % 2) * 256 + 256]
        kdtok = [sb(f"kdtok{h}", [128, 4, 128], BF16) for h in range(4)]
        attm = Ring([sb(f"attm{i}", [128, 128], BF16) for i in range(4)])
        oT = [sb(f"oT{h}", [128, 2, TT]) for h in range(4)]
        S = [[sb(f"S{l}_{h}", [128, 256]) for h in range(4)] for l in range(L)]
        Sb = [[sb(f"Sb{l}_{h}", [128, 256], BF16) for h in range(4)] for l in range(L)]
        csc = [sb(f"csc{l}", [128, 8, 2]) for l in range(L)]
        crg = [sb(f"crg{l}", [128, 8, 3]) for l in range(L)]
        chh = [sb(f"chh{l}", [128, 8]) for l in range(L)]
        ub = Ring([sb(f"ub{i}", [128, TT + 3]) for i in range(2)])
        xcb = sb("xcb", [128, TT], BF16)
        xs = sb("xs", [128, 8, NS])
        ssc = sb("ssc", [128, 8, 2, NS])
        srgc = sb("srgc", [128, 8, 3, NS])
        srgh = sb("srgh", [128, 8, NS])
        nsc = sb("nsc", [128, 8, 2, NS])
        nrgc = sb("nrgc", [128, 8, 3, NS])
        nrgh = sb("nrgh", [128, 8, NS])
        aTs = sb("aTs", [128, 4, NS])
        qss = sb("qss", [128, 4, NS], BF16)
        kvs = sb("kvs", [NS, 4, 384], BF16)
        km4 = Ring([Buf(sqb.a[0:NS, i, :].rearrange("p (h d) -> p h d", h=4), Tile(f"km4_{i}")) for i in range(2)])
        Sin = [Buf(oT[i].a.rearrange("p a (b c) -> p (a b) c", b=2), oT[i].T) for i in range(2)]
        Snw = [Buf(oT[i].a.rearrange("p a (b c) -> p (a b) c", b=2), oT[i].T) for i in range(2, 4)]
        snbq = [Buf(qk.a[:, 2 * i:2 * i + 2, :].rearrange("p k (a v) -> p (k a) v", a=2), Tile(f"snbq{i}")) for i in range(2)]
        sgs = sb("sgs", [128, 8, NS])
        snb4 = Ring([Buf(ub.bufs[i].a[:, 0:512].bitcast(BF16).rearrange("p (h v) -> p h v", h=4), ub.bufs[i].T) for i in range(2)])
        oTs = sb("oTs", [128, 8, NS])
        hT_s = sb("hT_s", [128, 8, NS], BF16)
        pbr_s = [sb(f"pbr_s{i}", [128, 8, NS], BF16) for i in range(3)]
        mg_s = sb("mg_s", [128, 8, NS], BF16)
        rsb_s = [sb(f"rsb_s{i}", [128, NS]) for i in range(4)]
        gb_s = [sb(f"gb_s{i}", [128, NS]) for i in range(3)]
        wk_s = Ring([sb(f"wk_s{i}", [128, NS]) for i in range(4)])
        xcd_s = Ring([sb(f"xcd_s{i}", [128, NS]) for i in range(2)])
        xcb_s = sb("xcb_s", [128, NS], BF16)

        zr = Ring([ps(f"z{i}", [128, 512]) for i in range(8)])

        class _SmView:
            def next(self):
                b = zr.next()
                return Buf(b.a[:, 0:256], b.T)
        smr = _SmView()

        def wview(w, l):
            return w[l].rearrange("(k p) n -> p k n", p=128)

        def wload(pieces):
            slot = wslots.next()
            pairs = [(slot.a[:, :, d0:d0 + n], src) for (src, d0, n) in pieces]
            if os.environ.get("KNODMA") and wslots.i > 3:
                return slot
            DMA("pool", slot.T.name, pairs, [], [slot.T])
            return slot

        class WQ:
            def __init__(self):
                self.plan = []
                self.issued = []
                self.pos = 0

            def add(self, pieces):
                self.plan.append(pieces)

            def get(self):
                while len(self.issued) < min(len(self.plan), self.pos + 3):
                    self.issued.append(wload(self.plan[len(self.issued)]))
                s = self.issued[self.pos]
                self.pos += 1
                return s

        wq = WQ()

        def plan_layer(l):
            wi = wview(w_in, l)
            wq.add([(wi[:, :, OLR:OLR + 16], 0, 16)])
            for h in range(4):
                wq.add([(wi[:, :, OQ + 128 * h:OQ + 128 * h + 128], 0, 128),
                        (wi[:, :, OK_ + 128 * h:OK_ + 128 * h + 128], 128, 128),
                        (wi[:, :, OV + 256 * h:OV + 256 * h + 256], 256, 256)])
            for h in range(4):
                wq.add([(wi[:, :, OGG + 256 * h:OGG + 256 * h + 256], 0, 256)])
            for j in range(8):
                wq.add([(wi[:, :, o + 128 * j:o + 128 * j + 128], 128 * q, 128) for q, o in enumerate((OX, OC, OH, OB))])
                wq.add([(wi[:, :, o + 128 * j:o + 128 * j + 128], 128 * q, 128) for q, o in enumerate((OGS, OGR))])
            for i in range(8):
                wq.add([(wi[:, :, OM + 1024 * br + 128 * i:OM + 1024 * br + 128 * i + 128], 128 * br, 128) for br in range(3)])
                wq.add([(wview(wbr[br], l)[:, :, 128 * i:128 * i + 128], 128 * br, 128) for br in range(3)])
            wo = wview(w_out, l)
            for g in range(2):
                wq.add([(wo[:, :, 512 * g:512 * g + 512], 0, 512)])

        for l in range(L):
            wa = wview(w_ada, l)
            for g in range(6):
                wq.add([(wa[:, :, 512 * g:512 * g + 512], 0, 512)])
        for _pass in range(NT):
            for l in range(L):
                plan_layer(l)

        DMA("sp", "consts", [(consts.a, consts_d[:, :])], [], [consts.T])
        for l in range(L):
            DMA("sp", f"lvec{l}", [(lvec[l].a, lvec_d[l])], [], [lvec[l].T])
            DMA("sp", f"gwa{l}", [(gwa[l].a, gwa_d[l])], [], [gwa[l].T])
            DMA("pool", f"rgwa{l}", [(rgwa[l].a, rgwa_d[l].rearrange("n i j -> i n j"))], [], [rgwa[l].T])
            DMA("pool", f"rgwx{l}", [(rgwx[l].a, rgwx_d[l].rearrange("n i j -> i n j"))], [], [rgwx[l].T])
        DMA("sp", "cs", [(cs.a, cT[:, :, :])], [], [cs.T])
        VCPY(onesb.a, onesf, [consts.T], [onesb.T])
        P.op("dve", lambda e: e.memset(zlr.a, 1.0), [], [zlr.T])
        for l in range(L if STAGE >= -2 else 0):
            for h in range(4):
                P.op("dve", (lambda a: (lambda e: e.memset(a, 0.0)))(S[l][h].a), [], [S[l][h].T])
                P.op("dve", (lambda a: (lambda e: e.memset(a, 0.0)))(Sb[l][h].a), [], [Sb[l][h].T])
            P.op("dve", (lambda a: (lambda e: e.memset(a, 0.0)))(csc[l].a), [], [csc[l].T])
            P.op("dve", (lambda a: (lambda e: e.memset(a, 0.0)))(crg[l].a), [], [crg[l].T])
            P.op("dve", (lambda a: (lambda e: e.memset(a, 0.0)))(chh[l].a), [], [chh[l].T])
        ACT(scb.a, cs.a, AF.Silu, [cs.T], [scb.T])
        for l in range(L if STAGE >= -1 else 0):
            lv = lvec[l]
            ACT(cneg[l].a, lv.a[:, VLAM:VLAM + 8], AF.Exp, [lv.T], [cneg[l].T], scale=-1.0)
            ACT(cneg[l].a, cneg[l].a, AF.Ln, [cneg[l].T], [cneg[l].T], bias=1.0)
            TS(cneg[l].a, cneg[l].a, -8.0, None, MUL, None, [cneg[l].T], [cneg[l].T])
            TS(hc[l].a, cneg[l].a, 0.5, None, MUL, None, [cneg[l].T], [hc[l].T])
            TS(hvec[l].a, lv.a, 0.5, None, MUL, None, [lv.T], [hvec[l].T])
            for g in range(6):
                slot = wq.get()
                for cc in range(4):
                    ch = g * 4 + cc
                    z = zr.next()
                    for k in range(8):
                        MM(z.a[:, 0:1 + NS], slot.a[:, k, cc * 128:cc * 128 + 128], scb.a[:, k, :], k == 0, k == 7,
                           [slot.T, scb.T], [z.T])
                    ACT(modT[l].a[:, ch, :], z.a[:, 0:1 + NS], AF.Identity, [z.T, lv.T], [modT[l].T],
                        bias=lv.a[:, VBADA + ch:VBADA + ch + 1])
            TS(Ap[l].a, modT[l].a[:, 8:16, 0], 1.0, None, ADD, None, [modT[l].T], [Ap[l].T])
            TTo(Ap[l].a, Ap[l].a, lv.a[:, VNG:VNG + 8], MUL, [Ap[l].T, lv.T], [Ap[l].T])
            for k in range(8):
                TS(As[l].a[:, k, :], modT[l].a[:, 8 + k, 1:1 + NS], 1.0, lv.a[:, VNG + k:VNG + k + 1], ADD, MUL,
                   [modT[l].T, lv.T], [As[l].T])
            TS(gqp[l].a, modT[l].a[:, 16:24, 0], 0.25, None, MUL, None, [modT[l].T], [gqp[l].T])
            TS(gqs[l].a, modT[l].a[:, 16:24, 1:1 + NS], 0.25, None, MUL, None, [modT[l].T], [gqs[l].T])

        def proj(slot, c0, M, rhs, N, extra_R=()):
            z = zr.next()
            for k in range(8):
                MM(z.a[0:M, 0:N], slot.a[:, k, c0:c0 + M], rhs.a[:, k, 0:N], k == 0, k == 7,
                   [slot.T, rhs.T] + list(extra_R), [z.T])
            return z

        def rms_stats(src_fn, nk, N, scale, Rsrc, rstd=rstd):
            z = zr.next()
            for k0 in range(0, nk, 4):
                for k in range(k0, min(nk, k0 + 4)):
                    ACT(sqb.a[:, k % 4, 0:N], src_fn(k), AF.Square, Rsrc, [sqb.T])
                for k in range(k0, min(nk, k0 + 4)):
                    MM(z.a[:, 0:N], onesb.a, sqb.a[:, k % 4, 0:N], k == 0, k == nk - 1, [onesb.T, sqb.T], [z.T])
            ACT(rstd.a[:, 0:N], z.a[:, 0:N], AF.Ln, [z.T], [rstd.T], bias=EPS, scale=scale)
            ACT(rstd.a[:, 0:N], rstd.a[:, 0:N], AF.Exp, [rstd.T], [rstd.T], scale=-0.5)

        class Ctx:
            pass

        def block(l, cx):
            x, N, samp, tile_idx = cx.x, cx.N, cx.samp, cx.tile_idx
            hT, pbr, mg, gbuf, wk, xcd, xcb = cx.hT, cx.pbr, cx.mg, cx.gb, cx.wk, cx.xcd, cx.xcb
            lv = lvec[l]
            md = modT[l]
            last_tile = (not samp) and tile_idx == NT - 1
            PB = int(os.environ.get("KPB", "99")) if not samp else 99
            rms_stats(lambda k: x.a[:, k, 0:N], 8, N, 1.0 / D, [x.T])
            for k in range(8):
                t = wk.next()
                if not samp:
                    STT(t.a[:, 0:N], x.a[:, k, 0:N], Ap[l].a[:, k:k + 1], rstd.a[:, 0:N], MUL, MUL,
                        [x.T, Ap[l].T, rstd.T], [t.T])
                    ACT(hT.a[:, k, 0:N], t.a[:, 0:N], AF.Identity, [t.T, md.T], [hT.T], bias=md.a[:, k, 0:1])
                else:
                    TTo(t.a[:, 0:N], x.a[:, k, 0:N], rstd.a[:, 0:N], MUL, [x.T, rstd.T], [t.T])
                    TTo(t.a[:, 0:N], t.a[:, 0:N], As[l].a[:, k, :], MUL, [t.T, As[l].T], [t.T])
                    TTo(hT.a[:, k, 0:N], t.a[:, 0:N], md.a[:, k, 1:1 + NS], ADD, [t.T, md.T], [hT.T])

            if PB <= 1:
                return
            slot = yield
            z = proj(slot, 0, 16, hT, N)
            ACPY(zlr.a[0:16, 0:N], z.a[0:16, 0:N], [z.T], [zlr.T])
            if not samp:
                for c in range(4):
                    z = zr.next()
                    MM(z.a[:, :], zlr.a[0:17, c * 128:c * 128 + 128], gwa[l].a[0:17, :], True, True, [zlr.T, gwa[l].T], [z.T])
                    ACT(sp_tok.a[:, c, :], z.a[:, :], AF.Exp, [z.T], [sp_tok.T], scale=-1.0)
                    ACT(sp_tok.a[:, c, :], sp_tok.a[:, c, :], AF.Ln, [sp_tok.T], [sp_tok.T], bias=1.0)
                for c in range(4):
                    z = zr.next()
                    MM(z.a[:, :], triUpNeg, sp_tok.a[:, c, :], True, True, [consts.T, sp_tok.T], [z.T])
                    ACT(ekd.a[:, c, :], z.a[:, :], AF.Exp, [z.T], [ekd.T])
            else:
                z = zr.next()
                for h in range(4):
                    MM(z.a[:, h * NS:(h + 1) * NS], gwa[l].a[0:17, h * 128:h * 128 + 128], zlr.a[0:17, 0:N], True, True,
                       [zlr.T, gwa[l].T], [z.T])
                av = aTs.a.rearrange("p h s -> p (h s)")
                ACT(av, z.a[:, 0:4 * NS], AF.Exp, [z.T], [aTs.T], scale=-1.0)
                ACT(av, av, AF.Ln, [aTs.T], [aTs.T], bias=1.0)
                ACT(av, av, AF.Exp, [aTs.T], [aTs.T], scale=-1.0 / 16.0)

            if PB <= 2:
                return
            for h in range(4):
                slot = yield
                if not samp:
                    zb = zr.next()
                    for c in range(4):
                        MM(zb.a[:, c * 128:c * 128 + 128], sp_tok.a[:, c, h * 128:h * 128 + 128], triNeg, True, True,
                           [sp_tok.T, consts.T], [zb.T])
                    E2 = wk.next()
                    E1 = wk.next()
                    ACT(E1.a, zb.a, AF.Exp, [zb.T], [E1.T])
                    ACT(E2.a, zb.a, AF.Exp, [zb.T], [E2.T], scale=-1.0)
                    ACPY(eb[h].a, E1.a.rearrange("p (c n) -> p c n", c=4)[:, :, 127], [E1.T], [eb[h].T])
                    KA = int(os.environ.get("KA", "9"))
                    if KA <= 1:
                        continue
                    zq = proj(slot, 0, 128, hT, N)
                    STT(qe[h].a, zq.a, float(128 ** -0.5), E1.a, MUL, MUL, [zq.T, E1.T], [qe[h].T])
                    zk = proj(slot, 128, 128, hT, N)
                    TTo(ke[h].a, zk.a, E2.a, MUL, [zk.T, E2.T], [ke[h].T])
                    if KA <= 2:
                        continue
                    for c in range(4):
                        z = zr.next()
                        for k in range(8):
                            MM(z.a[:, 0:384], hT.a[:, k, c * 128:c * 128 + 128], slot.a[:, k, 128:512], k == 0, k == 7,
                               [hT.T, slot.T], [z.T])
                        if KA <= 3:
                            continue
                        ACPY(vt(h, c), z.a[:, 128:384], [z.T], [vtok[h].T])
                        if KA <= 4:
                            continue
                        TTo(kdtok[h].a[:, c, :], z.a[:, 0:128], ekd.a[:, c, h * 128:h * 128 + 128], MUL, [z.T, ekd.T, vtok[h].T], [kdtok[h].T])
                else:
                    zq = proj(slot, 0, 128, hT, N)
                    TS(qss.a[:, h, :], zq.a[:, 0:N], float(128 ** -0.5), None, MUL, None, [zq.T], [qss.T])
                    z = zr.next()
                    for k in range(8):
                        MM(z.a[0:NS, 0:384], hT.a[:, k, 0:N], slot.a[:, k, 128:512], k == 0, k == 7, [hT.T, slot.T], [z.T])
                    ACPY(kvs.a[:, h, :], z.a[0:NS, 0:384], [z.T], [kvs.T])

            if PB <= 3:
                return
            if not samp:
                for c in range(4):
                    cs_ = slice(c * 128, c * 128 + 128)
                    sa4 = zr.next()
                    for h in range(4):
                        MM(sa4.a[:, h * 128:h * 128 + 128], ke[h].a[:, cs_], qe[h].a[:, cs_], True, True, [qk.T], [sa4.T])
                    ams = []
                    for h in range(4):
                        am = attm.next()
                        TTo(am.a, sa4.a[:, h * 128:h * 128 + 128], maskT, MUL, [sa4.T, consts.T], [am.T])
                        ams.append(am)
                    sos = [zr.next(), zr.next()]
                    for h in range(4):
                        so = sos[h // 2]
                        for vc in range(2):
                            col = (h % 2) * 256 + vc * 128
                            MM(so.a[:, col:col + 128], vt(h, c)[:, vc * 128:vc * 128 + 128], ams[h].a, True, False,
                               [vtok[h].T, ams[h].T], [so.T])
                            MM(so.a[:, col:col + 128], Sb[l][h].a[:, vc * 128:vc * 128 + 128], qe[h].a[:, cs_], False, True,
                               [Sb[l][h].T, qk.T], [so.T])
                    sus = [zr.next(), zr.next()]
                    for h in range(4):
                        su = sus[h // 2]
                        MM(su.a[:, (h % 2) * 256:(h % 2) * 256 + 256], kdtok[h].a[:, c, :], vt(h, c), True, True,
                           [kdtok[h].T, vtok[h].T], [su.T])
                    for h in range(4):
                        so = sos[h // 2]
                        ACPY(oT[h].a[:, :, cs_], so.a[:, (h % 2) * 256:(h % 2) * 256 + 256].rearrange("p (a b) -> p a b", a=2),
                             [so.T], [oT[h].T])
                    for h in range(4):
                        su = sus[h // 2]
                        STT(S[l][h].a, S[l][h].a, eb[h].a[:, c:c + 1], su.a[:, (h % 2) * 256:(h % 2) * 256 + 256], MUL, ADD,
                            [S[l][h].T, eb[h].T, su.T], [S[l][h].T])
                        ACPY(Sb[l][h].a, S[l][h].a, [S[l][h].T], [Sb[l][h].T])
                if last_tile:
                    for h in range(4):
                        DMA("sp", f"S{l}_{h}", [(ogla_p[l, h], S[l][h].a)], [S[l][h].T], [])
            else:
                pass

            def s_A(s_):
                si = Sin[s_ % 2]
                DMA("sp", "ld_" + si.T.name + "s", [(si.a, sgla[l, s_].rearrange("h d v -> d h v"))], [], [si.T])
                km = km4.bufs[s_ % 2]
                TS(km.a, kvs.a[:, :, 0:128], ident[0:NS, s_:s_ + 1], None, MUL, None, [kvs.T, consts.T, sqb.T], [km.T])

            def s_B(s_):
                si, sn, km, sbf = Sin[s_ % 2], Snw[s_ % 2], km4.bufs[s_ % 2], snbq[s_ % 2]
                sus = [zr.next(), zr.next()]
                for h in range(4):
                    su = sus[h // 2]
                    MM(su.a[:, (h % 2) * 256:(h % 2) * 256 + 256], km.a[:, h, :], kvs.a[:, h, 128:384], True, True,
                       [km.T, kvs.T, sqb.T], [su.T])
                for h in range(4):
                    su = sus[h // 2]
                    STT(sn.a[:, h, :], si.a[:, h, :], aTs.a[:, h, s_:s_ + 1], su.a[:, (h % 2) * 256:(h % 2) * 256 + 256], MUL, ADD,
                        [si.T, aTs.T, su.T], [sn.T])
                ACPY(sbf.a, sn.a, [sn.T, qk.T], [sbf.T])
                DMA("sp", "st_" + sn.T.name + "s", [(ogla_s[l, s_].rearrange("h d v -> d h v"), sn.a)], [sn.T], [])

            def s_C(s_):
                sbf = snbq[s_ % 2]
                so = zr.next()
                for h in range(4):
                    for vc in range(2):
                        MM(so.a[:, 2 * h + vc:2 * h + vc + 1], sbf.a[:, h, vc * 128:vc * 128 + 128], qss.a[:, h, s_:s_ + 1], True, True,
                           [sbf.T, qss.T, qk.T], [so.T])
                ACPY(oTs.a[:, :, s_:s_ + 1], so.a[:, 0:8].rearrange("p (a b) -> p a b", b=1), [so.T], [oTs.T])

            def sstep(n):
                if n == 0:
                    ACPY(qk.a[0:1, 0, 0:1], qk.a[0:1, 0, 0:1], [], [qk.T])
                    ACPY(sqb.a[0:1, 3, 0:1], sqb.a[0:1, 3, 0:1], [], [sqb.T])
                    s_A(0)
                if n + 1 < NS:
                    s_A(n + 1)
                if 0 <= n - 1 < NS:
                    s_C(n - 1)
                if n < NS:
                    s_B(n)

            if PB <= 4:
                return
            rsb = cx.rsb
            for h in range(4):
                if not samp:
                    rms_stats(lambda vc: oT[h].a[:, vc, :], 2, N, 1.0 / 256, [oT[h].T], rsb[h])
            for h in range(4):
                slot = yield
                for vc in range(2):
                    zg = proj(slot, vc * 128, 128, hT, N)
                    if samp:
                        ACT(sgs.a[:, 2 * h + vc, :], zg.a[:, 0:N], AF.Tanh, [zg.T], [sgs.T], scale=0.5)
                        STT(sgs.a[:, 2 * h + vc, :], sgs.a[:, 2 * h + vc, :], 1.0, zg.a[:, 0:N], ADD, MUL, [sgs.T, zg.T], [sgs.T])
                        continue
                    sg = wk.next()
                    ACT(sg.a[:, 0:N], zg.a[:, 0:N], AF.Tanh, [zg.T], [sg.T], scale=0.5)
                    STT(sg.a[:, 0:N], sg.a[:, 0:N], 1.0, zg.a[:, 0:N], ADD, MUL, [sg.T, zg.T], [sg.T])
                    t1 = wk.next()
                    osrc = oT[h].a[:, vc, :] if not samp else oTs.a[:, 2 * h + vc, :]
                    oTt = oT[h].T if not samp else oTs.T
                    STT(t1.a[:, 0:N], osrc, lv.a[:, VGNG + vc:VGNG + vc + 1], rsb[h].a[:, 0:N], MUL, MUL,
                        [oTt, lv.T, rsb[h].T], [t1.T])
                    TTo(pbr[0].a[:, 2 * h + vc, 0:N], t1.a[:, 0:N], sg.a[:, 0:N], MUL, [t1.T, sg.T], [pbr[0].T])

            if PB <= 5:
                return
            hv = hvec[l]
            def rg2_tail(j, zrr, zi, zg, xc):
                a_ = wk.next()
                ACT(a_.a[:, 0:N], zrr.a[:, 0:N], AF.Tanh, [zrr.T, hv.T], [a_.T], bias=hv.a[:, VRGBA + j:VRGBA + j + 1], scale=0.5)
                ig = wk.next()
                ACT(ig.a[:, 0:N], zi.a[:, 0:N], AF.Tanh, [zi.T, hv.T], [ig.T], bias=hv.a[:, VRGBX + j:VRGBX + j + 1], scale=0.5)
                sg = wk.next()
                ACT(sg.a[:, 0:N], zg.a[:, 0:N], AF.Tanh, [zg.T], [sg.T], scale=0.5)
                ACT(a_.a[:, 0:N], a_.a[:, 0:N], AF.Exp, [a_.T, hc[l].T], [a_.T], bias=hc[l].a[:, j:j + 1], scale=hc[l].a[:, j:j + 1])
                sq_ = wk.next()
                STT(sq_.a[:, 0:N], a_.a[:, 0:N], 0.9999999, a_.a[:, 0:N], MIN, MUL, [a_.T], [sq_.T])
                ACT(sq_.a[:, 0:N], sq_.a[:, 0:N], AF.Sqrt, [sq_.T], [sq_.T], bias=0.25, scale=-0.25)
                STT(ig.a[:, 0:N], ig.a[:, 0:N], 1.0, xc.a[:, 0:N], ADD, MUL, [ig.T, xc.T], [ig.T])
                STT(sg.a[:, 0:N], sg.a[:, 0:N], 1.0, zg.a[:, 0:N], ADD, MUL, [sg.T, zg.T], [sg.T])
                TTo(ig.a[:, 0:N], ig.a[:, 0:N], sq_.a[:, 0:N], MUL, [ig.T, sq_.T], [ig.T])
                if not samp:
                    P.op("dve", (lambda o, d0, d1, ini: (lambda e: e.tensor_tensor_scan(out=o, data0=d0, data1=d1, initial=ini,
                                                                                        op0=MUL, op1=ADD)))(
                        xc.a[:, 0:N], a_.a[:, 0:N], ig.a[:, 0:N], chh[l].a[:, j:j + 1]),
                        [a_.T, ig.T, chh[l].T], [xc.T])
                    VCPY(chh[l].a[:, j:j + 1], xc.a[:, N - 1:N], [xc.T], [chh[l].T])
                    hsrc = xc.a[:, 0:N]
                    hT_ = xc.T
                else:
                    TTo(a_.a[:, 0:N], a_.a[:, 0:N], srgh.a[:, j, :], MUL, [a_.T, srgh.T], [a_.T])
                    TTo(nrgh.a[:, j, :], a_.a[:, 0:N], ig.a[:, 0:N], ADD, [a_.T, ig.T], [nrgh.T])
                    hsrc = nrgh.a[:, j, :]
                    hT_ = nrgh.T
                TTo(pbr[2].a[:, j, 0:N], hsrc, sg.a[:, 0:N], MUL, [hT_, sg.T], [pbr[2].T])

            pending = None
            for j in range(8):
                slotA = yield
                if samp:
                    sstep(2 * j)
                cw = [lv.a[:, VRGW + 4 * j + r:VRGW + 4 * j + r + 1] for r in range(4)]
                cb = lv.a[:, VRGCB + j:VRGCB + j + 1]
                zx = proj(slotA, 0, 128, hT, N)
                xc = xcd.next()
                if not samp:
                    ur = ub.next()
                    ACPY(ur.a[:, 3:3 + N], zx.a[:, 0:N], [zx.T], [ur.T])
                    ACPY(ur.a[:, 0:3], crg[l].a[:, j, :], [crg[l].T], [ur.T])
                    ACPY(crg[l].a[:, j, :], ur.a[:, N:N + 3], [ur.T], [crg[l].T])
                    ACT(xc.a[:, 0:N], ur.a[:, 0:N], AF.Identity, [ur.T, lv.T], [xc.T], bias=cb, scale=cw[0])
                    for r in range(1, 4):
                        STT(xc.a[:, 0:N], ur.a[:, r:r + N], cw[r], xc.a[:, 0:N], MUL, ADD, [ur.T, lv.T, xc.T], [xc.T])
                else:
                    ACPY(nrgc.a[:, j, 2, :], zx.a[:, 0:N], [zx.T], [nrgc.T])
                    VCPY(nrgc.a[:, j, 0:2, :], srgc.a[:, j, 1:3, :], [srgc.T], [nrgc.T])
                    TS(xc.a[:, 0:N], srgc.a[:, j, 0, :], cw[0], cb, MUL, ADD, [srgc.T, lv.T], [xc.T])
                    for r in range(1, 3):
                        STT(xc.a[:, 0:N], srgc.a[:, j, r, :], cw[r], xc.a[:, 0:N], MUL, ADD, [srgc.T, lv.T, xc.T], [xc.T])
                    STT(xc.a[:, 0:N], nrgc.a[:, j, 2, :], cw[3], xc.a[:, 0:N], MUL, ADD, [nrgc.T, lv.T, xc.T], [xc.T])
                if pending is not None:
                    rg2_tail(*pending)
                    pending = None
                ACPY(xcb.a[:, 0:N], xc.a[:, 0:N], [xc.T], [xcb.T])

                slot = slotA
                w0 = lv.a[:, VSCW + 3 * j + 0:VSCW + 3 * j + 1]
                w1 = lv.a[:, VSCW + 3 * j + 1:VSCW + 3 * j + 2]
                w2 = lv.a[:, VSCW + 3 * j + 2:VSCW + 3 * j + 3]
                zc = proj(slot, 128, 128, hT, N)
                tcb = wk.next()
                ACPY(tcb.a[:, 0:N], zc.a[:, 0:N], [zc.T], [tcb.T])
                zh = proj(slot, 256, 128, hT, N)
                uc = wk.next()
                if not samp:
                    u = ub.next()
                    TTo(u.a[:, 2:2 + N], zh.a[:, 0:N], tcb.a[:, 0:N], MUL, [zh.T, tcb.T], [u.T])
                    ACPY(u.a[:, 0:2], csc[l].a[:, j, :], [csc[l].T], [u.T])
                    ACPY(csc[l].a[:, j, :], u.a[:, N:N + 2], [u.T], [csc[l].T])
                    ACT(uc.a[:, 0:N], u.a[:, 0:N], AF.Identity, [u.T, lv.T], [uc.T], scale=w0)
                    STT(uc.a[:, 0:N], u.a[:, 1:1 + N], w1, uc.a[:, 0:N], MUL, ADD, [u.T, lv.T, uc.T], [uc.T])
                    STT(uc.a[:, 0:N], u.a[:, 2:2 + N], w2, uc.a[:, 0:N], MUL, ADD, [u.T, lv.T, uc.T], [uc.T])
                else:
                    TTo(nsc.a[:, j, 1, :], zh.a[:, 0:N], tcb.a[:, 0:N], MUL, [zh.T, tcb.T], [nsc.T])
                    VCPY(nsc.a[:, j, 0, :], ssc.a[:, j, 1, :], [ssc.T], [nsc.T])
                    TS(uc.a[:, 0:N], ssc.a[:, j, 0, :], w0, None, MUL, None, [ssc.T, lv.T], [uc.T])
                    STT(uc.a[:, 0:N], ssc.a[:, j, 1, :], w1, uc.a[:, 0:N], MUL, ADD, [ssc.T, lv.T, uc.T], [uc.T])
                    STT(uc.a[:, 0:N], nsc.a[:, j, 1, :], w2, uc.a[:, 0:N], MUL, ADD, [nsc.T, lv.T, uc.T], [uc.T])
                zb_ = proj(slot, 384, 128, hT, N)
                TTo(uc.a[:, 0:N], zb_.a[:, 0:N], uc.a[:, 0:N], MUL, [zb_.T, uc.T], [uc.T])
                slotB = yield
                if samp:
                    sstep(2 * j + 1)
                zg = proj(slotB, 0, 128, hT, N)
                sg = wk.next()
                ACT(sg.a[:, 0:N], zg.a[:, 0:N], AF.Tanh, [zg.T], [sg.T], scale=0.5)
                STT(sg.a[:, 0:N], sg.a[:, 0:N], 1.0, zg.a[:, 0:N], ADD, MUL, [sg.T, zg.T], [sg.T])
                TTo(pbr[1].a[:, j, 0:N], uc.a[:, 0:N], sg.a[:, 0:N], MUL, [uc.T, sg.T], [pbr[1].T])

                slot = slotB
                zrr = zr.next()
                MM(zrr.a[:, 0:N], rgwa[l].a[:, j, :], xcb.a[:, 0:N], True, True, [rgwa[l].T, xcb.T], [zrr.T])
                zi = zr.next()
                MM(zi.a[:, 0:N], rgwx[l].a[:, j, :], xcb.a[:, 0:N], True, True, [rgwx[l].T, xcb.T], [zi.T])
                zg = proj(slot, 128, 128, hT, N)
                if samp or not cx.defer:
                    rg2_tail(j, zrr, zi, zg, xc)
                else:
                    pending = (j, zrr, zi, zg, xc)
            if pending is not None:
                rg2_tail(*pending)
                pending = None
            if samp:
                sstep(NS)
                for h in range(4):
                    rms_stats(lambda vc: oTs.a[:, 2 * h + vc, :], 2, N, 1.0 / 256, [oTs.T], rsb[h])
                for h in range(4):
                    for vc in range(2):
                        t1 = wk.next()
                        STT(t1.a[:, 0:N], oTs.a[:, 2 * h + vc, :], lv.a[:, VGNG + vc:VGNG + vc + 1], rsb[h].a[:, 0:N], MUL, MUL,
                            [oTs.T, lv.T, rsb[h].T], [t1.T])
                        TTo(pbr[0].a[:, 2 * h + vc, 0:N], t1.a[:, 0:N], sgs.a[:, 2 * h + vc, :], MUL, [t1.T, sgs.T], [pbr[0].T])
            if last_tile:
                DMA("sp", f"csc{l}", [(osc_p[l], csc[l].a)], [csc[l].T], [])
                DMA("sp", f"crg{l}", [(orgc_p[l], crg[l].a)], [crg[l].T], [])
                DMA("sp", f"chh{l}", [(orgh_p[l], chh[l].a)], [chh[l].T], [])
            if samp:
                DMA("sp", "nsc", [(osc_s[l], nsc.a)], [nsc.T], [])
                DMA("sp", "nrgc", [(orgc_s[l], nrgc.a)], [nrgc.T], [])
                DMA("sp", "nrgh", [(orgh_s[l], nrgh.a)], [nrgh.T], [])

            if PB <= 7:
                return
            for i in range(8):
                s1 = yield
                for br in range(3):
                    zm = proj(s1, 128 * br, 128, hT, N)
                    ACT(gbuf[br].a[:, 0:N], zm.a[:, 0:N], AF.Tanh, [zm.T, hv.T], [gbuf[br].T],
                        bias=hv.a[:, VBM + 8 * br + i:VBM + 8 * br + i + 1], scale=0.5)
                acc = wk.next()
                s2 = yield
                for br in range(3):
                    zy = proj(s2, 128 * br, 128, pbr[br], N)
                    if br == 0:
                        STT(acc.a[:, 0:N], gbuf[0].a[:, 0:N], 1.0, zy.a[:, 0:N], ADD, MUL, [zy.T, gbuf[0].T], [acc.T])
                    else:
                        STT(gbuf[br].a[:, 0:N], gbuf[br].a[:, 0:N], 1.0, zy.a[:, 0:N], ADD, MUL, [zy.T, gbuf[br].T], [gbuf[br].T])
                        if br == 1:
                            TTo(acc.a[:, 0:N], acc.a[:, 0:N], gbuf[1].a[:, 0:N], ADD, [acc.T, gbuf[1].T], [acc.T])
                        else:
                            TTo(mg.a[:, i, 0:N], acc.a[:, 0:N], gbuf[2].a[:, 0:N], ADD, [acc.T, gbuf[2].T], [mg.T])

            for i in range(8):
                if i % 4 == 0:
                    so_ = yield
                zo = proj(so_, (i % 4) * 128, 128, mg, N)
                if not samp:
                    STT(x.a[:, i, 0:N], zo.a[:, 0:N], gqp[l].a[:, i:i + 1], x.a[:, i, 0:N], MUL, ADD, [zo.T, gqp[l].T, x.T], [x.T])
                else:
                    t = wk.next()
                    TTo(t.a[:, 0:N], zo.a[:, 0:N], gqs[l].a[:, i, :], MUL, [zo.T, gqs[l].T], [t.T])
                    TTo(x.a[:, i, 0:N], x.a[:, i, 0:N], t.a[:, 0:N], ADD, [x.T, t.T], [x.T])

        def final_norm(x, N, dst, dst_dram, key):
            rms_stats(lambda k: x.a[:, k, 0:N], 8, N, 1.0 / D, [x.T])
            for k in range(8):
                STT(dst.a[:, k, 0:N], x.a[:, k, 0:N], lvec[0].a[:, VFG + k:VFG + k + 1], rstd.a[:, 0:N], MUL, MUL,
                    [x.T, lvec[0].T, rstd.T], [dst.T])
            DMA("sp", key, [(dst_dram, dst.a[:, :, 0:N])], [dst.T], [])

        def run_blocks(l, ctxs):
            gens = [block(l, c) for c in ctxs]
            active = []
            for g in gens:
                try:
                    next(g)
                    active.append(g)
                except StopIteration:
                    pass
            while active:
                slot = wq.get()
                nxt = []
                for g in active:
                    try:
                        g.send(slot)
                        nxt.append(g)
                    except StopIteration:
                        pass
                active = nxt

        cs_ctx = Ctx()
        cs_ctx.x, cs_ctx.N, cs_ctx.samp, cs_ctx.tile_idx = xs, NS, True, 0
        cs_ctx.hT, cs_ctx.pbr, cs_ctx.mg, cs_ctx.gb, cs_ctx.wk, cs_ctx.xcd, cs_ctx.xcb, cs_ctx.rsb = (
            hT_s, pbr_s, mg_s, gb_s, wk_s, xcd_s, xcb_s, rsb_s)

        def prompt_ctx(xbuf, tt):
            c = Ctx()
            c.x, c.N, c.samp, c.tile_idx = xbuf, TT, False, tt
            c.hT, c.pbr, c.mg, c.gb, c.wk, c.xcd, c.xcb, c.rsb = hT, pbr, mg, gbuf, wk, xcd, xcb, [rstd, gbuf[0], gbuf[1], gbuf[2]]
            return c

        DMA("sp", "xs", [(xs.a, xsT[:, :, :])], [], [xs.T])
        xcur = xt.next()
        DMA("sp", xcur.T.name, [(xcur.a, xT[:, :, 0:TT])], [], [xcur.T])
        for tt in range(NT):
            if tt + 1 < NT:
                xnext = xt.next()
                DMA("sp", xnext.T.name, [(xnext.a, xT[:, :, (tt + 1) * TT:(tt + 2) * TT])], [], [xnext.T])
            for l in range(L):
                ctxs = [prompt_ctx(xcur, tt)]
                if tt == 0:
                    DMA("sp", "ssc", [(ssc.a, sscf[l])], [], [ssc.T])
                    DMA("sp", "srgc", [(srgc.a, srgcf[l])], [], [srgc.T])
                    DMA("sp", "srgh", [(srgh.a, srghf[l])], [], [srgh.T])
                    ctxs.append(cs_ctx)
                for c_ in ctxs:
                    c_.defer = (len(ctxs) == 1)
                run_blocks(l, ctxs)
            final_norm(xcur, TT, xcur, yT[:, :, tt * TT:(tt + 1) * TT], xcur.T.name)
            if tt == 0:
                final_norm(xs, NS, xs, ysT[:, :, :], "xs")
            if tt + 1 < NT:
                xcur = xnext
        assert STAGE < 9 or wq.pos == len(wq.plan), (wq.pos, len(wq.plan))
        outs = [b.T for b in xt.bufs] + [xs.T, nsc.T, nrgc.T, nrgh.T] + [b.T for b in Snw]
        for l in range(L):
            outs += [csc[l].T, crg[l].T, chh[l].T] + [S[l][h].T for h in range(4)]
        P.wait_all("sp", outs)
        P.emit(st)
    return nc


_CACHE = {}


def _fm(v):
    return np.ascontiguousarray(np.swapaxes(v.reshape(v.shape[:-1] + (8, 128)), -1, -2))


def kernel(x_prompt, x_sample, c_prompt, c_sample, state_gla, state_sc_conv, state_rg_conv, state_rg_h,
           w_ada, b_ada, norm_gain, w_in, gla_w_alpha, gla_b_alpha, gla_norm_gain, gla_w_branch, sc_conv_w,
           sc_w_branch, rg_conv_w, rg_conv_b, rg_w_a, rg_b_a, rg_w_x, rg_b_x, rg_lambda, rg_w_branch,
           b_merge, w_out, final_gain):
    f = lambda a: np.ascontiguousarray(np.asarray(a, dtype=np.float32))
    x_prompt, x_sample, c_prompt, c_sample = f(x_prompt), f(x_sample), f(c_prompt), f(c_sample)
    state_gla, state_sc_conv, state_rg_conv, state_rg_h = f(state_gla), f(state_sc_conv), f(state_rg_conv), f(state_rg_h)
    n = 8
    if "nc" not in _CACHE:
        _CACHE["nc"] = build_program()
    nc = _CACHE["nc"]
    lvec = np.zeros((L, 128, NV), np.float32)
    for l in range(L):
        lvec[l, :, VNG:VNG + 8] = _fm(f(norm_gain)[l])
        lvec[l, :, VBADA:VBADA + 24] = f(b_ada)[l].reshape(24, 128).T
        lvec[l, :, VGNG:VGNG + 2] = f(gla_norm_gain)[l].reshape(2, 128).T
        lvec[l, :, VSCW:VSCW + 24] = f(sc_conv_w)[l].reshape(3, 8, 128).transpose(2, 1, 0).reshape(128, 24)
        lvec[l, :, VRGW:VRGW + 32] = f(rg_conv_w)[l].reshape(4, 8, 128).transpose(2, 1, 0).reshape(128, 32)
        lvec[l, :, VRGCB:VRGCB + 8] = _fm(f(rg_conv_b)[l])
        lvec[l, :, VRGBA:VRGBA + 8] = _fm(f(rg_b_a)[l])
        lvec[l, :, VRGBX:VRGBX + 8] = _fm(f(rg_b_x)[l])
        lvec[l, :, VLAM:VLAM + 8] = _fm(f(rg_lambda)[l])
        lvec[l, :, VBM:VBM + 24] = f(b_merge)[l].reshape(24, 128).T
        lvec[l, :, VFG:VFG + 8] = _fm(f(final_gain))
    gwa = np.concatenate([f(gla_w_alpha), f(gla_b_alpha)[:, None, :]], axis=1)
    s_idx = np.arange(128)[:, None]
    t_idx = np.arange(128)[None, :]
    consts = np.concatenate([
        np.eye(128, dtype=np.float32),
        np.ones((128, 128), np.float32),
        np.where(s_idx <= t_idx, -1.0 / 16.0, 0.0).astype(np.float32),
        np.where(s_idx > t_idx, -1.0 / 16.0, 0.0).astype(np.float32),
        np.where(s_idx <= t_idx, 1.0, 0.0).astype(np.float32),
    ], axis=1)
    shared = {
        "w_ada": f(w_ada), "w_in": f(w_in), "wgla": f(gla_w_branch), "wsc": f(sc_w_branch), "wrg": f(rg_w_branch),
        "w_out": f(w_out), "rgwa": f(rg_w_a), "rgwx": f(rg_w_x), "gwa": np.ascontiguousarray(gwa),
        "lvec": lvec, "consts": np.ascontiguousarray(consts),
    }
    in_maps = []
    for c in range(n):
        sl = slice(c * NS, (c + 1) * NS)
        m = dict(shared)
        m["xT"] = np.ascontiguousarray(x_prompt[c].reshape(2048, 8, 128).transpose(2, 1, 0))
        m["xsT"] = np.ascontiguousarray(x_sample[sl, 0, :].reshape(NS, 8, 128).transpose(2, 1, 0))
        cc = np.concatenate([c_prompt[c:c + 1], c_sample[sl]], axis=0)
        m["cT"] = np.ascontiguousarray(cc.reshape(1 + NS, 8, 128).transpose(2, 1, 0))
        m["sgla"] = np.ascontiguousarray(state_gla[:, sl])
        m["sscf"] = np.ascontiguousarray(state_sc_conv[:, sl].reshape(L, NS, 2, 8, 128).transpose(0, 4, 3, 2, 1))
        m["srgcf"] = np.ascontiguousarray(state_rg_conv[:, sl].reshape(L, NS, 3, 8, 128).transpose(0, 4, 3, 2, 1))
        m["srghf"] = np.ascontiguousarray(state_rg_h[:, sl].reshape(L, NS, 8, 128).transpose(0, 3, 2, 1))
        in_maps.append(m)
    if os.environ.get("KCORES"):
        n1 = int(os.environ["KCORES"])
        res = run_bass_kernel_spmd(nc, in_maps[:n1], core_ids=list(range(n1)), trace=bool(os.environ.get("KTRACE")))
        return res
    res = run_bass_kernel_spmd(nc, in_maps, core_ids=list(range(n)))
    R = res.results
    B = 8
    y_prompt = np.stack([R[c]["yT"].transpose(2, 1, 0).reshape(2048, D) for c in range(n)], axis=0)
    y_sample = np.concatenate([R[c]["ysT"].transpose(2, 1, 0).reshape(NS, 1, D) for c in range(n)], axis=0)
    gla_p = np.stack([R[c]["ogla_p"] for c in range(n)], axis=1)
    sc_p = np.stack([R[c]["osc_p"].transpose(0, 3, 2, 1).reshape(L, 2, D) for c in range(n)], axis=1)
    rgc_p = np.stack([R[c]["orgc_p"].transpose(0, 3, 2, 1).reshape(L, 3, D) for c in range(n)], axis=1)
    rgh_p = np.stack([R[c]["orgh_p"].transpose(0, 2, 1).reshape(L, D) for c in range(n)], axis=1)
    gla_s = np.concatenate([R[c]["ogla_s"] for c in range(n)], axis=1)
    sc_s = np.concatenate([R[c]["osc_s"].transpose(0, 4, 3, 2, 1).reshape(L, NS, 2, D) for c in range(n)], axis=1)
    rgc_s = np.concatenate([R[c]["orgc_s"].transpose(0, 4, 3, 2, 1).reshape(L, NS, 3, D) for c in range(n)], axis=1)
    rgh_s = np.concatenate([R[c]["orgh_s"].transpose(0, 3, 2, 1).reshape(L, NS, D) for c in range(n)], axis=1)
    outs = (y_prompt, y_sample, gla_p, sc_p, rgc_p, rgh_p, gla_s, sc_s, rgc_s, rgh_s)
    return tuple(np.ascontiguousarray(o.astype(np.float32)) for o in outs)
```

```python
from contextlib import ExitStack
import os
import numpy as np
import concourse.bass as bass
import concourse.mybir as mybir
from concourse.bass_utils import run_bass_kernel_spmd

F32 = mybir.dt.float32
BF16 = mybir.dt.bfloat16
AF = mybir.ActivationFunctionType
ALU = mybir.AluOpType

D = 1024
L = 2
TT = 512
NT = 4
NS = 16
NV = 160
N_IN = 12304
OQ, OK_, OV, OGG, OLR = 0, 512, 1024, 2048, 3072
OB, OC, OH, OGS, OX, OGR, OM = 3088, 4112, 5136, 6160, 7184, 8208, 9232
VNG, VBADA, VGNG, VSCW, VRGW, VRGCB, VRGBA, VRGBX, VLAM, VBM, VFG = 0, 8, 32, 34, 58, 90, 98, 106, 114, 122, 146
EPS = 1e-6


class Tile:
    __slots__ = ("name", "w", "r")

    def __init__(self, name):
        self.name = name
        self.w = None
        self.r = []


class Buf:
    __slots__ = ("a", "T")

    def __init__(self, a, T):
        self.a = a
        self.T = T


class Prog:
    ENGS = ("pe", "act", "dve", "pool", "sp")

    def __init__(self, nc):
        self.nc = nc
        self.streams = {e: [] for e in self.ENGS}
        self.dma_cnt = {}
        self.sig = {e: set() for e in self.ENGS}
        self.window = {"act": 1 << 30, "dve": 1 << 30, "pool": 1 << 30}

    def _collect(self, eng, idx, reads, writes):
        deps = set()
        for t in reads:
            if t.w is not None:
                deps.add(t.w)
        for t in writes:
            if t.w is not None:
                deps.add(t.w)
            deps.update(t.r)
        out = []
        for d in deps:
            if d[0] == "e" and d[1] == eng:
                if eng in ("pe", "sp"):
                    continue
                if idx - d[2] > self.window[eng]:
                    continue
            out.append(d)
        for d in out:
            if d[0] == "e":
                self.sig[d[1]].add(d[2])
        return out

    def _mark(self, me, reads, writes):
        for t in reads:
            t.r.append(me)
            if len(t.r) > 64:
                t.r = t.r[-48:]
        for t in writes:
            t.w = me
            t.r = []

    def op(self, eng, fn, reads=(), writes=()):
        idx = len(self.streams[eng])
        deps = self._collect(eng, idx, reads, writes)
        self.streams[eng].append((fn, deps, None))
        self._mark(("e", eng, idx), reads, writes)

    def dma(self, eng, fns, key, reads=(), writes=()):
        idx = len(self.streams[eng])
        deps = self._collect(eng, idx, reads, writes)
        cnt = self.dma_cnt.get(key, 0)
        for i, fn in enumerate(fns):
            cnt += 16
            self.streams[eng].append((fn, deps if i == 0 else [], key))
        self.dma_cnt[key] = cnt
        self._mark(("s", key, cnt), reads, writes)

    def wait_all(self, eng, tiles):
        idx = len(self.streams[eng])
        deps = self._collect(eng, idx, (), tiles)
        self.streams[eng].append((None, deps, None))

    def emit(self, stack):
        nc = self.nc
        esem = {e: stack.enter_context(nc.semaphore("S_" + e)) for e in self.ENGS if e != "sp"}
        dsem = {k: stack.enter_context(nc.semaphore("D_" + str(k))) for k in self.dma_cnt}
        cnt_at = {}
        for e in self.ENGS:
            c = 0
            m = {}
            for i in range(len(self.streams[e])):
                if i in self.sig[e]:
                    c += 1
                    m[i] = c
            cnt_at[e] = m
        block = stack.enter_context(nc.Block())

        def run(eng_name):
            def body(eng):
                waited = {}
                for i, (fn, deps, key) in enumerate(self.streams[eng_name]):
                    for d in deps:
                        if d[0] == "e":
                            sem, val, k = esem[d[1]], cnt_at[d[1]][d[2]], ("e", d[1])
                        else:
                            sem, val, k = dsem[d[1]], d[2], ("s", d[1])
                        if waited.get(k, 0) >= val:
                            continue
                        waited[k] = val
                        eng.wait_ge(sem, val)
                    if fn is None:
                        continue
                    ins = fn(eng)
                    if key is not None:
                        ins.then_inc(dsem[key], 16)
                    elif i in self.sig[eng_name]:
                        ins.then_inc(esem[eng_name], 1)
            return body

        block.tensor(run("pe"))
        block.scalar(run("act"))
        block.vector(run("dve"))
        block.gpsimd(run("pool"))
        block.sync(run("sp"))


class Ring:
    def __init__(self, bufs):
        self.bufs = bufs
        self.i = 0

    def next(self):
        b = self.bufs[self.i % len(self.bufs)]
        self.i += 1
        return b


def build_program():
    nc = bass.Bass("TRN2", target_bir_lowering=False)

    def din(name, shape):
        return nc.dram_tensor(name, list(shape), F32, kind="ExternalInput").ap()

    def dout(name, shape):
        return nc.dram_tensor(name, list(shape), F32, kind="ExternalOutput").ap()

    xT = din("xT", [128, 8, NT * TT])
    xsT = din("xsT", [128, 8, NS])
    cT = din("cT", [128, 8, 1 + NS])
    sgla = din("sgla", [L, NS, 4, 128, 256])
    sscf = din("sscf", [L, 128, 8, 2, NS])
    srgcf = din("srgcf", [L, 128, 8, 3, NS])
    srghf = din("srghf", [L, 128, 8, NS])
    w_ada = din("w_ada", [L, D, 3 * D])
    w_in = din("w_in", [L, D, N_IN])
    wbr = [din("wgla", [L, D, D]), din("wsc", [L, D, D]), din("wrg", [L, D, D])]
    w_out = din("w_out", [L, D, D])
    rgwa_d = din("rgwa", [L, 8, 128, 128])
    rgwx_d = din("rgwx", [L, 8, 128, 128])
    gwa_d = din("gwa", [L, 17, 512])
    lvec_d = din("lvec", [L, 128, NV])
    consts_d = din("consts", [128, 5 * 128])

    yT = dout("yT", [128, 8, NT * TT])
    ysT = dout("ysT", [128, 8, NS])
    ogla_p = dout("ogla_p", [L, 4, 128, 256])
    osc_p = dout("osc_p", [L, 128, 8, 2])
    orgc_p = dout("orgc_p", [L, 128, 8, 3])
    orgh_p = dout("orgh_p", [L, 128, 8])
    ogla_s = dout("ogla_s", [L, NS, 4, 128, 256])
    osc_s = dout("osc_s", [L, 128, 8, 2, NS])
    orgc_s = dout("orgc_s", [L, 128, 8, 3, NS])
    orgh_s = dout("orgh_s", [L, 128, 8, NS])

    with ExitStack() as st:
        P = Prog(nc)
        STAGE = int(os.environ.get("KSTAGE", "9"))

        def sb(name, shape, dt=F32):
            t = st.enter_context(nc.sbuf_tensor("s_" + name, list(shape), dt))
            return Buf(t[:], Tile(name))

        def ps(name, shape):
            t = st.enter_context(nc.psum_tensor("p_" + name, list(shape), F32))
            return Buf(t[:], Tile(name))

        def MM(out, lhsT, rhs, start, stop, R, W):
            P.op("pe", lambda e: e.matmul(out, lhsT=lhsT, rhs=rhs, start=start, stop=stop), R, W)

        def ACT(out, in_, func, R, W, bias=None, scale=None):
            kw = {}
            if bias is not None:
                kw["bias"] = bias
            if scale is not None:
                kw["scale"] = scale
            P.op("act", lambda e: e.activation(out=out, in_=in_, func=func, **kw), R, W)

        def ACPY(out, in_, R, W):
            P.op("act", lambda e: e.copy(out=out, in_=in_), R, W)

        def VCPY(out, in_, R, W):
            P.op("dve", lambda e: e.tensor_copy(out=out, in_=in_), R, W)

        def TTo(out, a, b, op, R, W):
            P.op("dve", lambda e: e.tensor_tensor(out=out, in0=a, in1=b, op=op), R, W)

        def TS(out, a, s1, s2, op0, op1, R, W):
            if op1 is None:
                P.op("dve", lambda e: e.tensor_scalar(out=out, in0=a, scalar1=s1, scalar2=None, op0=op0), R, W)
            else:
                P.op("dve", lambda e: e.tensor_scalar(out=out, in0=a, scalar1=s1, scalar2=s2, op0=op0, op1=op1), R, W)

        def STT(out, a, s, b, op0, op1, R, W):
            P.op("dve", lambda e: e.scalar_tensor_tensor(out=out, in0=a, scalar=s, in1=b, op0=op0, op1=op1), R, W)

        def PTS(out, a, s1, s2, op0, op1, R, W):
            P.op("pool", lambda e: e.tensor_scalar(out=out, in0=a, scalar1=s1, scalar2=s2, op0=op0, op1=op1), R, W)

        def PSTT(out, a, s, b, op0, op1, R, W):
            P.op("pool", lambda e: e.scalar_tensor_tensor(out=out, in0=a, scalar=s, in1=b, op0=op0, op1=op1), R, W)

        def PCPY(out, in_, R, W):
            P.op("pool", lambda e: e.tensor_copy(out=out, in_=in_), R, W)

        def DMA(eng, key, pairs, R, W):
            fns = []
            for (o, i) in pairs:
                fns.append((lambda o, i: (lambda e: e.dma_start(out=o, in_=i)))(o, i))
            P.dma(eng, fns, key, R, W)

        MUL, ADD, MIN = ALU.mult, ALU.add, ALU.min

        consts = sb("consts", [128, 5 * 128])
        ident = consts.a[:, 0:128]
        onesf = consts.a[:, 128:256]
        triNeg = consts.a[:, 256:384]
        triUpNeg = consts.a[:, 384:512]
        maskT = consts.a[:, 512:640]
        onesb = sb("onesb", [128, 128], BF16)
        lvec = [sb(f"lvec{l}", [128, NV]) for l in range(L)]
        cneg = [sb(f"cneg{l}", [128, 8]) for l in range(L)]
        gwa = [sb(f"gwa{l}", [17, 512]) for l in range(L)]
        rgwa = [sb(f"rgwa{l}", [128, 8, 128], BF16) for l in range(L)]
        rgwx = [sb(f"rgwx{l}", [128, 8, 128], BF16) for l in range(L)]
        modT = [sb(f"modT{l}", [128, 24, 1 + NS]) for l in range(L)]
        Ap = [sb(f"Ap{l}", [128, 8]) for l in range(L)]
        As = [sb(f"As{l}", [128, 8, NS]) for l in range(L)]
        cs = sb("cs", [128, 8, 1 + NS])
        scb = sb("scb", [128, 8, 1 + NS], BF16)

        wslots = Ring([sb(f"ws{i}", [128, 8, 512], BF16) for i in range(3)])
        xt = Ring([sb(f"xt{i}", [128, 8, TT]) for i in range(2)])
        hT = sb("hT", [128, 8, TT], BF16)
        pbr0 = sb("pbr0", [128, 8, TT], BF16)
        raw1 = sb("raw1", [128, 4 * 512])
        raw2 = sb("raw2", [128, 4 * 512])
        pbr = [pbr0,
               Buf(raw1.a.bitcast(BF16).rearrange("p (k n) -> p k n", k=8), raw1.T),
               Buf(raw2.a.bitcast(BF16).rearrange("p (k n) -> p k n", k=8), raw2.T)]
        mg = sb("mg", [128, 8, TT], BF16)
        sqb = sb("sqb", [128, 4, TT], BF16)
        rstd = sb("rstd", [128, TT])
        wk = Ring([sb(f"wk{i}", [128, TT]) for i in range(4)])
        xcd = Ring([sb(f"xcd{i}", [128, TT]) for i in range(2)])
        gbuf = [sb(f"gb{i}", [128, TT]) for i in range(3)]
        zlr = sb("zlr", [17, TT])
        sp_tok = Buf(raw1.a.rearrange("p (c n) -> p c n", c=4), raw1.T)
        ekd = Buf(raw2.a.rearrange("p (c n) -> p c n", c=4), raw2.T)
        eb = [sb(f"eb{h}", [128, 4]) for h in range(4)]
        qk = sb("qk", [128, 8, TT], BF16)
        qe = [Buf(qk.a[:, h, :], qk.T) for h in range(4)]
        ke = [Buf(qk.a[:, 4 + h, :], qk.T) for h in range(4)]
        hvec = [sb(f"hvec{l}", [128, NV]) for l in range(L)]
        hc = [sb(f"hc{l}", [128, 8]) for l in range(L)]
        gqp = [sb(f"gqp{l}", [128, 8]) for l in range(L)]
        gqs = [sb(f"gqs{l}", [128, 8, NS]) for l in range(L)]
        vtok = [Buf(None, mg.T) for h in range(4)]

        def vt(h, c):
            return mg.a[:, 2 * h + c // 2, (c % 2) * 256:(c % 2) * 256 + 256]
        kdtok = [sb(f"kdtok{h}", [128, 4, 128], BF16) for h in range(4)]
        attm = Ring([sb(f"attm{i}", [128, 128], BF16) for i in range(4)])
        oT = [sb(f"oT{h}", [128, 2, TT]) for h in range(4)]
        S = [[sb(f"S{l}_{h}", [128, 256]) for h in range(4)] for l in range(L)]
        Sb = [[sb(f"Sb{l}_{h}", [128, 256], BF16) for h in range(4)] for l in range(L)]
        csc = [sb(f"csc{l}", [128, 8, 2]) for l in range(L)]
        crg = [sb(f"crg{l}", [128, 8, 3]) for l in range(L)]
        chh = [sb(f"chh{l}", [128, 8]) for l in range(L)]
        ub = Ring([sb(f"ub{i}", [128, TT + 3]) for i in range(2)])
        xcb = sb("xcb", [128, TT], BF16)
        xs = sb("xs", [128, 8, NS])
        ssc = sb("ssc", [128, 8, 2, NS])
        srgc = sb("srgc", [128, 8, 3, NS])
        srgh = sb("srgh", [128, 8, NS])
        nsc = sb("nsc", [128, 8, 2, NS])
        nrgc = sb("nrgc", [128, 8, 3, NS])
        nrgh = sb("nrgh", [128, 8, NS])
        aTs = sb("aTs", [128, 4, NS])
        qss = sb("qss", [128, 4, NS], BF16)
        kvs = sb("kvs", [NS, 4, 384], BF16)
        km4 = Ring([Buf(sqb.a[0:NS, i, :].rearrange("p (h d) -> p h d", h=4), Tile(f"km4_{i}")) for i in range(2)])
        Sin = [Buf(oT[i].a.rearrange("p a (b c) -> p (a b) c", b=2), oT[i].T) for i in range(2)]
        Snw = [Buf(oT[i].a.rearrange("p a (b c) -> p (a b) c", b=2), oT[i].T) for i in range(2, 4)]
        snbq = [Buf(qk.a[:, 2 * i:2 * i + 2, :].rearrange("p k (a v) -> p (k a) v", a=2), Tile(f"snbq{i}")) for i in range(2)]
        sgs = sb("sgs", [128, 8, NS])
        snb4 = Ring([Buf(ub.bufs[i].a[:, 0:512].bitcast(BF16).rearrange("p (h v) -> p h v", h=4), ub.bufs[i].T) for i in range(2)])
        oTs = sb("oTs", [128, 8, NS])
        hT_s = sb("hT_s", [128, 8, NS], BF16)
        pbr_s = [sb(f"pbr_s{i}", [128, 8, NS], BF16) for i in range(3)]
        mg_s = sb("mg_s", [128, 8, NS], BF16)
        rsb_s = [sb(f"rsb_s{i}", [128, NS]) for i in range(4)]
        gb_s = [sb(f"gb_s{i}", [128, NS]) for i in range(3)]
        wk_s = Ring([sb(f"wk_s{i}", [128, NS]) for i in range(4)])
        xcd_s = Ring([sb(f"xcd_s{i}", [128, NS]) for i in range(2)])
        xcb_s = sb("xcb_s", [128, NS], BF16)

        zr = Ring([ps(f"z{i}", [128, 512]) for i in range(8)])

        class _SmView:
            def next(self):
                b = zr.next()
                return Buf(b.a[:, 0:256], b.T)
        smr = _SmView()

        def wview(w, l):
            return w[l].rearrange("(k p) n -> p k n", p=128)

        def wload(pieces):
            slot = wslots.next()
            pairs = [(slot.a[:, :, d0:d0 + n], src) for (src, d0, n) in pieces]
            if os.environ.get("KNODMA") and wslots.i > 3:
                return slot
            DMA("pool", slot.T.name, pairs, [], [slot.T])
            return slot

        class WQ:
            def __init__(self):
                self.plan = []
                self.issued = []
                self.pos = 0

            def add(self, pieces):
                self.plan.append(pieces)

            def get(self):
                while len(self.issued) < min(len(self.plan), self.pos + 3):
                    self.issued.append(wload(self.plan[len(self.issued)]))
                s = self.issued[self.pos]
                self.pos += 1
                return s

        wq = WQ()

        def plan_layer(l):
            wi = wview(w_in, l)
            wq.add([(wi[:, :, OLR:OLR + 16], 0, 16)])
            for h in range(4):
                wq.add([(wi[:, :, OQ + 128 * h:OQ + 128 * h + 128], 0, 128),
                        (wi[:, :, OK_ + 128 * h:OK_ + 128 * h + 128], 128, 128),
                        (wi[:, :, OV + 256 * h:OV + 256 * h + 256], 256, 256)])
            for h in range(4):
                wq.add([(wi[:, :, OGG + 256 * h:OGG + 256 * h + 256], 0, 256)])
            for j in range(8):
                wq.add([(wi[:, :, o + 128 * j:o + 128 * j + 128], 128 * q, 128) for q, o in enumerate((OX, OC, OH, OB))])
                wq.add([(wi[:, :, o + 128 * j:o + 128 * j + 128], 128 * q, 128) for q, o in enumerate((OGS, OGR))])
            for i in range(8):
                wq.add([(wi[:, :, OM + 1024 * br + 128 * i:OM + 1024 * br + 128 * i + 128], 128 * br, 128) for br in range(3)])
                wq.add([(wview(wbr[br], l)[:, :, 128 * i:128 * i + 128], 128 * br, 128) for br in range(3)])
            wo = wview(w_out, l)
            for g in range(2):
                wq.add([(wo[:, :, 512 * g:512 * g + 512], 0, 512)])

        for l in range(L):
            wa = wview(w_ada, l)
            for g in range(6):
                wq.add([(wa[:, :, 512 * g:512 * g + 512], 0, 512)])
        for _pass in range(NT):
            for l in range(L):
                plan_layer(l)

        DMA("sp", "consts", [(consts.a, consts_d[:, :])], [], [consts.T])
        for l in range(L):
            DMA("sp", f"lvec{l}", [(lvec[l].a, lvec_d[l])], [], [lvec[l].T])
            DMA("sp", f"gwa{l}", [(gwa[l].a, gwa_d[l])], [], [gwa[l].T])
            DMA("pool", f"rgwa{l}", [(rgwa[l].a, rgwa_d[l].rearrange("n i j -> i n j"))], [], [rgwa[l].T])
            DMA("pool", f"rgwx{l}", [(rgwx[l].a, rgwx_d[l].rearrange("n i j -> i n j"))], [], [rgwx[l].T])
        DMA("sp", "cs", [(cs.a, cT[:, :, :])], [], [cs.T])
        VCPY(onesb.a, onesf, [consts.T], [onesb.T])
        P.op("dve", lambda e: e.memset(zlr.a, 1.0), [], [zlr.T])
        for l in range(L if STAGE >= -2 else 0):
            for h in range(4):
                P.op("dve", (lambda a: (lambda e: e.memset(a, 0.0)))(S[l][h].a), [], [S[l][h].T])
                P.op("dve", (lambda a: (lambda e: e.memset(a, 0.0)))(Sb[l][h].a), [], [Sb[l][h].T])
            P.op("dve", (lambda a: (lambda e: e.memset(a, 0.0)))(csc[l].a), [], [csc[l].T])
            P.op("dve", (lambda a: (lambda e: e.memset(a, 0.0)))(crg[l].a), [], [crg[l].T])
            P.op("dve", (lambda a: (lambda e: e.memset(a, 0.0)))(chh[l].a), [], [chh[l].T])
        ACT(scb.a, cs.a, AF.Silu, [cs.T], [scb.T])
        for l in range(L if STAGE >= -1 else 0):
            lv = lvec[l]
            ACT(cneg[l].a, lv.a[:, VLAM:VLAM + 8], AF.Exp, [lv.T], [cneg[l].T], scale=-1.0)
            ACT(cneg[l].a, cneg[l].a, AF.Ln, [cneg[l].T], [cneg[l].T], bias=1.0)
            TS(cneg[l].a, cneg[l].a, -8.0, None, MUL, None, [cneg[l].T], [cneg[l].T])
            TS(hc[l].a, cneg[l].a, 0.5, None, MUL, None, [cneg[l].T], [hc[l].T])
            TS(hvec[l].a, lv.a, 0.5, None, MUL, None, [lv.T], [hvec[l].T])
            for g in range(6):
                slot = wq.get()
                for cc in range(4):
                    ch = g * 4 + cc
                    z = zr.next()
                    for k in range(8):
                        MM(z.a[:, 0:1 + NS], slot.a[:, k, cc * 128:cc * 128 + 128], scb.a[:, k, :], k == 0, k == 7,
                           [slot.T, scb.T], [z.T])
                    ACT(modT[l].a[:, ch, :], z.a[:, 0:1 + NS], AF.Identity, [z.T, lv.T], [modT[l].T],
                        bias=lv.a[:, VBADA + ch:VBADA + ch + 1])
            TS(Ap[l].a, modT[l].a[:, 8:16, 0], 1.0, None, ADD, None, [modT[l].T], [Ap[l].T])
            TTo(Ap[l].a, Ap[l].a, lv.a[:, VNG:VNG + 8], MUL, [Ap[l].T, lv.T], [Ap[l].T])
            for k in range(8):
                TS(As[l].a[:, k, :], modT[l].a[:, 8 + k, 1:1 + NS], 1.0, lv.a[:, VNG + k:VNG + k + 1], ADD, MUL,
                   [modT[l].T, lv.T], [As[l].T])
            TS(gqp[l].a, modT[l].a[:, 16:24, 0], 0.25, None, MUL, None, [modT[l].T], [gqp[l].T])
            TS(gqs[l].a, modT[l].a[:, 16:24, 1:1 + NS], 0.25, None, MUL, None, [modT[l].T], [gqs[l].T])

        def proj(slot, c0, M, rhs, N, extra_R=()):
            z = zr.next()
            for k in range(8):
                MM(z.a[0:M, 0:N], slot.a[:, k, c0:c0 + M], rhs.a[:, k, 0:N], k == 0, k == 7,
                   [slot.T, rhs.T] + list(extra_R), [z.T])
            return z

        def rms_stats(src_fn, nk, N, scale, Rsrc, rstd=rstd):
            z = zr.next()
            for k0 in range(0, nk, 4):
                for k in range(k0, min(nk, k0 + 4)):
                    ACT(sqb.a[:, k % 4, 0:N], src_fn(k), AF.Square, Rsrc, [sqb.T])
                for k in range(k0, min(nk, k0 + 4)):
                    MM(z.a[:, 0:N], onesb.a, sqb.a[:, k % 4, 0:N], k == 0, k == nk - 1, [onesb.T, sqb.T], [z.T])
            ACT(rstd.a[:, 0:N], z.a[:, 0:N], AF.Ln, [z.T], [rstd.T], bias=EPS, scale=scale)
            ACT(rstd.a[:, 0:N], rstd.a[:, 0:N], AF.Exp, [rstd.T], [rstd.T], scale=-0.5)

        class Ctx:
            pass

        def block(l, cx):
            x, N, samp, tile_idx = cx.x, cx.N, cx.samp, cx.tile_idx
            hT, pbr, mg, gbuf, wk, xcd, xcb = cx.hT, cx.pbr, cx.mg, cx.gb, cx.wk, cx.xcd, cx.xcb
            lv = lvec[l]
            md = modT[l]
            last_tile = (not samp) and tile_idx == NT - 1
            PB = int(os.environ.get("KPB", "99")) if not samp else 99
            rms_stats(lambda k: x.a[:, k, 0:N], 8, N, 1.0 / D, [x.T])
            for k in range(8):
                t = wk.next()
                if not samp:
                    STT(t.a[:, 0:N], x.a[:, k, 0:N], Ap[l].a[:, k:k + 1], rstd.a[:, 0:N], MUL, MUL,
                        [x.T, Ap[l].T, rstd.T], [t.T])
                    ACT(hT.a[:, k, 0:N], t.a[:, 0:N], AF.Identity, [t.T, md.T], [hT.T], bias=md.a[:, k, 0:1])
                else:
                    TTo(t.a[:, 0:N], x.a[:, k, 0:N], rstd.a[:, 0:N], MUL, [x.T, rstd.T], [t.T])
                    TTo(t.a[:, 0:N], t.a[:, 0:N], As[l].a[:, k, :], MUL, [t.T, As[l].T], [t.T])
                    TTo(hT.a[:, k, 0:N], t.a[:, 0:N], md.a[:, k, 1:1 + NS], ADD, [t.T, md.T], [hT.T])

            if PB <= 1:
                return
            slot = yield
            z = proj(slot, 0, 16, hT, N)
            ACPY(zlr.a[0:16, 0:N], z.a[0:16, 0:N], [z.T], [zlr.T])
            if not samp:
                for c in range(4):
                    z = zr.next()
                    MM(z.a[:, :], zlr.a[0:17, c * 128:c * 128 + 128], gwa[l].a[0:17, :], True, True, [zlr.T, gwa[l].T], [z.T])
                    ACT(sp_tok.a[:, c, :], z.a[:, :], AF.Exp, [z.T], [sp_tok.T], scale=-1.0)
                    ACT(sp_tok.a[:, c, :], sp_tok.a[:, c, :], AF.Ln, [sp_tok.T], [sp_tok.T], bias=1.0)
                for c in range(4):
                    z = zr.next()
                    MM(z.a[:, :], triUpNeg, sp_tok.a[:, c, :], True, True, [consts.T, sp_tok.T], [z.T])
                    ACT(ekd.a[:, c, :], z.a[:, :], AF.Exp, [z.T], [ekd.T])
            else:
                z = zr.next()
                for h in range(4):
                    MM(z.a[:, h * NS:(h + 1) * NS], gwa[l].a[0:17, h * 128:h * 128 + 128], zlr.a[0:17, 0:N], True, True,
                       [zlr.T, gwa[l].T], [z.T])
                av = aTs.a.rearrange("p h s -> p (h s)")
                ACT(av, z.a[:, 0:4 * NS], AF.Exp, [z.T], [aTs.T], scale=-1.0)
                ACT(av, av, AF.Ln, [aTs.T], [aTs.T], bias=1.0)
                ACT(av, av, AF.Exp, [aTs.T], [aTs.T], scale=-1.0 / 16.0)

            if PB <= 2:
                return
            for h in range(4):
                slot = yield
                if not samp:
                    zb = zr.next()
                    for c in range(4):
                        MM(zb.a[:, c * 128:c * 128 + 128], sp_tok.a[:, c, h * 128:h * 128 + 128], triNeg, True, True,
                           [sp_tok.T, consts.T], [zb.T])
                    E2 = wk.next()
                    E1 = wk.next()
                    ACT(E1.a, zb.a, AF.Exp, [zb.T], [E1.T])
                    ACT(E2.a, zb.a, AF.Exp, [zb.T], [E2.T], scale=-1.0)
                    ACPY(eb[h].a, E1.a.rearrange("p (c n) -> p c n", c=4)[:, :, 127], [E1.T], [eb[h].T])
                    KA = int(os.environ.get("KA", "9"))
                    if KA <= 1:
                        continue
                    zq = proj(slot, 0, 128, hT, N)
                    STT(qe[h].a, zq.a, float(128 ** -0.5), E1.a, MUL, MUL, [zq.T, E1.T], [qe[h].T])
                    zk = proj(slot, 128, 128, hT, N)
                    TTo(ke[h].a, zk.a, E2.a, MUL, [zk.T, E2.T], [ke[h].T])
                    if KA <= 2:
                        continue
                    for c in range(4):
                        z = zr.next()
                        for k in range(8):
                            MM(z.a[:, 0:384], hT.a[:, k, c * 128:c * 128 + 128], slot.a[:, k, 128:512], k == 0, k == 7,
                               [hT.T, slot.T], [z.T])
                        if KA <= 3:
                            continue
                        ACPY(vt(h, c), z.a[:, 128:384], [z.T], [vtok[h].T])
                        if KA <= 4:
                            continue
                        TTo(kdtok[h].a[:, c, :], z.a[:, 0:128], ekd.a[:, c, h * 128:h * 128 + 128], MUL, [z.T, ekd.T, vtok[h].T], [kdtok[h].T])
                else:
                    zq = proj(slot, 0, 128, hT, N)
                    TS(qss.a[:, h, :], zq.a[:, 0:N], float(128 ** -0.5), None, MUL, None, [zq.T], [qss.T])
                    z = zr.next()
                    for k in range(8):
                        MM(z.a[0:NS, 0:384], hT.a[:, k, 0:N], slot.a[:, k, 128:512], k == 0, k == 7, [hT.T, slot.T], [z.T])
                    ACPY(kvs.a[:, h, :], z.a[0:NS, 0:384], [z.T], [kvs.T])

            if PB <= 3:
                return
            if not samp:
                for c in range(4):
                    cs_ = slice(c * 128, c * 128 + 128)
                    sa4 = zr.next()
                    for h in range(4):
                        MM(sa4.a[:, h * 128:h * 128 + 128], ke[h].a[:, cs_], qe[h].a[:, cs_], True, True, [qk.T], [sa4.T])
                    ams = []
                    for h in range(4):
                        am = attm.next()
                        TTo(am.a, sa4.a[:, h * 128:h * 128 + 128], maskT, MUL, [sa4.T, consts.T], [am.T])
                        ams.append(am)
                    sos = [zr.next(), zr.next()]
                    for h in range(4):
                        so = sos[h // 2]
                        for vc in range(2):
                            col = (h % 2) * 256 + vc * 128
                            MM(so.a[:, col:col + 128], vt(h, c)[:, vc * 128:vc * 128 + 128], ams[h].a, True, False,
                               [vtok[h].T, ams[h].T], [so.T])
                            MM(so.a[:, col:col + 128], Sb[l][h].a[:, vc * 128:vc * 128 + 128], qe[h].a[:, cs_], False, True,
                               [Sb[l][h].T, qk.T], [so.T])
                    sus = [zr.next(), zr.next()]
                    for h in range(4):
                        su = sus[h // 2]
                        MM(su.a[:, (h % 2) * 256:(h % 2) * 256 + 256], kdtok[h].a[:, c, :], vt(h, c), True, True,
                           [kdtok[h].T, vtok[h].T], [su.T])
                    for h in range(4):
                        so = sos[h // 2]
                        ACPY(oT[h].a[:, :, cs_], so.a[:, (h % 2) * 256:(h % 2) * 256 + 256].rearrange("p (a b) -> p a b", a=2),
                             [so.T], [oT[h].T])
                    for h in range(4):
                        su = sus[h // 2]
                        STT(S[l][h].a, S[l][h].a, eb[h].a[:, c:c + 1], su.a[:, (h % 2) * 256:(h % 2) * 256 + 256], MUL, ADD,
                            [S[l][h].T, eb[h].T, su.T], [S[l][h].T])
                        ACPY(Sb[l][h].a, S[l][h].a, [S[l][h].T], [Sb[l][h].T])
                if last_tile:
                    for h in range(4):
                        DMA("sp", f"S{l}_{h}", [(ogla_p[l, h], S[l][h].a)], [S[l][h].T], [])
            else:
                pass

            def s_A(s_):
                si = Sin[s_ % 2]
                DMA("sp", "ld_" + si.T.name + "s", [(si.a, sgla[l, s_].rearrange("h d v -> d h v"))], [], [si.T])
                km = km4.bufs[s_ % 2]
                TS(km.a, kvs.a[:, :, 0:128], ident[0:NS, s_:s_ + 1], None, MUL, None, [kvs.T, consts.T, sqb.T], [km.T])

            def s_B(s_):
                si, sn, km, sbf = Sin[s_ % 2], Snw[s_ % 2], km4.bufs[s_ % 2], snbq[s_ % 2]
                sus = [zr.next(), zr.next()]
                for h in range(4):
                    su = sus[h // 2]
                    MM(su.a[:, (h % 2) * 256:(h % 2) * 256 + 256], km.a[:, h, :], kvs.a[:, h, 128:384], True, True,
                       [km.T, kvs.T, sqb.T], [su.T])
                for h in range(4):
                    su = sus[h // 2]
                    STT(sn.a[:, h, :], si.a[:, h, :], aTs.a[:, h, s_:s_ + 1], su.a[:, (h % 2) * 256:(h % 2) * 256 + 256], MUL, ADD,
                        [si.T, aTs.T, su.T], [sn.T])
                ACPY(sbf.a, sn.a, [sn.T, qk.T], [sbf.T])
                DMA("sp", "st_" + sn.T.name + "s", [(ogla_s[l, s_].rearrange("h d v -> d h v"), sn.a)], [sn.T], [])

            def s_C(s_):
                sbf = snbq[s_ % 2]
                so = zr.next()
                for h in range(4):
                    for vc in range(2):
                        MM(so.a[:, 2 * h + vc:2 * h + vc + 1], sbf.a[:, h, vc * 128:vc * 128 + 128], qss.a[:, h, s_:s_ + 1], True, True,
                           [sbf.T, qss.T, qk.T], [so.T])
                ACPY(oTs.a[:, :, s_:s_ + 1], so.a[:, 0:8].rearrange("p (a b) -> p a b", b=1), [so.T], [oTs.T])

            def sstep(n):
                if n == 0:
                    ACPY(qk.a[0:1, 0, 0:1], qk.a[0:1, 0, 0:1], [], [qk.T])
                    ACPY(sqb.a[0:1, 3, 0:1], sqb.a[0:1, 3, 0:1], [], [sqb.T])
                    s_A(0)
                if n + 1 < NS:
                    s_A(n + 1)
                if 0 <= n - 1 < NS:
                    s_C(n - 1)
                if n < NS:
                    s_B(n)

            if PB <= 4:
                return
            rsb = cx.rsb
            for h in range(4):
                if not samp:
                    rms_stats(lambda vc: oT[h].a[:, vc, :], 2, N, 1.0 / 256, [oT[h].T], rsb[h])
            for h in range(4):
                slot = yield
                for vc in range(2):
                    zg = proj(slot, vc * 128, 128, hT, N)
                    if samp:
                        ACT(sgs.a[:, 2 * h + vc, :], zg.a[:, 0:N], AF.Tanh, [zg.T], [sgs.T], scale=0.5)
                        STT(sgs.a[:, 2 * h + vc, :], sgs.a[:, 2 * h + vc, :], 1.0, zg.a[:, 0:N], ADD, MUL, [sgs.T, zg.T], [sgs.T])
                        continue
                    sg = wk.next()
                    ACT(sg.a[:, 0:N], zg.a[:, 0:N], AF.Tanh, [zg.T], [sg.T], scale=0.5)
                    STT(sg.a[:, 0:N], sg.a[:, 0:N], 1.0, zg.a[:, 0:N], ADD, MUL, [sg.T, zg.T], [sg.T])
                    t1 = wk.next()
                    osrc = oT[h].a[:, vc, :] if not samp else oTs.a[:, 2 * h + vc, :]
                    oTt = oT[h].T if not samp else oTs.T
                    STT(t1.a[:, 0:N], osrc, lv.a[:, VGNG + vc:VGNG + vc + 1], rsb[h].a[:, 0:N], MUL, MUL,
                        [oTt, lv.T, rsb[h].T], [t1.T])
                    TTo(pbr[0].a[:, 2 * h + vc, 0:N], t1.a[:, 0:N], sg.a[:, 0:N], MUL, [t1.T, sg.T], [pbr[0].T])

            if PB <= 5:
                return
            hv = hvec[l]
            for j in range(8):
                slotA = yield
                if samp:
                    sstep(2 * j)
                cw = [lv.a[:, VRGW + 4 * j + r:VRGW + 4 * j + r + 1] for r in range(4)]
                cb = lv.a[:, VRGCB + j:VRGCB + j + 1]
                zx = proj(slotA, 0, 128, hT, N)
                xc = xcd.next()
                if not samp:
                    ur = ub.next()
                    ACPY(ur.a[:, 3:3 + N], zx.a[:, 0:N], [zx.T], [ur.T])
                    ACPY(ur.a[:, 0:3], crg[l].a[:, j, :], [crg[l].T], [ur.T])
                    ACPY(crg[l].a[:, j, :], ur.a[:, N:N + 3], [ur.T], [crg[l].T])
                    ACT(xc.a[:, 0:N], ur.a[:, 0:N], AF.Identity, [ur.T, lv.T], [xc.T], bias=cb, scale=cw[0])
                    for r in range(1, 4):
                        STT(xc.a[:, 0:N], ur.a[:, r:r + N], cw[r], xc.a[:, 0:N], MUL, ADD, [ur.T, lv.T, xc.T], [xc.T])
                else:
                    ACPY(nrgc.a[:, j, 2, :], zx.a[:, 0:N], [zx.T], [nrgc.T])
                    VCPY(nrgc.a[:, j, 0:2, :], srgc.a[:, j, 1:3, :], [srgc.T], [nrgc.T])
                    TS(xc.a[:, 0:N], srgc.a[:, j, 0, :], cw[0], cb, MUL, ADD, [srgc.T, lv.T], [xc.T])
                    for r in range(1, 3):
                        STT(xc.a[:, 0:N], srgc.a[:, j, r, :], cw[r], xc.a[:, 0:N], MUL, ADD, [srgc.T, lv.T, xc.T], [xc.T])
                    STT(xc.a[:, 0:N], nrgc.a[:, j, 2, :], cw[3], xc.a[:, 0:N], MUL, ADD, [nrgc.T, lv.T, xc.T], [xc.T])
                ACPY(xcb.a[:, 0:N], xc.a[:, 0:N], [xc.T], [xcb.T])

                slot = slotA
                w0 = lv.a[:, VSCW + 3 * j + 0:VSCW + 3 * j + 1]
                w1 = lv.a[:, VSCW + 3 * j + 1:VSCW + 3 * j + 2]
                w2 = lv.a[:, VSCW + 3 * j + 2:VSCW + 3 * j + 3]
                zc = proj(slot, 128, 128, hT, N)
                tcb = wk.next()
                ACPY(tcb.a[:, 0:N], zc.a[:, 0:N], [zc.T], [tcb.T])
                zh = proj(slot, 256, 128, hT, N)
                uc = wk.next()
                if not samp:
                    u = ub.next()
                    TTo(u.a[:, 2:2 + N], zh.a[:, 0:N], tcb.a[:, 0:N], MUL, [zh.T, tcb.T], [u.T])
                    ACPY(u.a[:, 0:2], csc[l].a[:, j, :], [csc[l].T], [u.T])
                    ACPY(csc[l].a[:, j, :], u.a[:, N:N + 2], [u.T], [csc[l].T])
                    ACT(uc.a[:, 0:N], u.a[:, 0:N], AF.Identity, [u.T, lv.T], [uc.T], scale=w0)
                    STT(uc.a[:, 0:N], u.a[:, 1:1 + N], w1, uc.a[:, 0:N], MUL, ADD, [u.T, lv.T, uc.T], [uc.T])
                    STT(uc.a[:, 0:N], u.a[:, 2:2 + N], w2, uc.a[:, 0:N], MUL, ADD, [u.T, lv.T, uc.T], [uc.T])
                else:
                    TTo(nsc.a[:, j, 1, :], zh.a[:, 0:N], tcb.a[:, 0:N], MUL, [zh.T, tcb.T], [nsc.T])
                    VCPY(nsc.a[:, j, 0, :], ssc.a[:, j, 1, :], [ssc.T], [nsc.T])
                    TS(uc.a[:, 0:N], ssc.a[:, j, 0, :], w0, None, MUL, None, [ssc.T, lv.T], [uc.T])
                    STT(uc.a[:, 0:N], ssc.a[:, j, 1, :], w1, uc.a[:, 0:N], MUL, ADD, [ssc.T, lv.T, uc.T], [uc.T])
                    STT(uc.a[:, 0:N], nsc.a[:, j, 1, :], w2, uc.a[:, 0:N], MUL, ADD, [nsc.T, lv.T, uc.T], [uc.T])
                zb_ = proj(slot, 384, 128, hT, N)
                TTo(uc.a[:, 0:N], zb_.a[:, 0:N], uc.a[:, 0:N], MUL, [zb_.T, uc.T], [uc.T])
                slotB = yield
                if samp:
                    sstep(2 * j + 1)
                zg = proj(slotB, 0, 128, hT, N)
                sg = wk.next()
                ACT(sg.a[:, 0:N], zg.a[:, 0:N], AF.Tanh, [zg.T], [sg.T], scale=0.5)
                STT(sg.a[:, 0:N], sg.a[:, 0:N], 1.0, zg.a[:, 0:N], ADD, MUL, [sg.T, zg.T], [sg.T])
                TTo(pbr[1].a[:, j, 0:N], uc.a[:, 0:N], sg.a[:, 0:N], MUL, [uc.T, sg.T], [pbr[1].T])

                slot = slotB
                zrr = zr.next()
                MM(zrr.a[:, 0:N], rgwa[l].a[:, j, :], xcb.a[:, 0:N], True, True, [rgwa[l].T, xcb.T], [zrr.T])
                zi = zr.next()
                MM(zi.a[:, 0:N], rgwx[l].a[:, j, :], xcb.a[:, 0:N], True, True, [rgwx[l].T, xcb.T], [zi.T])
                zg = proj(slot, 128, 128, hT, N)
                a_ = wk.next()
                ACT(a_.a[:, 0:N], zrr.a[:, 0:N], AF.Tanh, [zrr.T, hv.T], [a_.T], bias=hv.a[:, VRGBA + j:VRGBA + j + 1], scale=0.5)
                ig = wk.next()
                ACT(ig.a[:, 0:N], zi.a[:, 0:N], AF.Tanh, [zi.T, hv.T], [ig.T], bias=hv.a[:, VRGBX + j:VRGBX + j + 1], scale=0.5)
                sg = wk.next()
                ACT(sg.a[:, 0:N], zg.a[:, 0:N], AF.Tanh, [zg.T], [sg.T], scale=0.5)
                ACT(a_.a[:, 0:N], a_.a[:, 0:N], AF.Exp, [a_.T, hc[l].T], [a_.T], bias=hc[l].a[:, j:j + 1], scale=hc[l].a[:, j:j + 1])
                sq_ = wk.next()
                STT(sq_.a[:, 0:N], a_.a[:, 0:N], 0.9999999, a_.a[:, 0:N], MIN, MUL, [a_.T], [sq_.T])
                ACT(sq_.a[:, 0:N], sq_.a[:, 0:N], AF.Sqrt, [sq_.T], [sq_.T], bias=0.25, scale=-0.25)
                STT(ig.a[:, 0:N], ig.a[:, 0:N], 1.0, xc.a[:, 0:N], ADD, MUL, [ig.T, xc.T], [ig.T])
                STT(sg.a[:, 0:N], sg.a[:, 0:N], 1.0, zg.a[:, 0:N], ADD, MUL, [sg.T, zg.T], [sg.T])
                TTo(ig.a[:, 0:N], ig.a[:, 0:N], sq_.a[:, 0:N], MUL, [ig.T, sq_.T], [ig.T])
                if not samp:
                    P.op("dve", (lambda o, d0, d1, ini: (lambda e: e.tensor_tensor_scan(out=o, data0=d0, data1=d1, initial=ini,
                                                                                        op0=MUL, op1=ADD)))(
                        xc.a[:, 0:N], a_.a[:, 0:N], ig.a[:, 0:N], chh[l].a[:, j:j + 1]),
                        [a_.T, ig.T, chh[l].T], [xc.T])
                    VCPY(chh[l].a[:, j:j + 1], xc.a[:, N - 1:N], [xc.T], [chh[l].T])
                    hsrc = xc.a[:, 0:N]
                    hT_ = xc.T
                else:
                    TTo(a_.a[:, 0:N], a_.a[:, 0:N], srgh.a[:, j, :], MUL, [a_.T, srgh.T], [a_.T])
                    TTo(nrgh.a[:, j, :], a_.a[:, 0:N], ig.a[:, 0:N], ADD, [a_.T, ig.T], [nrgh.T])
                    hsrc = nrgh.a[:, j, :]
                    hT_ = nrgh.T
                TTo(pbr[2].a[:, j, 0:N], hsrc, sg.a[:, 0:N], MUL, [hT_, sg.T], [pbr[2].T])
            if samp:
                sstep(NS)
                for h in range(4):
                    rms_stats(lambda vc: oTs.a[:, 2 * h + vc, :], 2, N, 1.0 / 256, [oTs.T], rsb[h])
                for h in range(4):
                    for vc in range(2):
                        t1 = wk.next()
                        STT(t1.a[:, 0:N], oTs.a[:, 2 * h + vc, :], lv.a[:, VGNG + vc:VGNG + vc + 1], rsb[h].a[:, 0:N], MUL, MUL,
                            [oTs.T, lv.T, rsb[h].T], [t1.T])
                        TTo(pbr[0].a[:, 2 * h + vc, 0:N], t1.a[:, 0:N], sgs.a[:, 2 * h + vc, :], MUL, [t1.T, sgs.T], [pbr[0].T])
            if last_tile:
                DMA("sp", f"csc{l}", [(osc_p[l], csc[l].a)], [csc[l].T], [])
                DMA("sp", f"crg{l}", [(orgc_p[l], crg[l].a)], [crg[l].T], [])
                DMA("sp", f"chh{l}", [(orgh_p[l], chh[l].a)], [chh[l].T], [])
            if samp:
                DMA("sp", "nsc", [(osc_s[l], nsc.a)], [nsc.T], [])
                DMA("sp", "nrgc", [(orgc_s[l], nrgc.a)], [nrgc.T], [])
                DMA("sp", "nrgh", [(orgh_s[l], nrgh.a)], [nrgh.T], [])

            if PB <= 7:
                return
            for i in range(8):
                s1 = yield
                for br in range(3):
                    zm = proj(s1, 128 * br, 128, hT, N)
                    ACT(gbuf[br].a[:, 0:N], zm.a[:, 0:N], AF.Tanh, [zm.T, hv.T], [gbuf[br].T],
                        bias=hv.a[:, VBM + 8 * br + i:VBM + 8 * br + i + 1], scale=0.5)
                acc = wk.next()
                s2 = yield
                for br in range(3):
                    zy = proj(s2, 128 * br, 128, pbr[br], N)
                    if br == 0:
                        STT(acc.a[:, 0:N], gbuf[0].a[:, 0:N], 1.0, zy.a[:, 0:N], ADD, MUL, [zy.T, gbuf[0].T], [acc.T])
                    else:
                        STT(gbuf[br].a[:, 0:N], gbuf[br].a[:, 0:N], 1.0, zy.a[:, 0:N], ADD, MUL, [zy.T, gbuf[br].T], [gbuf[br].T])
                        if br == 1:
                            TTo(acc.a[:, 0:N], acc.a[:, 0:N], gbuf[1].a[:, 0:N], ADD, [acc.T, gbuf[1].T], [acc.T])
                        else:
                            TTo(mg.a[:, i, 0:N], acc.a[:, 0:N], gbuf[2].a[:, 0:N], ADD, [acc.T, gbuf[2].T], [mg.T])

            for i in range(8):
                if i % 4 == 0:
                    so_ = yield
                zo = proj(so_, (i % 4) * 128, 128, mg, N)
                if not samp:
                    STT(x.a[:, i, 0:N], zo.a[:, 0:N], gqp[l].a[:, i:i + 1], x.a[:, i, 0:N], MUL, ADD, [zo.T, gqp[l].T, x.T], [x.T])
                else:
                    t = wk.next()
                    TTo(t.a[:, 0:N], zo.a[:, 0:N], gqs[l].a[:, i, :], MUL, [zo.T, gqs[l].T], [t.T])
                    TTo(x.a[:, i, 0:N], x.a[:, i, 0:N], t.a[:, 0:N], ADD, [x.T, t.T], [x.T])

        def final_norm(x, N, dst, dst_dram, key):
            rms_stats(lambda k: x.a[:, k, 0:N], 8, N, 1.0 / D, [x.T])
            for k in range(8):
                STT(dst.a[:, k, 0:N], x.a[:, k, 0:N], lvec[0].a[:, VFG + k:VFG + k + 1], rstd.a[:, 0:N], MUL, MUL,
                    [x.T, lvec[0].T, rstd.T], [dst.T])
            DMA("sp", key, [(dst_dram, dst.a[:, :, 0:N])], [dst.T], [])

        def run_blocks(l, ctxs):
            gens = [block(l, c) for c in ctxs]
            active = []
            for g in gens:
                try:
                    next(g)
                    active.append(g)
                except StopIteration:
                    pass
            while active:
                slot = wq.get()
                nxt = []
                for g in active:
                    try:
                        g.send(slot)
                        nxt.append(g)
                    except StopIteration:
                        pass
                active = nxt

        cs_ctx = Ctx()
        cs_ctx.x, cs_ctx.N, cs_ctx.samp, cs_ctx.tile_idx = xs, NS, True, 0
        cs_ctx.hT, cs_ctx.pbr, cs_ctx.mg, cs_ctx.gb, cs_ctx.wk, cs_ctx.xcd, cs_ctx.xcb, cs_ctx.rsb = (
            hT_s, pbr_s, mg_s, gb_s, wk_s, xcd_s, xcb_s, rsb_s)

        def prompt_ctx(xbuf, tt):
            c = Ctx()
            c.x, c.N, c.samp, c.tile_idx = xbuf, TT, False, tt
            c.hT, c.pbr, c.mg, c.gb, c.wk, c.xcd, c.xcb, c.rsb = hT, pbr, mg, gbuf, wk, xcd, xcb, [rstd, gbuf[0], gbuf[1], gbuf[2]]
            return c

        DMA("sp", "xs", [(xs.a, xsT[:, :, :])], [], [xs.T])
        xcur = xt.next()
        DMA("sp", xcur.T.name, [(xcur.a, xT[:, :, 0:TT])], [], [xcur.T])
        for tt in range(NT):
            if tt + 1 < NT:
                xnext = xt.next()
                DMA("sp", xnext.T.name, [(xnext.a, xT[:, :, (tt + 1) * TT:(tt + 2) * TT])], [], [xnext.T])
            for l in range(L):
                ctxs = [prompt_ctx(xcur, tt)]
                if tt == 0:
                    DMA("sp", "ssc", [(ssc.a, sscf[l])], [], [ssc.T])
                    DMA("sp", "srgc", [(srgc.a, srgcf[l])], [], [srgc.T])
                    DMA("sp", "srgh", [(srgh.a, srghf[l])], [], [srgh.T])
                    ctxs.append(cs_ctx)
                run_blocks(l, ctxs)
            final_norm(xcur, TT, xcur, yT[:, :, tt * TT:(tt + 1) * TT], xcur.T.name)
            if tt == 0:
                final_norm(xs, NS, xs, ysT[:, :, :], "xs")
            if tt + 1 < NT:
                xcur = xnext
        assert STAGE < 9 or wq.pos == len(wq.plan), (wq.pos, len(wq.plan))
        outs = [b.T for b in xt.bufs] + [xs.T, nsc.T, nrgc.T, nrgh.T] + [b.T for b in Snw]
        for l in range(L):
            outs += [csc[l].T, crg[l].T, chh[l].T] + [S[l][h].T for h in range(4)]
        P.wait_all("sp", outs)
        P.emit(st)
    return nc


_CACHE = {}


def _fm(v):
    return np.ascontiguousarray(np.swapaxes(v.reshape(v.shape[:-1] + (8, 128)), -1, -2))


def kernel(x_prompt, x_sample, c_prompt, c_sample, state_gla, state_sc_conv, state_rg_conv, state_rg_h,
           w_ada, b_ada, norm_gain, w_in, gla_w_alpha, gla_b_alpha, gla_norm_gain, gla_w_branch, sc_conv_w,
           sc_w_branch, rg_conv_w, rg_conv_b, rg_w_a, rg_b_a, rg_w_x, rg_b_x, rg_lambda, rg_w_branch,
           b_merge, w_out, final_gain):
    f = lambda a: np.ascontiguousarray(np.asarray(a, dtype=np.float32))
    x_prompt, x_sample, c_prompt, c_sample = f(x_prompt), f(x_sample), f(c_prompt), f(c_sample)
    state_gla, state_sc_conv, state_rg_conv, state_rg_h = f(state_gla), f(state_sc_conv), f(state_rg_conv), f(state_rg_h)
    n = 8
    if "nc" not in _CACHE:
        _CACHE["nc"] = build_program()
    nc = _CACHE["nc"]
    lvec = np.zeros((L, 128, NV), np.float32)
    for l in range(L):
        lvec[l, :, VNG:VNG + 8] = _fm(f(norm_gain)[l])
        lvec[l, :, VBADA:VBADA + 24] = f(b_ada)[l].reshape(24, 128).T
        lvec[l, :, VGNG:VGNG + 2] = f(gla_norm_gain)[l].reshape(2, 128).T
        lvec[l, :, VSCW:VSCW + 24] = f(sc_conv_w)[l].reshape(3, 8, 128).transpose(2, 1, 0).reshape(128, 24)
        lvec[l, :, VRGW:VRGW + 32] = f(rg_conv_w)[l].reshape(4, 8, 128).transpose(2, 1, 0).reshape(128, 32)
        lvec[l, :, VRGCB:VRGCB + 8] = _fm(f(rg_conv_b)[l])
        lvec[l, :, VRGBA:VRGBA + 8] = _fm(f(rg_b_a)[l])
        lvec[l, :, VRGBX:VRGBX + 8] = _fm(f(rg_b_x)[l])
        lvec[l, :, VLAM:VLAM + 8] = _fm(f(rg_lambda)[l])
        lvec[l, :, VBM:VBM + 24] = f(b_merge)[l].reshape(24, 128).T
        lvec[l, :, VFG:VFG + 8] = _fm(f(final_gain))
    gwa = np.concatenate([f(gla_w_alpha), f(gla_b_alpha)[:, None, :]], axis=1)
    s_idx = np.arange(128)[:, None]
    t_idx = np.arange(128)[None, :]
    consts = np.concatenate([
        np.eye(128, dtype=np.float32),
        np.ones((128, 128), np.float32),
        np.where(s_idx <= t_idx, -1.0 / 16.0, 0.0).astype(np.float32),
        np.where(s_idx > t_idx, -1.0 / 16.0, 0.0).astype(np.float32),
        np.where(s_idx <= t_idx, 1.0, 0.0).astype(np.float32),
    ], axis=1)
    shared = {
        "w_ada": f(w_ada), "w_in": f(w_in), "wgla": f(gla_w_branch), "wsc": f(sc_w_branch), "wrg": f(rg_w_branch),
        "w_out": f(w_out), "rgwa": f(rg_w_a), "rgwx": f(rg_w_x), "gwa": np.ascontiguousarray(gwa),
        "lvec": lvec, "consts": np.ascontiguousarray(consts),
    }
    in_maps = []
    for c in range(n):
        sl = slice(c * NS, (c + 1) * NS)
        m = dict(shared)
        m["xT"] = np.ascontiguousarray(x_prompt[c].reshape(2048, 8, 128).transpose(2, 1, 0))
        m["xsT"] = np.ascontiguousarray(x_sample[sl, 0, :].reshape(NS, 8, 128).transpose(2, 1, 0))
        cc = np.concatenate([c_prompt[c:c + 1], c_sample[sl]], axis=0)
        m["cT"] = np.ascontiguousarray(cc.reshape(1 + NS, 8, 128).transpose(2, 1, 0))
        m["sgla"] = np.ascontiguousarray(state_gla[:, sl])
        m["sscf"] = np.ascontiguousarray(state_sc_conv[:, sl].reshape(L, NS, 2, 8, 128).transpose(0, 4, 3, 2, 1))
        m["srgcf"] = np.ascontiguousarray(state_rg_conv[:, sl].reshape(L, NS, 3, 8, 128).transpose(0, 4, 3, 2, 1))
        m["srghf"] = np.ascontiguousarray(state_rg_h[:, sl].reshape(L, NS, 8, 128).transpose(0, 3, 2, 1))
        in_maps.append(m)
    if os.environ.get("KCORES"):
        n1 = int(os.environ["KCORES"])
        res = run_bass_kernel_spmd(nc, in_maps[:n1], core_ids=list(range(n1)), trace=bool(os.environ.get("KTRACE")))
        return res
    res = run_bass_kernel_spmd(nc, in_maps, core_ids=list(range(n)))
    R = res.results
    B = 8
    y_prompt = np.stack([R[c]["yT"].transpose(2, 1, 0).reshape(2048, D) for c in range(n)], axis=0)
    y_sample = np.concatenate([R[c]["ysT"].transpose(2, 1, 0).reshape(NS, 1, D) for c in range(n)], axis=0)
    gla_p = np.stack([R[c]["ogla_p"] for c in range(n)], axis=1)
    sc_p = np.stack([R[c]["osc_p"].transpose(0, 3, 2, 1).reshape(L, 2, D) for c in range(n)], axis=1)
    rgc_p = np.stack([R[c]["orgc_p"].transpose(0, 3, 2, 1).reshape(L, 3, D) for c in range(n)], axis=1)
    rgh_p = np.stack([R[c]["orgh_p"].transpose(0, 2, 1).reshape(L, D) for c in range(n)], axis=1)
    gla_s = np.concatenate([R[c]["ogla_s"] for c in range(n)], axis=1)
    sc_s = np.concatenate([R[c]["osc_s"].transpose(0, 4, 3, 2, 1).reshape(L, NS, 2, D) for c in range(n)], axis=1)
    rgc_s = np.concatenate([R[c]["orgc_s"].transpose(0, 4, 3, 2, 1).reshape(L, NS, 3, D) for c in range(n)], axis=1)
    rgh_s = np.concatenate([R[c]["orgh_s"].transpose(0, 3, 2, 1).reshape(L, NS, D) for c in range(n)], axis=1)
    outs = (y_prompt, y_sample, gla_p, sc_p, rgc_p, rgh_p, gla_s, sc_s, rgc_s, rgh_s)
    return tuple(np.ascontiguousarray(o.astype(np.float32)) for o in outs)
```

```python
from contextlib import ExitStack
import os
import numpy as np
import concourse.bass as bass
import concourse.mybir as mybir
from concourse.bass_utils import run_bass_kernel_spmd

F32 = mybir.dt.float32
BF16 = mybir.dt.bfloat16
AF = mybir.ActivationFunctionType
ALU = mybir.AluOpType

D = 1024
L = 2
TT = 512
NT = 4
NS = 16
NV = 160
N_IN = 12304
OQ, OK_, OV, OGG, OLR = 0, 512, 1024, 2048, 3072
OB, OC, OH, OGS, OX, OGR, OM = 3088, 4112, 5136, 6160, 7184, 8208, 9232
VNG, VBADA, VGNG, VSCW, VRGW, VRGCB, VRGBA, VRGBX, VLAM, VBM, VFG = 0, 8, 32, 34, 58, 90, 98, 106, 114, 122, 146
EPS = 1e-6


class Tile:
    __slots__ = ("name", "w", "r")

    def __init__(self, name):
        self.name = name
        self.w = None
        self.r = []


class Buf:
    __slots__ = ("a", "T")

    def __init__(self, a, T):
        self.a = a
        self.T = T


class Prog:
    ENGS = ("pe", "act", "dve", "pool", "sp")

    def __init__(self, nc):
        self.nc = nc
        self.streams = {e: [] for e in self.ENGS}
        self.dma_cnt = {}
        self.sig = {e: set() for e in self.ENGS}
        self.window = {"act": 1 << 30, "dve": 1 << 30, "pool": 1 << 30}

    def _collect(self, eng, idx, reads, writes):
        deps = set()
        for t in reads:
            if t.w is not None:
                deps.add(t.w)
        for t in writes:
            if t.w is not None:
                deps.add(t.w)
            deps.update(t.r)
        out = []
        for d in deps:
            if d[0] == "e" and d[1] == eng:
                if eng in ("pe", "sp"):
                    continue
                if idx - d[2] > self.window[eng]:
                    continue
            out.append(d)
        for d in out:
            if d[0] == "e":
                self.sig[d[1]].add(d[2])
        return out

    def _mark(self, me, reads, writes):
        for t in reads:
            t.r.append(me)
            if len(t.r) > 64:
                t.r = t.r[-48:]
        for t in writes:
            t.w = me
            t.r = []

    def op(self, eng, fn, reads=(), writes=()):
        idx = len(self.streams[eng])
        deps = self._collect(eng, idx, reads, writes)
        self.streams[eng].append((fn, deps, None))
        self._mark(("e", eng, idx), reads, writes)

    def dma(self, eng, fns, key, reads=(), writes=()):
        idx = len(self.streams[eng])
        deps = self._collect(eng, idx, reads, writes)
        cnt = self.dma_cnt.get(key, 0)
        for i, fn in enumerate(fns):
            cnt += 16
            self.streams[eng].append((fn, deps if i == 0 else [], key))
        self.dma_cnt[key] = cnt
        self._mark(("s", key, cnt), reads, writes)

    def wait_all(self, eng, tiles):
        idx = len(self.streams[eng])
        deps = self._collect(eng, idx, (), tiles)
        self.streams[eng].append((None, deps, None))

    def emit(self, stack):
        nc = self.nc
        esem = {e: stack.enter_context(nc.semaphore("S_" + e)) for e in self.ENGS if e != "sp"}
        dsem = {k: stack.enter_context(nc.semaphore("D_" + str(k))) for k in self.dma_cnt}
        cnt_at = {}
        for e in self.ENGS:
            c = 0
            m = {}
            for i in range(len(self.streams[e])):
                if i in self.sig[e]:
                    c += 1
                    m[i] = c
            cnt_at[e] = m
        block = stack.enter_context(nc.Block())

        def run(eng_name):
            def body(eng):
                waited = {}
                for i, (fn, deps, key) in enumerate(self.streams[eng_name]):
                    for d in deps:
                        if d[0] == "e":
                            sem, val, k = esem[d[1]], cnt_at[d[1]][d[2]], ("e", d[1])
                        else:
                            sem, val, k = dsem[d[1]], d[2], ("s", d[1])
                        if waited.get(k, 0) >= val:
                            continue
                        waited[k] = val
                        eng.wait_ge(sem, val)
                    if fn is None:
                        continue
                    ins = fn(eng)
                    if key is not None:
                        ins.then_inc(dsem[key], 16)
                    elif i in self.sig[eng_name]:
                        ins.then_inc(esem[eng_name], 1)
            return body

        block.tensor(run("pe"))
        block.scalar(run("act"))
        block.vector(run("dve"))
        block.gpsimd(run("pool"))
        block.sync(run("sp"))


class Ring:
    def __init__(self, bufs):
        self.bufs = bufs
        self.i = 0

    def next(self):
        b = self.bufs[self.i % len(self.bufs)]
        self.i += 1
        return b


def build_program():
    nc = bass.Bass("TRN2", target_bir_lowering=False)

    def din(name, shape):
        return nc.dram_tensor(name, list(shape), F32, kind="ExternalInput").ap()

    def dout(name, shape):
        return nc.dram_tensor(name, list(shape), F32, kind="ExternalOutput").ap()

    xT = din("xT", [128, 8, NT * TT])
    xsT = din("xsT", [128, 8, NS])
    cT = din("cT", [128, 8, 1 + NS])
    sgla = din("sgla", [L, NS, 4, 128, 256])
    sscf = din("sscf", [L, 128, 8, 2, NS])
    srgcf = din("srgcf", [L, 128, 8, 3, NS])
    srghf = din("srghf", [L, 128, 8, NS])
    w_ada = din("w_ada", [L, D, 3 * D])
    w_in = din("w_in", [L, D, N_IN])
    wbr = [din("wgla", [L, D, D]), din("wsc", [L, D, D]), din("wrg", [L, D, D])]
    w_out = din("w_out", [L, D, D])
    rgwa_d = din("rgwa", [L, 8, 128, 128])
    rgwx_d = din("rgwx", [L, 8, 128, 128])
    gwa_d = din("gwa", [L, 17, 512])
    lvec_d = din("lvec", [L, 128, NV])
    consts_d = din("consts", [128, 5 * 128])

    yT = dout("yT", [128, 8, NT * TT])
    ysT = dout("ysT", [128, 8, NS])
    ogla_p = dout("ogla_p", [L, 4, 128, 256])
    osc_p = dout("osc_p", [L, 128, 8, 2])
    orgc_p = dout("orgc_p", [L, 128, 8, 3])
    orgh_p = dout("orgh_p", [L, 128, 8])
    ogla_s = dout("ogla_s", [L, NS, 4, 128, 256])
    osc_s = dout("osc_s", [L, 128, 8, 2, NS])
    orgc_s = dout("orgc_s", [L, 128, 8, 3, NS])
    orgh_s = dout("orgh_s", [L, 128, 8, NS])

    with ExitStack() as st:
        P = Prog(nc)
        STAGE = int(os.environ.get("KSTAGE", "9"))

        def sb(name, shape, dt=F32):
            t = st.enter_context(nc.sbuf_tensor("s_" + name, list(shape), dt))
            return Buf(t[:], Tile(name))

        def ps(name, shape):
            t = st.enter_context(nc.psum_tensor("p_" + name, list(shape), F32))
            return Buf(t[:], Tile(name))

        def MM(out, lhsT, rhs, start, stop, R, W):
            P.op("pe", lambda e: e.matmul(out, lhsT=lhsT, rhs=rhs, start=start, stop=stop), R, W)

        def ACT(out, in_, func, R, W, bias=None, scale=None):
            kw = {}
            if bias is not None:
                kw["bias"] = bias
            if scale is not None:
                kw["scale"] = scale
            P.op("act", lambda e: e.activation(out=out, in_=in_, func=func, **kw), R, W)

        def ACPY(out, in_, R, W):
            P.op("act", lambda e: e.copy(out=out, in_=in_), R, W)

        def VCPY(out, in_, R, W):
            P.op("dve", lambda e: e.tensor_copy(out=out, in_=in_), R, W)

        def TTo(out, a, b, op, R, W):
            P.op("dve", lambda e: e.tensor_tensor(out=out, in0=a, in1=b, op=op), R, W)

        def TS(out, a, s1, s2, op0, op1, R, W):
            if op1 is None:
                P.op("dve", lambda e: e.tensor_scalar(out=out, in0=a, scalar1=s1, scalar2=None, op0=op0), R, W)
            else:
                P.op("dve", lambda e: e.tensor_scalar(out=out, in0=a, scalar1=s1, scalar2=s2, op0=op0, op1=op1), R, W)

        def STT(out, a, s, b, op0, op1, R, W):
            P.op("dve", lambda e: e.scalar_tensor_tensor(out=out, in0=a, scalar=s, in1=b, op0=op0, op1=op1), R, W)

        def PTS(out, a, s1, s2, op0, op1, R, W):
            P.op("pool", lambda e: e.tensor_scalar(out=out, in0=a, scalar1=s1, scalar2=s2, op0=op0, op1=op1), R, W)

        def PSTT(out, a, s, b, op0, op1, R, W):
            P.op("pool", lambda e: e.scalar_tensor_tensor(out=out, in0=a, scalar=s, in1=b, op0=op0, op1=op1), R, W)

        def PCPY(out, in_, R, W):
            P.op("pool", lambda e: e.tensor_copy(out=out, in_=in_), R, W)

        def DMA(eng, key, pairs, R, W):
            fns = []
            for (o, i) in pairs:
                fns.append((lambda o, i: (lambda e: e.dma_start(out=o, in_=i)))(o, i))
            P.dma(eng, fns, key, R, W)

        MUL, ADD, MIN = ALU.mult, ALU.add, ALU.min

        consts = sb("consts", [128, 5 * 128])
        ident = consts.a[:, 0:128]
        onesf = consts.a[:, 128:256]
        triNeg = consts.a[:, 256:384]
        triUpNeg = consts.a[:, 384:512]
        maskT = consts.a[:, 512:640]
        onesb = sb("onesb", [128, 128], BF16)
        lvec = [sb(f"lvec{l}", [128, NV]) for l in range(L)]
        cneg = [sb(f"cneg{l}", [128, 8]) for l in range(L)]
        gwa = [sb(f"gwa{l}", [17, 512]) for l in range(L)]
        rgwa = [sb(f"rgwa{l}", [128, 8, 128], BF16) for l in range(L)]
        rgwx = [sb(f"rgwx{l}", [128, 8, 128], BF16) for l in range(L)]
        modT = [sb(f"modT{l}", [128, 24, 1 + NS]) for l in range(L)]
        Ap = [sb(f"Ap{l}", [128, 8]) for l in range(L)]
        As = [sb(f"As{l}", [128, 8, NS]) for l in range(L)]
        cs = sb("cs", [128, 8, 1 + NS])
        scb = sb("scb", [128, 8, 1 + NS], BF16)

        wslots = Ring([sb(f"ws{i}", [128, 8, 512], BF16) for i in range(3)])
        xt = Ring([sb(f"xt{i}", [128, 8, TT]) for i in range(2)])
        hT = sb("hT", [128, 8, TT], BF16)
        pbr0 = sb("pbr0", [128, 8, TT], BF16)
        raw1 = sb("raw1", [128, 4 * 512])
        raw2 = sb("raw2", [128, 4 * 512])
        pbr = [pbr0,
               Buf(raw1.a.bitcast(BF16).rearrange("p (k n) -> p k n", k=8), raw1.T),
               Buf(raw2.a.bitcast(BF16).rearrange("p (k n) -> p k n", k=8), raw2.T)]
        mg = sb("mg", [128, 8, TT], BF16)
        sqb = sb("sqb", [128, 4, TT], BF16)
        rstd = sb("rstd", [128, TT])
        wk = Ring([sb(f"wk{i}", [128, TT]) for i in range(4)])
        xcd = Ring([sb(f"xcd{i}", [128, TT]) for i in range(2)])
        gbuf = [sb(f"gb{i}", [128, TT]) for i in range(3)]
        zlr = sb("zlr", [17, TT])
        sp_tok = Buf(raw1.a.rearrange("p (c n) -> p c n", c=4), raw1.T)
        ekd = Buf(raw2.a.rearrange("p (c n) -> p c n", c=4), raw2.T)
        eb = [sb(f"eb{h}", [128, 4]) for h in range(4)]
        qk = sb("qk", [128, 8, TT], BF16)
        qe = [Buf(qk.a[:, h, :], qk.T) for h in range(4)]
        ke = [Buf(qk.a[:, 4 + h, :], qk.T) for h in range(4)]
        hvec = [sb(f"hvec{l}", [128, NV]) for l in range(L)]
        hc = [sb(f"hc{l}", [128, 8]) for l in range(L)]
        gqp = [sb(f"gqp{l}", [128, 8]) for l in range(L)]
        gqs = [sb(f"gqs{l}", [128, 8, NS]) for l in range(L)]
        vtok = [Buf(None, mg.T) for h in range(4)]

        def vt(h, c):
            return mg.a[:, 2 * h + c // 2, (c % 2) * 256:(c % 2) * 256 + 256]
        kdtok = [sb(f"kdtok{h}", [128, 4, 128], BF16) for h in range(4)]
        attm = Ring([sb(f"attm{i}", [128, 128], BF16) for i in range(4)])
        oT = [sb(f"oT{h}", [128, 2, TT]) for h in range(4)]
        S = [[sb(f"S{l}_{h}", [128, 256]) for h in range(4)] for l in range(L)]
        Sb = [[sb(f"Sb{l}_{h}", [128, 256], BF16) for h in range(4)] for l in range(L)]
        csc = [sb(f"csc{l}", [128, 8, 2]) for l in range(L)]
        crg = [sb(f"crg{l}", [128, 8, 3]) for l in range(L)]
        chh = [sb(f"chh{l}", [128, 8]) for l in range(L)]
        ub = Ring([sb(f"ub{i}", [128, TT + 3]) for i in range(2)])
        xcb = sb("xcb", [128, TT], BF16)
        xs = sb("xs", [128, 8, NS])
        ssc = sb("ssc", [128, 8, 2, NS])
        srgc = sb("srgc", [128, 8, 3, NS])
        srgh = sb("srgh", [128, 8, NS])
        nsc = sb("nsc", [128, 8, 2, NS])
        nrgc = sb("nrgc", [128, 8, 3, NS])
        nrgh = sb("nrgh", [128, 8, NS])
        aTs = sb("aTs", [128, 4, NS])
        qss = sb("qss", [128, 4, NS], BF16)
        kvs = sb("kvs", [NS, 4, 384], BF16)
        km4 = Ring([Buf(sqb.a[0:NS, i, :].rearrange("p (h d) -> p h d", h=4), Tile(f"km4_{i}")) for i in range(2)])
        Sin = [Buf(oT[i].a.rearrange("p a (b c) -> p (a b) c", b=2), oT[i].T) for i in range(2)]
        Snw = [Buf(oT[i].a.rearrange("p a (b c) -> p (a b) c", b=2), oT[i].T) for i in range(2, 4)]
        snbq = [Buf(qk.a[:, 2 * i:2 * i + 2, :].rearrange("p k (a v) -> p (k a) v", a=2), Tile(f"snbq{i}")) for i in range(2)]
        sgs = sb("sgs", [128, 8, NS])
        snb4 = Ring([Buf(ub.bufs[i].a[:, 0:512].bitcast(BF16).rearrange("p (h v) -> p h v", h=4), ub.bufs[i].T) for i in range(2)])
        oTs = sb("oTs", [128, 8, NS])
        hT_s = sb("hT_s", [128, 8, NS], BF16)
        pbr_s = [sb(f"pbr_s{i}", [128, 8, NS], BF16) for i in range(3)]
        mg_s = sb("mg_s", [128, 8, NS], BF16)
        rsb_s = [sb(f"rsb_s{i}", [128, NS]) for i in range(4)]
        gb_s = [sb(f"gb_s{i}", [128, NS]) for i in range(3)]
        wk_s = Ring([sb(f"wk_s{i}", [128, NS]) for i in range(4)])
        xcd_s = Ring([sb(f"xcd_s{i}", [128, NS]) for i in range(2)])
        xcb_s = sb("xcb_s", [128, NS], BF16)

        zr = Ring([ps(f"z{i}", [128, 512]) for i in range(8)])

        class _SmView:
            def next(self):
                b = zr.next()
                return Buf(b.a[:, 0:256], b.T)
        smr = _SmView()

        def wview(w, l):
            return w[l].rearrange("(k p) n -> p k n", p=128)

        def wload(pieces):
            slot = wslots.next()
            pairs = [(slot.a[:, :, d0:d0 + n], src) for (src, d0, n) in pieces]
            if os.environ.get("KNODMA") and wslots.i > 3:
                return slot
            DMA("pool", slot.T.name, pairs, [], [slot.T])
            return slot

        class WQ:
            def __init__(self):
                self.plan = []
                self.issued = []
                self.pos = 0

            def add(self, pieces):
                self.plan.append(pieces)

            def get(self):
                while len(self.issued) < min(len(self.plan), self.pos + 3):
                    self.issued.append(wload(self.plan[len(self.issued)]))
                s = self.issued[self.pos]
                self.pos += 1
                return s

        wq = WQ()

        def plan_layer(l):
            wi = wview(w_in, l)
            wq.add([(wi[:, :, OLR:OLR + 16], 0, 16)])
            for h in range(4):
                wq.add([(wi[:, :, OQ + 128 * h:OQ + 128 * h + 128], 0, 128),
                        (wi[:, :, OK_ + 128 * h:OK_ + 128 * h + 128], 128, 128),
                        (wi[:, :, OV + 256 * h:OV + 256 * h + 256], 256, 256)])
            for h in range(4):
                wq.add([(wi[:, :, OGG + 256 * h:OGG + 256 * h + 256], 0, 256)])
            for j in range(8):
                wq.add([(wi[:, :, o + 128 * j:o + 128 * j + 128], 128 * q, 128) for q, o in enumerate((OX, OC, OH, OB))])
                wq.add([(wi[:, :, o + 128 * j:o + 128 * j + 128], 128 * q, 128) for q, o in enumerate((OGS, OGR))])
            for i in range(8):
                wq.add([(wi[:, :, OM + 1024 * br + 128 * i:OM + 1024 * br + 128 * i + 128], 128 * br, 128) for br in range(3)])
                wq.add([(wview(wbr[br], l)[:, :, 128 * i:128 * i + 128], 128 * br, 128) for br in range(3)])
            wo = wview(w_out, l)
            for g in range(2):
                wq.add([(wo[:, :, 512 * g:512 * g + 512], 0, 512)])

        for l in range(L):
            wa = wview(w_ada, l)
            for g in range(6):
                wq.add([(wa[:, :, 512 * g:512 * g + 512], 0, 512)])
        for _pass in range(NT):
            for l in range(L):
                plan_layer(l)

        DMA("sp", "consts", [(consts.a, consts_d[:, :])], [], [consts.T])
        for l in range(L):
            DMA("sp", f"lvec{l}", [(lvec[l].a, lvec_d[l])], [], [lvec[l].T])
            DMA("sp", f"gwa{l}", [(gwa[l].a, gwa_d[l])], [], [gwa[l].T])
            DMA("pool", f"rgwa{l}", [(rgwa[l].a, rgwa_d[l].rearrange("n i j -> i n j"))], [], [rgwa[l].T])
            DMA("pool", f"rgwx{l}", [(rgwx[l].a, rgwx_d[l].rearrange("n i j -> i n j"))], [], [rgwx[l].T])
        DMA("sp", "cs", [(cs.a, cT[:, :, :])], [], [cs.T])
        VCPY(onesb.a, onesf, [consts.T], [onesb.T])
        P.op("dve", lambda e: e.memset(zlr.a, 1.0), [], [zlr.T])
        for l in range(L if STAGE >= -2 else 0):
            for h in range(4):
                P.op("dve", (lambda a: (lambda e: e.memset(a, 0.0)))(S[l][h].a), [], [S[l][h].T])
                P.op("dve", (lambda a: (lambda e: e.memset(a, 0.0)))(Sb[l][h].a), [], [Sb[l][h].T])
            P.op("dve", (lambda a: (lambda e: e.memset(a, 0.0)))(csc[l].a), [], [csc[l].T])
            P.op("dve", (lambda a: (lambda e: e.memset(a, 0.0)))(crg[l].a), [], [crg[l].T])
            P.op("dve", (lambda a: (lambda e: e.memset(a, 0.0)))(chh[l].a), [], [chh[l].T])
        ACT(scb.a, cs.a, AF.Silu, [cs.T], [scb.T])
        for l in range(L if STAGE >= -1 else 0):
            lv = lvec[l]
            ACT(cneg[l].a, lv.a[:, VLAM:VLAM + 8], AF.Exp, [lv.T], [cneg[l].T], scale=-1.0)
            ACT(cneg[l].a, cneg[l].a, AF.Ln, [cneg[l].T], [cneg[l].T], bias=1.0)
            TS(cneg[l].a, cneg[l].a, -8.0, None, MUL, None, [cneg[l].T], [cneg[l].T])
            TS(hc[l].a, cneg[l].a, 0.5, None, MUL, None, [cneg[l].T], [hc[l].T])
            TS(hvec[l].a, lv.a, 0.5, None, MUL, None, [lv.T], [hvec[l].T])
            for g in range(6):
                slot = wq.get()
                for cc in range(4):
                    ch = g * 4 + cc
                    z = zr.next()
                    for k in range(8):
                        MM(z.a[:, 0:1 + NS], slot.a[:, k, cc * 128:cc * 128 + 128], scb.a[:, k, :], k == 0, k == 7,
                           [slot.T, scb.T], [z.T])
                    ACT(modT[l].a[:, ch, :], z.a[:, 0:1 + NS], AF.Identity, [z.T, lv.T], [modT[l].T],
                        bias=lv.a[:, VBADA + ch:VBADA + ch + 1])
            TS(Ap[l].a, modT[l].a[:, 8:16, 0], 1.0, None, ADD, None, [modT[l].T], [Ap[l].T])
            TTo(Ap[l].a, Ap[l].a, lv.a[:, VNG:VNG + 8], MUL, [Ap[l].T, lv.T], [Ap[l].T])
            for k in range(8):
                TS(As[l].a[:, k, :], modT[l].a[:, 8 + k, 1:1 + NS], 1.0, lv.a[:, VNG + k:VNG + k + 1], ADD, MUL,
                   [modT[l].T, lv.T], [As[l].T])
            TS(gqp[l].a, modT[l].a[:, 16:24, 0], 0.25, None, MUL, None, [modT[l].T], [gqp[l].T])
            TS(gqs[l].a, modT[l].a[:, 16:24, 1:1 + NS], 0.25, None, MUL, None, [modT[l].T], [gqs[l].T])

        def proj(slot, c0, M, rhs, N, extra_R=()):
            z = zr.next()
            for k in range(8):
                MM(z.a[0:M, 0:N], slot.a[:, k, c0:c0 + M], rhs.a[:, k, 0:N], k == 0, k == 7,
                   [slot.T, rhs.T] + list(extra_R), [z.T])
            return z

        def rms_stats(src_fn, nk, N, scale, Rsrc, rstd=rstd):
            z = zr.next()
            for k0 in range(0, nk, 4):
                for k in range(k0, min(nk, k0 + 4)):
                    ACT(sqb.a[:, k % 4, 0:N], src_fn(k), AF.Square, Rsrc, [sqb.T])
                for k in range(k0, min(nk, k0 + 4)):
                    MM(z.a[:, 0:N], onesb.a, sqb.a[:, k % 4, 0:N], k == 0, k == nk - 1, [onesb.T, sqb.T], [z.T])
            ACT(rstd.a[:, 0:N], z.a[:, 0:N], AF.Ln, [z.T], [rstd.T], bias=EPS, scale=scale)
            ACT(rstd.a[:, 0:N], rstd.a[:, 0:N], AF.Exp, [rstd.T], [rstd.T], scale=-0.5)

        class Ctx:
            pass

        def block(l, cx):
            x, N, samp, tile_idx = cx.x, cx.N, cx.samp, cx.tile_idx
            hT, pbr, mg, gbuf, wk, xcd, xcb = cx.hT, cx.pbr, cx.mg, cx.gb, cx.wk, cx.xcd, cx.xcb
            lv = lvec[l]
            md = modT[l]
            last_tile = (not samp) and tile_idx == NT - 1
            PB = int(os.environ.get("KPB", "99")) if not samp else 99
            rms_stats(lambda k: x.a[:, k, 0:N], 8, N, 1.0 / D, [x.T])
            for k in range(8):
                t = wk.next()
                if not samp:
                    STT(t.a[:, 0:N], x.a[:, k, 0:N], Ap[l].a[:, k:k + 1], rstd.a[:, 0:N], MUL, MUL,
                        [x.T, Ap[l].T, rstd.T], [t.T])
                    ACT(hT.a[:, k, 0:N], t.a[:, 0:N], AF.Identity, [t.T, md.T], [hT.T], bias=md.a[:, k, 0:1])
                else:
                    TTo(t.a[:, 0:N], x.a[:, k, 0:N], rstd.a[:, 0:N], MUL, [x.T, rstd.T], [t.T])
                    TTo(t.a[:, 0:N], t.a[:, 0:N], As[l].a[:, k, :], MUL, [t.T, As[l].T], [t.T])
                    TTo(hT.a[:, k, 0:N], t.a[:, 0:N], md.a[:, k, 1:1 + NS], ADD, [t.T, md.T], [hT.T])

            if PB <= 1:
                return
            slot = yield
            z = proj(slot, 0, 16, hT, N)
            ACPY(zlr.a[0:16, 0:N], z.a[0:16, 0:N], [z.T], [zlr.T])
            if not samp:
                for c in range(4):
                    z = zr.next()
                    MM(z.a[:, :], zlr.a[0:17, c * 128:c * 128 + 128], gwa[l].a[0:17, :], True, True, [zlr.T, gwa[l].T], [z.T])
                    ACT(sp_tok.a[:, c, :], z.a[:, :], AF.Exp, [z.T], [sp_tok.T], scale=-1.0)
                    ACT(sp_tok.a[:, c, :], sp_tok.a[:, c, :], AF.Ln, [sp_tok.T], [sp_tok.T], bias=1.0)
                for c in range(4):
                    z = zr.next()
                    MM(z.a[:, :], triUpNeg, sp_tok.a[:, c, :], True, True, [consts.T, sp_tok.T], [z.T])
                    ACT(ekd.a[:, c, :], z.a[:, :], AF.Exp, [z.T], [ekd.T])
            else:
                z = zr.next()
                for h in range(4):
                    MM(z.a[:, h * NS:(h + 1) * NS], gwa[l].a[0:17, h * 128:h * 128 + 128], zlr.a[0:17, 0:N], True, True,
                       [zlr.T, gwa[l].T], [z.T])
                av = aTs.a.rearrange("p h s -> p (h s)")
                ACT(av, z.a[:, 0:4 * NS], AF.Exp, [z.T], [aTs.T], scale=-1.0)
                ACT(av, av, AF.Ln, [aTs.T], [aTs.T], bias=1.0)
                ACT(av, av, AF.Exp, [aTs.T], [aTs.T], scale=-1.0 / 16.0)

            if PB <= 2:
                return
            for h in range(4):
                slot = yield
                if not samp:
                    zb = zr.next()
                    for c in range(4):
                        MM(zb.a[:, c * 128:c * 128 + 128], sp_tok.a[:, c, h * 128:h * 128 + 128], triNeg, True, True,
                           [sp_tok.T, consts.T], [zb.T])
                    E2 = wk.next()
                    E1 = wk.next()
                    ACT(E1.a, zb.a, AF.Exp, [zb.T], [E1.T])
                    ACT(E2.a, zb.a, AF.Exp, [zb.T], [E2.T], scale=-1.0)
                    ACPY(eb[h].a, E1.a.rearrange("p (c n) -> p c n", c=4)[:, :, 127], [E1.T], [eb[h].T])
                    KA = int(os.environ.get("KA", "9"))
                    if KA <= 1:
                        continue
                    zq = proj(slot, 0, 128, hT, N)
                    STT(qe[h].a, zq.a, float(128 ** -0.5), E1.a, MUL, MUL, [zq.T, E1.T], [qe[h].T])
                    zk = proj(slot, 128, 128, hT, N)
                    TTo(ke[h].a, zk.a, E2.a, MUL, [zk.T, E2.T], [ke[h].T])
                    if KA <= 2:
                        continue
                    for c in range(4):
                        z = zr.next()
                        for k in range(8):
                            MM(z.a[:, 0:384], hT.a[:, k, c * 128:c * 128 + 128], slot.a[:, k, 128:512], k == 0, k == 7,
                               [hT.T, slot.T], [z.T])
                        if KA <= 3:
                            continue
                        ACPY(vt(h, c), z.a[:, 128:384], [z.T], [vtok[h].T])
                        if KA <= 4:
                            continue
                        TTo(kdtok[h].a[:, c, :], z.a[:, 0:128], ekd.a[:, c, h * 128:h * 128 + 128], MUL, [z.T, ekd.T, vtok[h].T], [kdtok[h].T])
                else:
                    zq = proj(slot, 0, 128, hT, N)
                    TS(qss.a[:, h, :], zq.a[:, 0:N], float(128 ** -0.5), None, MUL, None, [zq.T], [qss.T])
                    z = zr.next()
                    for k in range(8):
                        MM(z.a[0:NS, 0:384], hT.a[:, k, 0:N], slot.a[:, k, 128:512], k == 0, k == 7, [hT.T, slot.T], [z.T])
                    ACPY(kvs.a[:, h, :], z.a[0:NS, 0:384], [z.T], [kvs.T])

            if PB <= 3:
                return
            if not samp:
                for c in range(4):
                    cs_ = slice(c * 128, c * 128 + 128)
                    sa4 = zr.next()
                    for h in range(4):
                        MM(sa4.a[:, h * 128:h * 128 + 128], ke[h].a[:, cs_], qe[h].a[:, cs_], True, True, [qk.T], [sa4.T])
                    ams = []
                    for h in range(4):
                        am = attm.next()
                        TTo(am.a, sa4.a[:, h * 128:h * 128 + 128], maskT, MUL, [sa4.T, consts.T], [am.T])
                        ams.append(am)
                    sos = [zr.next(), zr.next()]
                    for h in range(4):
                        so = sos[h // 2]
                        for vc in range(2):
                            col = (h % 2) * 256 + vc * 128
                            MM(so.a[:, col:col + 128], vt(h, c)[:, vc * 128:vc * 128 + 128], ams[h].a, True, False,
                               [vtok[h].T, ams[h].T], [so.T])
                            MM(so.a[:, col:col + 128], Sb[l][h].a[:, vc * 128:vc * 128 + 128], qe[h].a[:, cs_], False, True,
                               [Sb[l][h].T, qk.T], [so.T])
                    sus = [zr.next(), zr.next()]
                    for h in range(4):
                        su = sus[h // 2]
                        MM(su.a[:, (h % 2) * 256:(h % 2) * 256 + 256], kdtok[h].a[:, c, :], vt(h, c), True, True,
                           [kdtok[h].T, vtok[h].T], [su.T])
                    for h in range(4):
                        so = sos[h // 2]
                        ACPY(oT[h].a[:, :, cs_], so.a[:, (h % 2) * 256:(h % 2) * 256 + 256].rearrange("p (a b) -> p a b", a=2),
                             [so.T], [oT[h].T])
                    for h in range(4):
                        su = sus[h // 2]
                        STT(S[l][h].a, S[l][h].a, eb[h].a[:, c:c + 1], su.a[:, (h % 2) * 256:(h % 2) * 256 + 256], MUL, ADD,
                            [S[l][h].T, eb[h].T, su.T], [S[l][h].T])
                        ACPY(Sb[l][h].a, S[l][h].a, [S[l][h].T], [Sb[l][h].T])
                if last_tile:
                    for h in range(4):
                        DMA("sp", f"S{l}_{h}", [(ogla_p[l, h], S[l][h].a)], [S[l][h].T], [])
            else:
                pass

            def s_A(s_):
                si = Sin[s_ % 2]
                DMA("sp", "ld_" + si.T.name + "s", [(si.a, sgla[l, s_].rearrange("h d v -> d h v"))], [], [si.T])
                km = km4.bufs[s_ % 2]
                TS(km.a, kvs.a[:, :, 0:128], ident[0:NS, s_:s_ + 1], None, MUL, None, [kvs.T, consts.T, sqb.T], [km.T])

            def s_B(s_):
                si, sn, km, sbf = Sin[s_ % 2], Snw[s_ % 2], km4.bufs[s_ % 2], snbq[s_ % 2]
                sus = [zr.next(), zr.next()]
                for h in range(4):
                    su = sus[h // 2]
                    MM(su.a[:, (h % 2) * 256:(h % 2) * 256 + 256], km.a[:, h, :], kvs.a[:, h, 128:384], True, True,
                       [km.T, kvs.T, sqb.T], [su.T])
                for h in range(4):
                    su = sus[h // 2]
                    STT(sn.a[:, h, :], si.a[:, h, :], aTs.a[:, h, s_:s_ + 1], su.a[:, (h % 2) * 256:(h % 2) * 256 + 256], MUL, ADD,
                        [si.T, aTs.T, su.T], [sn.T])
                ACPY(sbf.a, sn.a, [sn.T, qk.T], [sbf.T])
                DMA("sp", "st_" + sn.T.name + "s", [(ogla_s[l, s_].rearrange("h d v -> d h v"), sn.a)], [sn.T], [])

            def s_C(s_):
                sbf = snbq[s_ % 2]
                so = zr.next()
                for h in range(4):
                    for vc in range(2):
                        MM(so.a[:, 2 * h + vc:2 * h + vc + 1], sbf.a[:, h, vc * 128:vc * 128 + 128], qss.a[:, h, s_:s_ + 1], True, True,
                           [sbf.T, qss.T, qk.T], [so.T])
                ACPY(oTs.a[:, :, s_:s_ + 1], so.a[:, 0:8].rearrange("p (a b) -> p a b", b=1), [so.T], [oTs.T])

            def sstep(n):
                if n == 0:
                    ACPY(qk.a[0:1, 0, 0:1], qk.a[0:1, 0, 0:1], [], [qk.T])
                    ACPY(sqb.a[0:1, 3, 0:1], sqb.a[0:1, 3, 0:1], [], [sqb.T])
                    s_A(0)
                if n + 1 < NS:
                    s_A(n + 1)
                if 0 <= n - 1 < NS:
                    s_C(n - 1)
                if n < NS:
                    s_B(n)

            if PB <= 4:
                return
            rsb = cx.rsb
            for h in range(4):
                if not samp:
                    rms_stats(lambda vc: oT[h].a[:, vc, :], 2, N, 1.0 / 256, [oT[h].T], rsb[h])
            for h in range(4):
                slot = yield
                for vc in range(2):
                    zg = proj(slot, vc * 128, 128, hT, N)
                    if samp:
                        ACT(sgs.a[:, 2 * h + vc, :], zg.a[:, 0:N], AF.Tanh, [zg.T], [sgs.T], scale=0.5)
                        STT(sgs.a[:, 2 * h + vc, :], sgs.a[:, 2 * h + vc, :], 1.0, zg.a[:, 0:N], ADD, MUL, [sgs.T, zg.T], [sgs.T])
                        continue
                    sg = wk.next()
                    ACT(sg.a[:, 0:N], zg.a[:, 0:N], AF.Tanh, [zg.T], [sg.T], scale=0.5)
                    STT(sg.a[:, 0:N], sg.a[:, 0:N], 1.0, zg.a[:, 0:N], ADD, MUL, [sg.T, zg.T], [sg.T])
                    t1 = wk.next()
                    osrc = oT[h].a[:, vc, :] if not samp else oTs.a[:, 2 * h + vc, :]
                    oTt = oT[h].T if not samp else oTs.T
                    STT(t1.a[:, 0:N], osrc, lv.a[:, VGNG + vc:VGNG + vc + 1], rsb[h].a[:, 0:N], MUL, MUL,
                        [oTt, lv.T, rsb[h].T], [t1.T])
                    TTo(pbr[0].a[:, 2 * h + vc, 0:N], t1.a[:, 0:N], sg.a[:, 0:N], MUL, [t1.T, sg.T], [pbr[0].T])

            if PB <= 5:
                return
            hv = hvec[l]
            def rg2_tail(j, zrr, zi, zg, xc):
                a_ = wk.next()
                ACT(a_.a[:, 0:N], zrr.a[:, 0:N], AF.Tanh, [zrr.T, hv.T], [a_.T], bias=hv.a[:, VRGBA + j:VRGBA + j + 1], scale=0.5)
                ig = wk.next()
                ACT(ig.a[:, 0:N], zi.a[:, 0:N], AF.Tanh, [zi.T, hv.T], [ig.T], bias=hv.a[:, VRGBX + j:VRGBX + j + 1], scale=0.5)
                sg = wk.next()
                ACT(sg.a[:, 0:N], zg.a[:, 0:N], AF.Tanh, [zg.T], [sg.T], scale=0.5)
                ACT(a_.a[:, 0:N], a_.a[:, 0:N], AF.Exp, [a_.T, hc[l].T], [a_.T], bias=hc[l].a[:, j:j + 1], scale=hc[l].a[:, j:j + 1])
                sq_ = wk.next()
                STT(sq_.a[:, 0:N], a_.a[:, 0:N], 0.9999999, a_.a[:, 0:N], MIN, MUL, [a_.T], [sq_.T])
                ACT(sq_.a[:, 0:N], sq_.a[:, 0:N], AF.Sqrt, [sq_.T], [sq_.T], bias=0.25, scale=-0.25)
                STT(ig.a[:, 0:N], ig.a[:, 0:N], 1.0, xc.a[:, 0:N], ADD, MUL, [ig.T, xc.T], [ig.T])
                STT(sg.a[:, 0:N], sg.a[:, 0:N], 1.0, zg.a[:, 0:N], ADD, MUL, [sg.T, zg.T], [sg.T])
                TTo(ig.a[:, 0:N], ig.a[:, 0:N], sq_.a[:, 0:N], MUL, [ig.T, sq_.T], [ig.T])
                if not samp:
                    P.op("dve", (lambda o, d0, d1, ini: (lambda e: e.tensor_tensor_scan(out=o, data0=d0, data1=d1, initial=ini,
                                                                                        op0=MUL, op1=ADD)))(
                        xc.a[:, 0:N], a_.a[:, 0:N], ig.a[:, 0:N], chh[l].a[:, j:j + 1]),
                        [a_.T, ig.T, chh[l].T], [xc.T])
                    VCPY(chh[l].a[:, j:j + 1], xc.a[:, N - 1:N], [xc.T], [chh[l].T])
                    hsrc = xc.a[:, 0:N]
                    hT_ = xc.T
                else:
                    TTo(a_.a[:, 0:N], a_.a[:, 0:N], srgh.a[:, j, :], MUL, [a_.T, srgh.T], [a_.T])
                    TTo(nrgh.a[:, j, :], a_.a[:, 0:N], ig.a[:, 0:N], ADD, [a_.T, ig.T], [nrgh.T])
                    hsrc = nrgh.a[:, j, :]
                    hT_ = nrgh.T
                TTo(pbr[2].a[:, j, 0:N], hsrc, sg.a[:, 0:N], MUL, [hT_, sg.T], [pbr[2].T])

            pending = None
            for j in range(8):
                slotA = yield
                if samp:
                    sstep(2 * j)
                cw = [lv.a[:, VRGW + 4 * j + r:VRGW + 4 * j + r + 1] for r in range(4)]
                cb = lv.a[:, VRGCB + j:VRGCB + j + 1]
                zx = proj(slotA, 0, 128, hT, N)
                xc = xcd.next()
                if not samp:
                    ur = ub.next()
                    ACPY(ur.a[:, 3:3 + N], zx.a[:, 0:N], [zx.T], [ur.T])
                    ACPY(ur.a[:, 0:3], crg[l].a[:, j, :], [crg[l].T], [ur.T])
                    ACPY(crg[l].a[:, j, :], ur.a[:, N:N + 3], [ur.T], [crg[l].T])
                    ACT(xc.a[:, 0:N], ur.a[:, 0:N], AF.Identity, [ur.T, lv.T], [xc.T], bias=cb, scale=cw[0])
                    for r in range(1, 4):
                        STT(xc.a[:, 0:N], ur.a[:, r:r + N], cw[r], xc.a[:, 0:N], MUL, ADD, [ur.T, lv.T, xc.T], [xc.T])
                else:
                    ACPY(nrgc.a[:, j, 2, :], zx.a[:, 0:N], [zx.T], [nrgc.T])
                    VCPY(nrgc.a[:, j, 0:2, :], srgc.a[:, j, 1:3, :], [srgc.T], [nrgc.T])
                    TS(xc.a[:, 0:N], srgc.a[:, j, 0, :], cw[0], cb, MUL, ADD, [srgc.T, lv.T], [xc.T])
                    for r in range(1, 3):
                        STT(xc.a[:, 0:N], srgc.a[:, j, r, :], cw[r], xc.a[:, 0:N], MUL, ADD, [srgc.T, lv.T, xc.T], [xc.T])
                    STT(xc.a[:, 0:N], nrgc.a[:, j, 2, :], cw[3], xc.a[:, 0:N], MUL, ADD, [nrgc.T, lv.T, xc.T], [xc.T])
                if pending is not None:
                    rg2_tail(*pending)
                    pending = None
                ACPY(xcb.a[:, 0:N], xc.a[:, 0:N], [xc.T], [xcb.T])

                slot = slotA
                w0 = lv.a[:, VSCW + 3 * j + 0:VSCW + 3 * j + 1]
                w1 = lv.a[:, VSCW + 3 * j + 1:VSCW + 3 * j + 2]
                w2 = lv.a[:, VSCW + 3 * j + 2:VSCW + 3 * j + 3]
                zc = proj(slot, 128, 128, hT, N)
                tcb = wk.next()
                ACPY(tcb.a[:, 0:N], zc.a[:, 0:N], [zc.T], [tcb.T])
                zh = proj(slot, 256, 128, hT, N)
                uc = wk.next()
                if not samp:
                    u = ub.next()
                    TTo(u.a[:, 2:2 + N], zh.a[:, 0:N], tcb.a[:, 0:N], MUL, [zh.T, tcb.T], [u.T])
                    ACPY(u.a[:, 0:2], csc[l].a[:, j, :], [csc[l].T], [u.T])
                    ACPY(csc[l].a[:, j, :], u.a[:, N:N + 2], [u.T], [csc[l].T])
                    ACT(uc.a[:, 0:N], u.a[:, 0:N], AF.Identity, [u.T, lv.T], [uc.T], scale=w0)
                    STT(uc.a[:, 0:N], u.a[:, 1:1 + N], w1, uc.a[:, 0:N], MUL, ADD, [u.T, lv.T, uc.T], [uc.T])
                    STT(uc.a[:, 0:N], u.a[:, 2:2 + N], w2, uc.a[:, 0:N], MUL, ADD, [u.T, lv.T, uc.T], [uc.T])
                else:
                    TTo(nsc.a[:, j, 1, :], zh.a[:, 0:N], tcb.a[:, 0:N], MUL, [zh.T, tcb.T], [nsc.T])
                    VCPY(nsc.a[:, j, 0, :], ssc.a[:, j, 1, :], [ssc.T], [nsc.T])
                    TS(uc.a[:, 0:N], ssc.a[:, j, 0, :], w0, None, MUL, None, [ssc.T, lv.T], [uc.T])
                    STT(uc.a[:, 0:N], ssc.a[:, j, 1, :], w1, uc.a[:, 0:N], MUL, ADD, [ssc.T, lv.T, uc.T], [uc.T])
                    STT(uc.a[:, 0:N], nsc.a[:, j, 1, :], w2, uc.a[:, 0:N], MUL, ADD, [nsc.T, lv.T, uc.T], [uc.T])
                zb_ = proj(slot, 384, 128, hT, N)
                TTo(uc.a[:, 0:N], zb_.a[:, 0:N], uc.a[:, 0:N], MUL, [zb_.T, uc.T], [uc.T])
                slotB = yield
                if samp:
                    sstep(2 * j + 1)
                zg = proj(slotB, 0, 128, hT, N)
                sg = wk.next()
                ACT(sg.a[:, 0:N], zg.a[:, 0:N], AF.Tanh, [zg.T], [sg.T], scale=0.5)
                STT(sg.a[:, 0:N], sg.a[:, 0:N], 1.0, zg.a[:, 0:N], ADD, MUL, [sg.T, zg.T], [sg.T])
                TTo(pbr[1].a[:, j, 0:N], uc.a[:, 0:N], sg.a[:, 0:N], MUL, [uc.T, sg.T], [pbr[1].T])

                slot = slotB
                zrr = zr.next()
                MM(zrr.a[:, 0:N], rgwa[l].a[:, j, :], xcb.a[:, 0:N], True, True, [rgwa[l].T, xcb.T], [zrr.T])
                zi = zr.next()
                MM(zi.a[:, 0:N], rgwx[l].a[:, j, :], xcb.a[:, 0:N], True, True, [rgwx[l].T, xcb.T], [zi.T])
                zg = proj(slot, 128, 128, hT, N)
                if samp or not cx.defer:
                    rg2_tail(j, zrr, zi, zg, xc)
                else:
                    pending = (j, zrr, zi, zg, xc)
            if pending is not None:
                rg2_tail(*pending)
                pending = None
            if samp:
                sstep(NS)
                for h in range(4):
                    rms_stats(lambda vc: oTs.a[:, 2 * h + vc, :], 2, N, 1.0 / 256, [oTs.T], rsb[h])
                for h in range(4):
                    for vc in range(2):
                        t1 = wk.next()
                        STT(t1.a[:, 0:N], oTs.a[:, 2 * h + vc, :], lv.a[:, VGNG + vc:VGNG + vc + 1], rsb[h].a[:, 0:N], MUL, MUL,
                            [oTs.T, lv.T, rsb[h].T], [t1.T])
                        TTo(pbr[0].a[:, 2 * h + vc, 0:N], t1.a[:, 0:N], sgs.a[:, 2 * h + vc, :], MUL, [t1.T, sgs.T], [pbr[0].T])
            if last_tile:
                DMA("sp", f"csc{l}", [(osc_p[l], csc[l].a)], [csc[l].T], [])
                DMA("sp", f"crg{l}", [(orgc_p[l], crg[l].a)], [crg[l].T], [])
                DMA("sp", f"chh{l}", [(orgh_p[l], chh[l].a)], [chh[l].T], [])
            if samp:
                DMA("sp", "nsc", [(osc_s[l], nsc.a)], [nsc.T], [])
                DMA("sp", "nrgc", [(orgc_s[l], nrgc.a)], [nrgc.T], [])
                DMA("sp", "nrgh", [(orgh_s[l], nrgh.a)], [nrgh.T], [])

            if PB <= 7:
                return
            for i in range(8):
                s1 = yield
                for br in range(3):
                    zm = proj(s1, 128 * br, 128, hT, N)
                    ACT(gbuf[br].a[:, 0:N], zm.a[:, 0:N], AF.Tanh, [zm.T, hv.T], [gbuf[br].T],
                        bias=hv.a[:, VBM + 8 * br + i:VBM + 8 * br + i + 1], scale=0.5)
                acc = wk.next()
                s2 = yield
                for br in range(3):
                    zy = proj(s2, 128 * br, 128, pbr[br], N)
                    if br == 0:
                        STT(acc.a[:, 0:N], gbuf[0].a[:, 0:N], 1.0, zy.a[:, 0:N], ADD, MUL, [zy.T, gbuf[0].T], [acc.T])
                    else:
                        STT(gbuf[br].a[:, 0:N], gbuf[br].a[:, 0:N], 1.0, zy.a[:, 0:N], ADD, MUL, [zy.T, gbuf[br].T], [gbuf[br].T])
                        if br == 1:
                            TTo(acc.a[:, 0:N], acc.a[:, 0:N], gbuf[1].a[:, 0:N], ADD, [acc.T, gbuf[1].T], [acc.T])
                        else:
                            TTo(mg.a[:, i, 0:N], acc.a[:, 0:N], gbuf[2].a[:, 0:N], ADD, [acc.T, gbuf[2].T], [mg.T])

            for i in range(8):
                if i % 4 == 0:
                    so_ = yield
                zo = proj(so_, (i % 4) * 128, 128, mg, N)
                if not samp:
                    STT(x.a[:, i, 0:N], zo.a[:, 0:N], gqp[l].a[:, i:i + 1], x.a[:, i, 0:N], MUL, ADD, [zo.T, gqp[l].T, x.T], [x.T])
                else:
                    t = wk.next()
                    TTo(t.a[:, 0:N], zo.a[:, 0:N], gqs[l].a[:, i, :], MUL, [zo.T, gqs[l].T], [t.T])
                    TTo(x.a[:, i, 0:N], x.a[:, i, 0:N], t.a[:, 0:N], ADD, [x.T, t.T], [x.T])

        def final_norm(x, N, dst, dst_dram, key):
            rms_stats(lambda k: x.a[:, k, 0:N], 8, N, 1.0 / D, [x.T])
            for k in range(8):
                STT(dst.a[:, k, 0:N], x.a[:, k, 0:N], lvec[0].a[:, VFG + k:VFG + k + 1], rstd.a[:, 0:N], MUL, MUL,
                    [x.T, lvec[0].T, rstd.T], [dst.T])
            DMA("sp", key, [(dst_dram, dst.a[:, :, 0:N])], [dst.T], [])

        def run_blocks(l, ctxs):
            gens = [block(l, c) for c in ctxs]
            active = []
            for g in gens:
                try:
                    next(g)
                    active.append(g)
                except StopIteration:
                    pass
            while active:
                slot = wq.get()
                nxt = []
                for g in active:
                    try:
                        g.send(slot)
                        nxt.append(g)
                    except StopIteration:
                        pass
                active = nxt

        cs_ctx = Ctx()
        cs_ctx.x, cs_ctx.N, cs_ctx.samp, cs_ctx.tile_idx = xs, NS, True, 0
        cs_ctx.hT, cs_ctx.pbr, cs_ctx.mg, cs_ctx.gb, cs_ctx.wk, cs_ctx.xcd, cs_ctx.xcb, cs_ctx.rsb = (
            hT_s, pbr_s, mg_s, gb_s, wk_s, xcd_s, xcb_s, rsb_s)

        def prompt_ctx(xbuf, tt):
            c = Ctx()
            c.x, c.N, c.samp, c.tile_idx = xbuf, TT, False, tt
            c.hT, c.pbr, c.mg, c.gb, c.wk, c.xcd, c.xcb, c.rsb = hT, pbr, mg, gbuf, wk, xcd, xcb, [rstd, gbuf[0], gbuf[1], gbuf[2]]
            return c

        DMA("sp", "xs", [(xs.a, xsT[:, :, :])], [], [xs.T])
        xcur = xt.next()
        DMA("sp", xcur.T.name, [(xcur.a, xT[:, :, 0:TT])], [], [xcur.T])
        for tt in range(NT):
            if tt + 1 < NT:
                xnext = xt.next()
                DMA("sp", xnext.T.name, [(xnext.a, xT[:, :, (tt + 1) * TT:(tt + 2) * TT])], [], [xnext.T])
            for l in range(L):
                ctxs = [prompt_ctx(xcur, tt)]
                if tt == 0:
                    DMA("sp", "ssc", [(ssc.a, sscf[l])], [], [ssc.T])
                    DMA("sp", "srgc", [(srgc.a, srgcf[l])], [], [srgc.T])
                    DMA("sp", "srgh", [(srgh.a, srghf[l])], [], [srgh.T])
                    ctxs.append(cs_ctx)
                for c_ in ctxs:
                    c_.defer = (len(ctxs) == 1)
                run_blocks(l, ctxs)
            final_norm(xcur, TT, xcur, yT[:, :, tt * TT:(tt + 1) * TT], xcur.T.name)
            if tt == 0:
                final_norm(xs, NS, xs, ysT[:, :, :], "xs")
            if tt + 1 < NT:
                xcur = xnext
        assert STAGE < 9 or wq.pos == len(wq.plan), (wq.pos, len(wq.plan))
        outs = [b.T for b in xt.bufs] + [xs.T, nsc.T, nrgc.T, nrgh.T] + [b.T for b in Snw]
        for l in range(L):
            outs += [csc[l].T, crg[l].T, chh[l].T] + [S[l][h].T for h in range(4)]
        P.wait_all("sp", outs)
        P.emit(st)
    return nc


_CACHE = {}


def _fm(v):
    return np.ascontiguousarray(np.swapaxes(v.reshape(v.shape[:-1] + (8, 128)), -1, -2))


def kernel(x_prompt, x_sample, c_prompt, c_sample, state_gla, state_sc_conv, state_rg_conv, state_rg_h,
           w_ada, b_ada, norm_gain, w_in, gla_w_alpha, gla_b_alpha, gla_norm_gain, gla_w_branch, sc_conv_w,
           sc_w_branch, rg_conv_w, rg_conv_b, rg_w_a, rg_b_a, rg_w_x, rg_b_x, rg_lambda, rg_w_branch,
           b_merge, w_out, final_gain):
    f = lambda a: np.ascontiguousarray(np.asarray(a, dtype=np.float32))
    x_prompt, x_sample, c_prompt, c_sample = f(x_prompt), f(x_sample), f(c_prompt), f(c_sample)
    state_gla, state_sc_conv, state_rg_conv, state_rg_h = f(state_gla), f(state_sc_conv), f(state_rg_conv), f(state_rg_h)
    n = 8
    if "nc" not in _CACHE:
        _CACHE["nc"] = build_program()
    nc = _CACHE["nc"]
    lvec = np.zeros((L, 128, NV), np.float32)
    for l in range(L):
        lvec[l, :, VNG:VNG + 8] = _fm(f(norm_gain)[l])
        lvec[l, :, VBADA:VBADA + 24] = f(b_ada)[l].reshape(24, 128).T
        lvec[l, :, VGNG:VGNG + 2] = f(gla_norm_gain)[l].reshape(2, 128).T
        lvec[l, :, VSCW:VSCW + 24] = f(sc_conv_w)[l].reshape(3, 8, 128).transpose(2, 1, 0).reshape(128, 24)
        lvec[l, :, VRGW:VRGW + 32] = f(rg_conv_w)[l].reshape(4, 8, 128).transpose(2, 1, 0).reshape(128, 32)
        lvec[l, :, VRGCB:VRGCB + 8] = _fm(f(rg_conv_b)[l])
        lvec[l, :, VRGBA:VRGBA + 8] = _fm(f(rg_b_a)[l])
        lvec[l, :, VRGBX:VRGBX + 8] = _fm(f(rg_b_x)[l])
        lvec[l, :, VLAM:VLAM + 8] = _fm(f(rg_lambda)[l])
        lvec[l, :, VBM:VBM + 24] = f(b_merge)[l].reshape(24, 128).T
        lvec[l, :, VFG:VFG + 8] = _fm(f(final_gain))
    gwa = np.concatenate([f(gla_w_alpha), f(gla_b_alpha)[:, None, :]], axis=1)
    s_idx = np.arange(128)[:, None]
    t_idx = np.arange(128)[None, :]
    consts = np.concatenate([
        np.eye(128, dtype=np.float32),
        np.ones((128, 128), np.float32),
        np.where(s_idx <= t_idx, -1.0 / 16.0, 0.0).astype(np.float32),
        np.where(s_idx > t_idx, -1.0 / 16.0, 0.0).astype(np.float32),
        np.where(s_idx <= t_idx, 1.0, 0.0).astype(np.float32),
    ], axis=1)
    shared = {
        "w_ada": f(w_ada), "w_in": f(w_in), "wgla": f(gla_w_branch), "wsc": f(sc_w_branch), "wrg": f(rg_w_branch),
        "w_out": f(w_out), "rgwa": f(rg_w_a), "rgwx": f(rg_w_x), "gwa": np.ascontiguousarray(gwa),
        "lvec": lvec, "consts": np.ascontiguousarray(consts),
    }
    in_maps = []
    for c in range(n):
        sl = slice(c * NS, (c + 1) * NS)
        m = dict(shared)
        m["xT"] = np.ascontiguousarray(x_prompt[c].reshape(2048, 8, 128).transpose(2, 1, 0))
        m["xsT"] = np.ascontiguousarray(x_sample[sl, 0, :].reshape(NS, 8, 128).transpose(2, 1, 0))
        cc = np.concatenate([c_prompt[c:c + 1], c_sample[sl]], axis=0)
        m["cT"] = np.ascontiguousarray(cc.reshape(1 + NS, 8, 128).transpose(2, 1, 0))
        m["sgla"] = np.ascontiguousarray(state_gla[:, sl])
        m["sscf"] = np.ascontiguousarray(state_sc_conv[:, sl].reshape(L, NS, 2, 8, 128).transpose(0, 4, 3, 2, 1))
        m["srgcf"] = np.ascontiguousarray(state_rg_conv[:, sl].reshape(L, NS, 3, 8, 128).transpose(0, 4, 3, 2, 1))
        m["srghf"] = np.ascontiguousarray(state_rg_h[:, sl].reshape(L, NS, 8, 128).transpose(0, 3, 2, 1))
        in_maps.append(m)
    if os.environ.get("KCORES"):
        n1 = int(os.environ["KCORES"])
        res = run_bass_kernel_spmd(nc, in_maps[:n1], core_ids=list(range(n1)), trace=bool(os.environ.get("KTRACE")))
        return res
    res = run_bass_kernel_spmd(nc, in_maps, core_ids=list(range(n)))
    R = res.results
    B = 8
    y_prompt = np.stack([R[c]["yT"].transpose(2, 1, 0).reshape(2048, D) for c in range(n)], axis=0)
    y_sample = np.concatenate([R[c]["ysT"].transpose(2, 1, 0).reshape(NS, 1, D) for c in range(n)], axis=0)
    gla_p = np.stack([R[c]["ogla_p"] for c in range(n)], axis=1)
    sc_p = np.stack([R[c]["osc_p"].transpose(0, 3, 2, 1).reshape(L, 2, D) for c in range(n)], axis=1)
    rgc_p = np.stack([R[c]["orgc_p"].transpose(0, 3, 2, 1).reshape(L, 3, D) for c in range(n)], axis=1)
    rgh_p = np.stack([R[c]["orgh_p"].transpose(0, 2, 1).reshape(L, D) for c in range(n)], axis=1)
    gla_s = np.concatenate([R[c]["ogla_s"] for c in range(n)], axis=1)
    sc_s = np.concatenate([R[c]["osc_s"].transpose(0, 4, 3, 2, 1).reshape(L, NS, 2, D) for c in range(n)], axis=1)
    rgc_s = np.concatenate([R[c]["orgc_s"].transpose(0, 4, 3, 2, 1).reshape(L, NS, 3, D) for c in range(n)], axis=1)
    rgh_s = np.concatenate([R[c]["orgh_s"].transpose(0, 3, 2, 1).reshape(L, NS, D) for c in range(n)], axis=1)
    outs = (y_prompt, y_sample, gla_p, sc_p, rgc_p, rgh_p, gla_s, sc_s, rgc_s, rgh_s)
    return tuple(np.ascontiguousarray(o.astype(np.float32)) for o in outs)
```

```python
from contextlib import ExitStack
import os
import numpy as np
import concourse.bass as bass
import concourse.mybir as mybir
from concourse.bass_utils import run_bass_kernel_spmd

F32 = mybir.dt.float32
BF16 = mybir.dt.bfloat16
AF = mybir.ActivationFunctionType
ALU = mybir.AluOpType

D = 1024
L = 2
TT = 512
NT = 4
NS = 16
NV = 160
N_IN = 12304
OQ, OK_, OV, OGG, OLR = 0, 512, 1024, 2048, 3072
OB, OC, OH, OGS, OX, OGR, OM = 3088, 4112, 5136, 6160, 7184, 8208, 9232
VNG, VBADA, VGNG, VSCW, VRGW, VRGCB, VRGBA, VRGBX, VLAM, VBM, VFG = 0, 8, 32, 34, 58, 90, 98, 106, 114, 122, 146
EPS = 1e-6


class Tile:
    __slots__ = ("name", "w", "r")

    def __init__(self, name):
        self.name = name
        self.w = None
        self.r = []


class Buf:
    __slots__ = ("a", "T")

    def __init__(self, a, T):
        self.a = a
        self.T = T


class Prog:
    ENGS = ("pe", "act", "dve", "pool", "sp")

    def __init__(self, nc):
        self.nc = nc
        self.streams = {e: [] for e in self.ENGS}
        self.dma_cnt = {}
        self.sig = {e: set() for e in self.ENGS}
        self.window = {"act": 1 << 30, "dve": 1 << 30, "pool": 1 << 30}

    def _collect(self, eng, idx, reads, writes):
        deps = set()
        for t in reads:
            if t.w is not None:
                deps.add(t.w)
        for t in writes:
            if t.w is not None:
                deps.add(t.w)
            deps.update(t.r)
        out = []
        for d in deps:
            if d[0] == "e" and d[1] == eng:
                if eng in ("pe", "sp"):
                    continue
                if idx - d[2] > self.window[eng]:
                    continue
            out.append(d)
        for d in out:
            if d[0] == "e":
                self.sig[d[1]].add(d[2])
        return out

    def _mark(self, me, reads, writes):
        for t in reads:
            t.r.append(me)
            if len(t.r) > 64:
                t.r = t.r[-48:]
        for t in writes:
            t.w = me
            t.r = []

    def op(self, eng, fn, reads=(), writes=()):
        idx = len(self.streams[eng])
        deps = self._collect(eng, idx, reads, writes)
        self.streams[eng].append((fn, deps, None))
        self._mark(("e", eng, idx), reads, writes)

    def dma(self, eng, fns, key, reads=(), writes=()):
        idx = len(self.streams[eng])
        deps = self._collect(eng, idx, reads, writes)
        cnt = self.dma_cnt.get(key, 0)
        for i, fn in enumerate(fns):
            cnt += 16
            self.streams[eng].append((fn, deps if i == 0 else [], key))
        self.dma_cnt[key] = cnt
        self._mark(("s", key, cnt), reads, writes)

    def wait_all(self, eng, tiles):
        idx = len(self.streams[eng])
        deps = self._collect(eng, idx, (), tiles)
        self.streams[eng].append((None, deps, None))

    def emit(self, stack):
        nc = self.nc
        esem = {e: stack.enter_context(nc.semaphore("S_" + e)) for e in self.ENGS if e != "sp"}
        dsem = {k: stack.enter_context(nc.semaphore("D_" + str(k))) for k in self.dma_cnt}
        cnt_at = {}
        for e in self.ENGS:
            c = 0
            m = {}
            for i in range(len(self.streams[e])):
                if i in self.sig[e]:
                    c += 1
                    m[i] = c
            cnt_at[e] = m
        block = stack.enter_context(nc.Block())

        def run(eng_name):
            def body(eng):
                waited = {}
                for i, (fn, deps, key) in enumerate(self.streams[eng_name]):
                    for d in deps:
                        if d[0] == "e":
                            sem, val, k = esem[d[1]], cnt_at[d[1]][d[2]], ("e", d[1])
                        else:
                            sem, val, k = dsem[d[1]], d[2], ("s", d[1])
                        if waited.get(k, 0) >= val:
                            continue
                        waited[k] = val
                        eng.wait_ge(sem, val)
                    if fn is None:
                        continue
                    ins = fn(eng)
                    if key is not None:
                        ins.then_inc(dsem[key], 16)
                    elif i in self.sig[eng_name]:
                        ins.then_inc(esem[eng_name], 1)
            return body

        block.tensor(run("pe"))
        block.scalar(run("act"))
        block.vector(run("dve"))
        block.gpsimd(run("pool"))
        block.sync(run("sp"))


class Ring:
    def __init__(self, bufs):
        self.bufs = bufs
        self.i = 0

    def next(self):
        b = self.bufs[self.i % len(self.bufs)]
        self.i += 1
        return b


def build_program():
    nc = bass.Bass("TRN2", target_bir_lowering=False)

    def din(name, shape):
        return nc.dram_tensor(name, list(shape), F32, kind="ExternalInput").ap()

    def dout(name, shape):
        return nc.dram_tensor(name, list(shape), F32, kind="ExternalOutput").ap()

    xT = din("xT", [128, 8, NT * TT])
    xsT = din("xsT", [128, 8, NS])
    cT = din("cT", [128, 8, 1 + NS])
    sgla = din("sgla", [L, NS, 4, 128, 256])
    sscf = din("sscf", [L, 128, 8, 2, NS])
    srgcf = din("srgcf", [L, 128, 8, 3, NS])
    srghf = din("srghf", [L, 128, 8, NS])
    w_ada = din("w_ada", [L, D, 3 * D])
    w_in = din("w_in", [L, D, N_IN])
    wbr = [din("wgla", [L, D, D]), din("wsc", [L, D, D]), din("wrg", [L, D, D])]
    w_out = din("w_out", [L, D, D])
    rgwa_d = din("rgwa", [L, 8, 128, 128])
    rgwx_d = din("rgwx", [L, 8, 128, 128])
    gwa_d = din("gwa", [L, 17, 512])
    lvec_d = din("lvec", [L, 128, NV])
    consts_d = din("consts", [128, 5 * 128])

    yT = dout("yT", [128, 8, NT * TT])
    ysT = dout("ysT", [128, 8, NS])
    ogla_p = dout("ogla_p", [L, 4, 128, 256])
    osc_p = dout("osc_p", [L, 128, 8, 2])
    orgc_p = dout("orgc_p", [L, 128, 8, 3])
    orgh_p = dout("orgh_p", [L, 128, 8])
    ogla_s = dout("ogla_s", [L, NS, 4, 128, 256])
    osc_s = dout("osc_s", [L, 128, 8, 2, NS])
    orgc_s = dout("orgc_s", [L, 128, 8, 3, NS])
    orgh_s = dout("orgh_s", [L, 128, 8, NS])

    with ExitStack() as st:
        P = Prog(nc)
        STAGE = int(os.environ.get("KSTAGE", "9"))

        def sb(name, shape, dt=F32):
            t = st.enter_context(nc.sbuf_tensor("s_" + name, list(shape), dt))
            return Buf(t[:], Tile(name))

        def ps(name, shape):
            t = st.enter_context(nc.psum_tensor("p_" + name, list(shape), F32))
            return Buf(t[:], Tile(name))

        def MM(out, lhsT, rhs, start, stop, R, W):
            P.op("pe", lambda e: e.matmul(out, lhsT=lhsT, rhs=rhs, start=start, stop=stop), R, W)

        def ACT(out, in_, func, R, W, bias=None, scale=None):
            kw = {}
            if bias is not None:
                kw["bias"] = bias
            if scale is not None:
                kw["scale"] = scale
            P.op("act", lambda e: e.activation(out=out, in_=in_, func=func, **kw), R, W)

        def ACPY(out, in_, R, W):
            P.op("act", lambda e: e.copy(out=out, in_=in_), R, W)

        def VCPY(out, in_, R, W):
            P.op("dve", lambda e: e.tensor_copy(out=out, in_=in_), R, W)

        def TTo(out, a, b, op, R, W):
            P.op("dve", lambda e: e.tensor_tensor(out=out, in0=a, in1=b, op=op), R, W)

        def TS(out, a, s1, s2, op0, op1, R, W):
            if op1 is None:
                P.op("dve", lambda e: e.tensor_scalar(out=out, in0=a, scalar1=s1, scalar2=None, op0=op0), R, W)
            else:
                P.op("dve", lambda e: e.tensor_scalar(out=out, in0=a, scalar1=s1, scalar2=s2, op0=op0, op1=op1), R, W)

        def STT(out, a, s, b, op0, op1, R, W):
            P.op("dve", lambda e: e.scalar_tensor_tensor(out=out, in0=a, scalar=s, in1=b, op0=op0, op1=op1), R, W)

        def PTS(out, a, s1, s2, op0, op1, R, W):
            P.op("pool", lambda e: e.tensor_scalar(out=out, in0=a, scalar1=s1, scalar2=s2, op0=op0, op1=op1), R, W)

        def PSTT(out, a, s, b, op0, op1, R, W):
            P.op("pool", lambda e: e.scalar_tensor_tensor(out=out, in0=a, scalar=s, in1=b, op0=op0, op1=op1), R, W)

        def PCPY(out, in_, R, W):
            P.op("pool", lambda e: e.tensor_copy(out=out, in_=in_), R, W)

        def DMA(eng, key, pairs, R, W):
            fns = []
            for (o, i) in pairs:
                fns.append((lambda o, i: (lambda e: e.dma_start(out=o, in_=i)))(o, i))
            P.dma(eng, fns, key, R, W)

        MUL, ADD, MIN = ALU.mult, ALU.add, ALU.min

        consts = sb("consts", [128, 5 * 128])
        ident = consts.a[:, 0:128]
        onesf = consts.a[:, 128:256]
        triNeg = consts.a[:, 256:384]
        triUpNeg = consts.a[:, 384:512]
        maskT = consts.a[:, 512:640]
        onesb = sb("onesb", [128, 128], BF16)
        lvec = [sb(f"lvec{l}", [128, NV]) for l in range(L)]
        cneg = [sb(f"cneg{l}", [128, 8]) for l in range(L)]
        gwa = [sb(f"gwa{l}", [17, 512]) for l in range(L)]
        rgwa = [sb(f"rgwa{l}", [128, 8, 128], BF16) for l in range(L)]
        rgwx = [sb(f"rgwx{l}", [128, 8, 128], BF16) for l in range(L)]
        modT = [sb(f"modT{l}", [128, 24, 1 + NS]) for l in range(L)]
        Ap = [sb(f"Ap{l}", [128, 8]) for l in range(L)]
        As = [sb(f"As{l}", [128, 8, NS]) for l in range(L)]
        cs = sb("cs", [128, 8, 1 + NS])
        scb = sb("scb", [128, 8, 1 + NS], BF16)

        wslots = Ring([sb(f"ws{i}", [128, 8, 512], BF16) for i in range(3)])
        xt = Ring([sb(f"xt{i}", [128, 8, TT]) for i in range(2)])
        hT = sb("hT", [128, 8, TT], BF16)
        pbr0 = sb("pbr0", [128, 8, TT], BF16)
        raw1 = sb("raw1", [128, 4 * 512])
        raw2 = sb("raw2", [128, 4 * 512])
        pbr = [pbr0,
               Buf(raw1.a.bitcast(BF16).rearrange("p (k n) -> p k n", k=8), raw1.T),
               Buf(raw2.a.bitcast(BF16).rearrange("p (k n) -> p k n", k=8), raw2.T)]
        mg = sb("mg", [128, 8, TT], BF16)
        sqb = sb("sqb", [128, 4, TT], BF16)
        rstd = sb("rstd", [128, TT])
        wk = Ring([sb(f"wk{i}", [128, TT]) for i in range(4)])
        xcd = Ring([sb(f"xcd{i}", [128, TT]) for i in range(2)])
        gbuf = [sb(f"gb{i}", [128, TT]) for i in range(3)]
        zlr = sb("zlr", [17, TT])
        sp_tok = Buf(raw1.a.rearrange("p (c n) -> p c n", c=4), raw1.T)
        ekd = Buf(raw2.a.rearrange("p (c n) -> p c n", c=4), raw2.T)
        eb = [sb(f"eb{h}", [128, 4]) for h in range(4)]
        qk = sb("qk", [128, 8, TT], BF16)
        qe = [Buf(qk.a[:, h, :], qk.T) for h in range(4)]
        ke = [Buf(qk.a[:, 4 + h, :], qk.T) for h in range(4)]
        hvec = [sb(f"hvec{l}", [128, NV]) for l in range(L)]
        hc = [sb(f"hc{l}", [128, 8]) for l in range(L)]
        gqp = [sb(f"gqp{l}", [128, 8]) for l in range(L)]
        gqs = [sb(f"gqs{l}", [128, 8, NS]) for l in range(L)]
        vtok = [Buf(None, mg.T) for h in range(4)]

        def vt(h, c):
            return mg.a[:, 2 * h + c // 2, (c % 2) * 256:(c % 2) * 256 + 256]
        kdtok = [sb(f"kdtok{h}", [128, 4, 128], BF16) for h in range(4)]
        attm = Ring([sb(f"attm{i}", [128, 128], BF16) for i in range(4)])
        oT = [sb(f"oT{h}", [128, 2, TT]) for h in range(4)]
        S = [[sb(f"S{l}_{h}", [128, 256]) for h in range(4)] for l in range(L)]
        Sb = [[sb(f"Sb{l}_{h}", [128, 256], BF16) for h in range(4)] for l in range(L)]
        csc = [sb(f"csc{l}", [128, 8, 2]) for l in range(L)]
        crg = [sb(f"crg{l}", [128, 8, 3]) for l in range(L)]
        chh = [sb(f"chh{l}", [128, 8]) for l in range(L)]
        ub = Ring([sb(f"ub{i}", [128, TT + 3]) for i in range(2)])
        xcb = sb("xcb", [128, TT], BF16)
        xs = sb("xs", [128, 8, NS])
        ssc = sb("ssc", [128, 8, 2, NS])
        srgc = sb("srgc", [128, 8, 3, NS])
        srgh = sb("srgh", [128, 8, NS])
        nsc = sb("nsc", [128, 8, 2, NS])
        nrgc = sb("nrgc", [128, 8, 3, NS])
        nrgh = sb("nrgh", [128, 8, NS])
        aTs = sb("aTs", [128, 4, NS])
        qss = sb("qss", [128, 4, NS], BF16)
        kvs = sb("kvs", [NS, 4, 384], BF16)
        km4 = Ring([Buf(sqb.a[0:NS, i, :].rearrange("p (h d) -> p h d", h=4), Tile(f"km4_{i}")) for i in range(2)])
        Sin = [Buf(oT[i].a.rearrange("p a (b c) -> p (a b) c", b=2), oT[i].T) for i in range(2)]
        Snw = [Buf(oT[i].a.rearrange("p a (b c) -> p (a b) c", b=2), oT[i].T) for i in range(2, 4)]
        snbq = [Buf(qk.a[:, 2 * i:2 * i + 2, :].rearrange("p k (a v) -> p (k a) v", a=2), Tile(f"snbq{i}")) for i in range(2)]
        sgs = sb("sgs", [128, 8, NS])
        snb4 = Ring([Buf(ub.bufs[i].a[:, 0:512].bitcast(BF16).rearrange("p (h v) -> p h v", h=4), ub.bufs[i].T) for i in range(2)])
        oTs = sb("oTs", [128, 8, NS])
        hT_s = sb("hT_s", [128, 8, NS], BF16)
        pbr_s = [sb(f"pbr_s{i}", [128, 8, NS], BF16) for i in range(3)]
        mg_s = sb("mg_s", [128, 8, NS], BF16)
        rsb_s = [sb(f"rsb_s{i}", [128, NS]) for i in range(4)]
        gb_s = [sb(f"gb_s{i}", [128, NS]) for i in range(3)]
        wk_s = Ring([sb(f"wk_s{i}", [128, NS]) for i in range(4)])
        xcd_s = Ring([sb(f"xcd_s{i}", [128, NS]) for i in range(2)])
        xcb_s = sb("xcb_s", [128, NS], BF16)

        zr = Ring([ps(f"z{i}", [128, 512]) for i in range(8)])

        class _SmView:
            def next(self):
                b = zr.next()
                return Buf(b.a[:, 0:256], b.T)
        smr = _SmView()

        def wview(w, l):
            return w[l].rearrange("(k p) n -> p k n", p=128)

        def wload(pieces):
            slot = wslots.next()
            pairs = [(slot.a[:, :, d0:d0 + n], src) for (src, d0, n) in pieces]
            if os.environ.get("KNODMA") and wslots.i > 3:
                return slot
            DMA("pool", slot.T.name, pairs, [], [slot.T])
            return slot

        class WQ:
            def __init__(self):
                self.plan = []
                self.issued = []
                self.pos = 0

            def add(self, pieces):
                self.plan.append(pieces)

            def get(self):
                while len(self.issued) < min(len(self.plan), self.pos + 3):
                    self.issued.append(wload(self.plan[len(self.issued)]))
                s = self.issued[self.pos]
                self.pos += 1
                return s

        wq = WQ()

        def plan_layer(l):
            wi = wview(w_in, l)
            wq.add([(wi[:, :, OLR:OLR + 16], 0, 16)])
            for h in range(4):
                wq.add([(wi[:, :, OQ + 128 * h:OQ + 128 * h + 128], 0, 128),
                        (wi[:, :, OK_ + 128 * h:OK_ + 128 * h + 128], 128, 128),
                        (wi[:, :, OV + 256 * h:OV + 256 * h + 256], 256, 256)])
            for h in range(4):
                wq.add([(wi[:, :, OGG + 256 * h:OGG + 256 * h + 256], 0, 256)])
            for j in range(8):
                wq.add([(wi[:, :, o + 128 * j:o + 128 * j + 128], 128 * q, 128) for q, o in enumerate((OX, OC, OH, OB))])
                wq.add([(wi[:, :, o + 128 * j:o + 128 * j + 128], 128 * q, 128) for q, o in enumerate((OGS, OGR))])
            for i in range(8):
                wq.add([(wi[:, :, OM + 1024 * br + 128 * i:OM + 1024 * br + 128 * i + 128], 128 * br, 128) for br in range(3)])
                wq.add([(wview(wbr[br], l)[:, :, 128 * i:128 * i + 128], 128 * br, 128) for br in range(3)])
            wo = wview(w_out, l)
            for g in range(2):
                wq.add([(wo[:, :, 512 * g:512 * g + 512], 0, 512)])

        for l in range(L):
            wa = wview(w_ada, l)
            for g in range(6):
                wq.add([(wa[:, :, 512 * g:512 * g + 512], 0, 512)])
        for _pass in range(NT):
            for l in range(L):
                plan_layer(l)

        DMA("sp", "consts", [(consts.a, consts_d[:, :])], [], [consts.T])
        for l in range(L):
            DMA("sp", f"lvec{l}", [(lvec[l].a, lvec_d[l])], [], [lvec[l].T])
            DMA("sp", f"gwa{l}", [(gwa[l].a, gwa_d[l])], [], [gwa[l].T])
            DMA("pool", f"rgwa{l}", [(rgwa[l].a, rgwa_d[l].rearrange("n i j -> i n j"))], [], [rgwa[l].T])
            DMA("pool", f"rgwx{l}", [(rgwx[l].a, rgwx_d[l].rearrange("n i j -> i n j"))], [], [rgwx[l].T])
        DMA("sp", "cs", [(cs.a, cT[:, :, :])], [], [cs.T])
        VCPY(onesb.a, onesf, [consts.T], [onesb.T])
        P.op("dve", lambda e: e.memset(zlr.a, 1.0), [], [zlr.T])
        for l in range(L if STAGE >= -2 else 0):
            for h in range(4):
                P.op("dve", (lambda a: (lambda e: e.memset(a, 0.0)))(S[l][h].a), [], [S[l][h].T])
                P.op("dve", (lambda a: (lambda e: e.memset(a, 0.0)))(Sb[l][h].a), [], [Sb[l][h].T])
            P.op("dve", (lambda a: (lambda e: e.memset(a, 0.0)))(csc[l].a), [], [csc[l].T])
            P.op("dve", (lambda a: (lambda e: e.memset(a, 0.0)))(crg[l].a), [], [crg[l].T])
            P.op("dve", (lambda a: (lambda e: e.memset(a, 0.0)))(chh[l].a), [], [chh[l].T])
        ACT(scb.a, cs.a, AF.Silu, [cs.T], [scb.T])
        for l in range(L if STAGE >= -1 else 0):
            lv = lvec[l]
            ACT(cneg[l].a, lv.a[:, VLAM:VLAM + 8], AF.Exp, [lv.T], [cneg[l].T], scale=-1.0)
            ACT(cneg[l].a, cneg[l].a, AF.Ln, [cneg[l].T], [cneg[l].T], bias=1.0)
            TS(cneg[l].a, cneg[l].a, -8.0, None, MUL, None, [cneg[l].T], [cneg[l].T])
            TS(hc[l].a, cneg[l].a, 0.5, None, MUL, None, [cneg[l].T], [hc[l].T])
            TS(hvec[l].a, lv.a, 0.5, None, MUL, None, [lv.T], [hvec[l].T])
            for g in range(6):
                slot = wq.get()
                for cc in range(4):
                    ch = g * 4 + cc
                    z = zr.next()
                    for k in range(8):
                        MM(z.a[:, 0:1 + NS], slot.a[:, k, cc * 128:cc * 128 + 128], scb.a[:, k, :], k == 0, k == 7,
                           [slot.T, scb.T], [z.T])
                    ACT(modT[l].a[:, ch, :], z.a[:, 0:1 + NS], AF.Identity, [z.T, lv.T], [modT[l].T],
                        bias=lv.a[:, VBADA + ch:VBADA + ch + 1])
            TS(Ap[l].a, modT[l].a[:, 8:16, 0], 1.0, None, ADD, None, [modT[l].T], [Ap[l].T])
            TTo(Ap[l].a, Ap[l].a, lv.a[:, VNG:VNG + 8], MUL, [Ap[l].T, lv.T], [Ap[l].T])
            for k in range(8):
                TS(As[l].a[:, k, :], modT[l].a[:, 8 + k, 1:1 + NS], 1.0, lv.a[:, VNG + k:VNG + k + 1], ADD, MUL,
                   [modT[l].T, lv.T], [As[l].T])
            TS(gqp[l].a, modT[l].a[:, 16:24, 0], 0.25, None, MUL, None, [modT[l].T], [gqp[l].T])
            TS(gqs[l].a, modT[l].a[:, 16:24, 1:1 + NS], 0.25, None, MUL, None, [modT[l].T], [gqs[l].T])

        def proj(slot, c0, M, rhs, N, extra_R=()):
            z = zr.next()
            for k in range(8):
                MM(z.a[0:M, 0:N], slot.a[:, k, c0:c0 + M], rhs.a[:, k, 0:N], k == 0, k == 7,
                   [slot.T, rhs.T] + list(extra_R), [z.T])
            return z

        def rms_stats(src_fn, nk, N, scale, Rsrc, rstd=rstd):
            z = zr.next()
            for k0 in range(0, nk, 4):
                for k in range(k0, min(nk, k0 + 4)):
                    if k % 2 == 1:
                        TTo(sqb.a[:, k % 4, 0:N], src_fn(k), src_fn(k), MUL, Rsrc, [sqb.T])
                    else:
                        ACT(sqb.a[:, k % 4, 0:N], src_fn(k), AF.Square, Rsrc, [sqb.T])
                for k in range(k0, min(nk, k0 + 4)):
                    MM(z.a[:, 0:N], onesb.a, sqb.a[:, k % 4, 0:N], k == 0, k == nk - 1, [onesb.T, sqb.T], [z.T])
            ACT(rstd.a[:, 0:N], z.a[:, 0:N], AF.Ln, [z.T], [rstd.T], bias=EPS, scale=scale)
            ACT(rstd.a[:, 0:N], rstd.a[:, 0:N], AF.Exp, [rstd.T], [rstd.T], scale=-0.5)

        class Ctx:
            pass

        def block(l, cx):
            x, N, samp, tile_idx = cx.x, cx.N, cx.samp, cx.tile_idx
            hT, pbr, mg, gbuf, wk, xcd, xcb = cx.hT, cx.pbr, cx.mg, cx.gb, cx.wk, cx.xcd, cx.xcb
            lv = lvec[l]
            md = modT[l]
            last_tile = (not samp) and tile_idx == NT - 1
            PB = int(os.environ.get("KPB", "99")) if not samp else 99
            rms_stats(lambda k: x.a[:, k, 0:N], 8, N, 1.0 / D, [x.T])
            for k in range(8):
                t = wk.next()
                if not samp:
                    STT(t.a[:, 0:N], x.a[:, k, 0:N], Ap[l].a[:, k:k + 1], rstd.a[:, 0:N], MUL, MUL,
                        [x.T, Ap[l].T, rstd.T], [t.T])
                    ACT(hT.a[:, k, 0:N], t.a[:, 0:N], AF.Identity, [t.T, md.T], [hT.T], bias=md.a[:, k, 0:1])
                else:
                    TTo(t.a[:, 0:N], x.a[:, k, 0:N], rstd.a[:, 0:N], MUL, [x.T, rstd.T], [t.T])
                    TTo(t.a[:, 0:N], t.a[:, 0:N], As[l].a[:, k, :], MUL, [t.T, As[l].T], [t.T])
                    TTo(hT.a[:, k, 0:N], t.a[:, 0:N], md.a[:, k, 1:1 + NS], ADD, [t.T, md.T], [hT.T])

            if PB <= 1:
                return
            slot = yield
            z = proj(slot, 0, 16, hT, N)
            ACPY(zlr.a[0:16, 0:N], z.a[0:16, 0:N], [z.T], [zlr.T])
            if not samp:
                for c in range(4):
                    z = zr.next()
                    MM(z.a[:, :], zlr.a[0:17, c * 128:c * 128 + 128], gwa[l].a[0:17, :], True, True, [zlr.T, gwa[l].T], [z.T])
                    ACT(sp_tok.a[:, c, :], z.a[:, :], AF.Exp, [z.T], [sp_tok.T], scale=-1.0)
                    ACT(sp_tok.a[:, c, :], sp_tok.a[:, c, :], AF.Ln, [sp_tok.T], [sp_tok.T], bias=1.0)
                for c in range(4):
                    z = zr.next()
                    MM(z.a[:, :], triUpNeg, sp_tok.a[:, c, :], True, True, [consts.T, sp_tok.T], [z.T])
                    ACT(ekd.a[:, c, :], z.a[:, :], AF.Exp, [z.T], [ekd.T])
            else:
                z = zr.next()
                for h in range(4):
                    MM(z.a[:, h * NS:(h + 1) * NS], gwa[l].a[0:17, h * 128:h * 128 + 128], zlr.a[0:17, 0:N], True, True,
                       [zlr.T, gwa[l].T], [z.T])
                av = aTs.a.rearrange("p h s -> p (h s)")
                ACT(av, z.a[:, 0:4 * NS], AF.Exp, [z.T], [aTs.T], scale=-1.0)
                ACT(av, av, AF.Ln, [aTs.T], [aTs.T], bias=1.0)
                ACT(av, av, AF.Exp, [aTs.T], [aTs.T], scale=-1.0 / 16.0)

            if PB <= 2:
                return
            for h in range(4):
                slot = yield
                if not samp:
                    zb = zr.next()
                    for c in range(4):
                        MM(zb.a[:, c * 128:c * 128 + 128], sp_tok.a[:, c, h * 128:h * 128 + 128], triNeg, True, True,
                           [sp_tok.T, consts.T], [zb.T])
                    E2 = wk.next()
                    E1 = wk.next()
                    ACT(E1.a, zb.a, AF.Exp, [zb.T], [E1.T])
                    ACT(E2.a, zb.a, AF.Exp, [zb.T], [E2.T], scale=-1.0)
                    ACPY(eb[h].a, E1.a.rearrange("p (c n) -> p c n", c=4)[:, :, 127], [E1.T], [eb[h].T])
                    KA = int(os.environ.get("KA", "9"))
                    if KA <= 1:
                        continue
                    zq = proj(slot, 0, 128, hT, N)
                    STT(qe[h].a, zq.a, float(128 ** -0.5), E1.a, MUL, MUL, [zq.T, E1.T], [qe[h].T])
                    zk = proj(slot, 128, 128, hT, N)
                    TTo(ke[h].a, zk.a, E2.a, MUL, [zk.T, E2.T], [ke[h].T])
                    if KA <= 2:
                        continue
                    for c in range(4):
                        z = zr.next()
                        for k in range(8):
                            MM(z.a[:, 0:384], hT.a[:, k, c * 128:c * 128 + 128], slot.a[:, k, 128:512], k == 0, k == 7,
                               [hT.T, slot.T], [z.T])
                        if KA <= 3:
                            continue
                        ACPY(vt(h, c), z.a[:, 128:384], [z.T], [vtok[h].T])
                        if KA <= 4:
                            continue
                        TTo(kdtok[h].a[:, c, :], z.a[:, 0:128], ekd.a[:, c, h * 128:h * 128 + 128], MUL, [z.T, ekd.T, vtok[h].T], [kdtok[h].T])
                else:
                    zq = proj(slot, 0, 128, hT, N)
                    TS(qss.a[:, h, :], zq.a[:, 0:N], float(128 ** -0.5), None, MUL, None, [zq.T], [qss.T])
                    z = zr.next()
                    for k in range(8):
                        MM(z.a[0:NS, 0:384], hT.a[:, k, 0:N], slot.a[:, k, 128:512], k == 0, k == 7, [hT.T, slot.T], [z.T])
                    ACPY(kvs.a[:, h, :], z.a[0:NS, 0:384], [z.T], [kvs.T])

            if PB <= 3:
                return
            if not samp:
                for c in range(4):
                    cs_ = slice(c * 128, c * 128 + 128)
                    sa4 = zr.next()
                    for h in range(4):
                        MM(sa4.a[:, h * 128:h * 128 + 128], ke[h].a[:, cs_], qe[h].a[:, cs_], True, True, [qk.T], [sa4.T])
                    ams = []
                    for h in range(4):
                        am = attm.next()
                        TTo(am.a, sa4.a[:, h * 128:h * 128 + 128], maskT, MUL, [sa4.T, consts.T], [am.T])
                        ams.append(am)
                    sos = [zr.next(), zr.next()]
                    for h in range(4):
                        so = sos[h // 2]
                        for vc in range(2):
                            col = (h % 2) * 256 + vc * 128
                            MM(so.a[:, col:col + 128], vt(h, c)[:, vc * 128:vc * 128 + 128], ams[h].a, True, False,
                               [vtok[h].T, ams[h].T], [so.T])
                            MM(so.a[:, col:col + 128], Sb[l][h].a[:, vc * 128:vc * 128 + 128], qe[h].a[:, cs_], False, True,
                               [Sb[l][h].T, qk.T], [so.T])
                    sus = [zr.next(), zr.next()]
                    for h in range(4):
                        su = sus[h // 2]
                        MM(su.a[:, (h % 2) * 256:(h % 2) * 256 + 256], kdtok[h].a[:, c, :], vt(h, c), True, True,
                           [kdtok[h].T, vtok[h].T], [su.T])
                    for h in range(4):
                        so = sos[h // 2]
                        ACPY(oT[h].a[:, :, cs_], so.a[:, (h % 2) * 256:(h % 2) * 256 + 256].rearrange("p (a b) -> p a b", a=2),
                             [so.T], [oT[h].T])
                    for h in range(4):
                        su = sus[h // 2]
                        STT(S[l][h].a, S[l][h].a, eb[h].a[:, c:c + 1], su.a[:, (h % 2) * 256:(h % 2) * 256 + 256], MUL, ADD,
                            [S[l][h].T, eb[h].T, su.T], [S[l][h].T])
                        ACPY(Sb[l][h].a, S[l][h].a, [S[l][h].T], [Sb[l][h].T])
                if last_tile:
                    for h in range(4):
                        DMA("sp", f"S{l}_{h}", [(ogla_p[l, h], S[l][h].a)], [S[l][h].T], [])
            else:
                pass

            def s_A(s_):
                si = Sin[s_ % 2]
                DMA("sp", "ld_" + si.T.name + "s", [(si.a, sgla[l, s_].rearrange("h d v -> d h v"))], [], [si.T])
                km = km4.bufs[s_ % 2]
                TS(km.a, kvs.a[:, :, 0:128], ident[0:NS, s_:s_ + 1], None, MUL, None, [kvs.T, consts.T, sqb.T], [km.T])

            def s_B(s_):
                si, sn, km, sbf = Sin[s_ % 2], Snw[s_ % 2], km4.bufs[s_ % 2], snbq[s_ % 2]
                sus = [zr.next(), zr.next()]
                for h in range(4):
                    su = sus[h // 2]
                    MM(su.a[:, (h % 2) * 256:(h % 2) * 256 + 256], km.a[:, h, :], kvs.a[:, h, 128:384], True, True,
                       [km.T, kvs.T, sqb.T], [su.T])
                for h in range(4):
                    su = sus[h // 2]
                    STT(sn.a[:, h, :], si.a[:, h, :], aTs.a[:, h, s_:s_ + 1], su.a[:, (h % 2) * 256:(h % 2) * 256 + 256], MUL, ADD,
                        [si.T, aTs.T, su.T], [sn.T])
                ACPY(sbf.a, sn.a, [sn.T, qk.T], [sbf.T])
                DMA("sp", "st_" + sn.T.name + "s", [(ogla_s[l, s_].rearrange("h d v -> d h v"), sn.a)], [sn.T], [])

            def s_C(s_):
                sbf = snbq[s_ % 2]
                so = zr.next()
                for h in range(4):
                    for vc in range(2):
                        MM(so.a[:, 2 * h + vc:2 * h + vc + 1], sbf.a[:, h, vc * 128:vc * 128 + 128], qss.a[:, h, s_:s_ + 1], True, True,
                           [sbf.T, qss.T, qk.T], [so.T])
                ACPY(oTs.a[:, :, s_:s_ + 1], so.a[:, 0:8].rearrange("p (a b) -> p a b", b=1), [so.T], [oTs.T])

            def sstep(n):
                if n == 0:
                    ACPY(qk.a[0:1, 0, 0:1], qk.a[0:1, 0, 0:1], [], [qk.T])
                    ACPY(sqb.a[0:1, 3, 0:1], sqb.a[0:1, 3, 0:1], [], [sqb.T])
                    s_A(0)
                if n + 1 < NS:
                    s_A(n + 1)
                if 0 <= n - 1 < NS:
                    s_C(n - 1)
                if n < NS:
                    s_B(n)

            if PB <= 4:
                return
            rsb = cx.rsb
            for h in range(4):
                if not samp:
                    rms_stats(lambda vc: oT[h].a[:, vc, :], 2, N, 1.0 / 256, [oT[h].T], rsb[h])
            for h in range(4):
                slot = yield
                for vc in range(2):
                    zg = proj(slot, vc * 128, 128, hT, N)
                    if samp:
                        ACT(sgs.a[:, 2 * h + vc, :], zg.a[:, 0:N], AF.Tanh, [zg.T], [sgs.T], scale=0.5)
                        STT(sgs.a[:, 2 * h + vc, :], sgs.a[:, 2 * h + vc, :], 1.0, zg.a[:, 0:N], ADD, MUL, [sgs.T, zg.T], [sgs.T])
                        continue
                    sg = wk.next()
                    ACT(sg.a[:, 0:N], zg.a[:, 0:N], AF.Tanh, [zg.T], [sg.T], scale=0.5)
                    STT(sg.a[:, 0:N], sg.a[:, 0:N], 1.0, zg.a[:, 0:N], ADD, MUL, [sg.T, zg.T], [sg.T])
                    t1 = wk.next()
                    osrc = oT[h].a[:, vc, :] if not samp else oTs.a[:, 2 * h + vc, :]
                    oTt = oT[h].T if not samp else oTs.T
                    STT(t1.a[:, 0:N], osrc, lv.a[:, VGNG + vc:VGNG + vc + 1], rsb[h].a[:, 0:N], MUL, MUL,
                        [oTt, lv.T, rsb[h].T], [t1.T])
                    TTo(pbr[0].a[:, 2 * h + vc, 0:N], t1.a[:, 0:N], sg.a[:, 0:N], MUL, [t1.T, sg.T], [pbr[0].T])

            if PB <= 5:
                return
            hv = hvec[l]
            def rg2_tail(j, zrr, zi, zg, xc):
                a_ = wk.next()
                ACT(a_.a[:, 0:N], zrr.a[:, 0:N], AF.Tanh, [zrr.T, hv.T], [a_.T], bias=hv.a[:, VRGBA + j:VRGBA + j + 1], scale=0.5)
                ig = wk.next()
                ACT(ig.a[:, 0:N], zi.a[:, 0:N], AF.Tanh, [zi.T, hv.T], [ig.T], bias=hv.a[:, VRGBX + j:VRGBX + j + 1], scale=0.5)
                sg = wk.next()
                ACT(sg.a[:, 0:N], zg.a[:, 0:N], AF.Tanh, [zg.T], [sg.T], scale=0.5)
                ACT(a_.a[:, 0:N], a_.a[:, 0:N], AF.Exp, [a_.T, hc[l].T], [a_.T], bias=hc[l].a[:, j:j + 1], scale=hc[l].a[:, j:j + 1])
                sq_ = wk.next()
                STT(sq_.a[:, 0:N], a_.a[:, 0:N], 0.9999999, a_.a[:, 0:N], MIN, MUL, [a_.T], [sq_.T])
                ACT(sq_.a[:, 0:N], sq_.a[:, 0:N], AF.Sqrt, [sq_.T], [sq_.T], bias=0.25, scale=-0.25)
                STT(ig.a[:, 0:N], ig.a[:, 0:N], 1.0, xc.a[:, 0:N], ADD, MUL, [ig.T, xc.T], [ig.T])
                STT(sg.a[:, 0:N], sg.a[:, 0:N], 1.0, zg.a[:, 0:N], ADD, MUL, [sg.T, zg.T], [sg.T])
                TTo(ig.a[:, 0:N], ig.a[:, 0:N], sq_.a[:, 0:N], MUL, [ig.T, sq_.T], [ig.T])
                if not samp:
                    P.op("dve", (lambda o, d0, d1, ini: (lambda e: e.tensor_tensor_scan(out=o, data0=d0, data1=d1, initial=ini,
                                                                                        op0=MUL, op1=ADD)))(
                        xc.a[:, 0:N], a_.a[:, 0:N], ig.a[:, 0:N], chh[l].a[:, j:j + 1]),
                        [a_.T, ig.T, chh[l].T], [xc.T])
                    VCPY(chh[l].a[:, j:j + 1], xc.a[:, N - 1:N], [xc.T], [chh[l].T])
                    hsrc = xc.a[:, 0:N]
                    hT_ = xc.T
                else:
                    TTo(a_.a[:, 0:N], a_.a[:, 0:N], srgh.a[:, j, :], MUL, [a_.T, srgh.T], [a_.T])
                    TTo(nrgh.a[:, j, :], a_.a[:, 0:N], ig.a[:, 0:N], ADD, [a_.T, ig.T], [nrgh.T])
                    hsrc = nrgh.a[:, j, :]
                    hT_ = nrgh.T
                TTo(pbr[2].a[:, j, 0:N], hsrc, sg.a[:, 0:N], MUL, [hT_, sg.T], [pbr[2].T])

            pending = None
            for j in range(8):
                slotA = yield
                if samp:
                    sstep(2 * j)
                cw = [lv.a[:, VRGW + 4 * j + r:VRGW + 4 * j + r + 1] for r in range(4)]
                cb = lv.a[:, VRGCB + j:VRGCB + j + 1]
                zx = proj(slotA, 0, 128, hT, N)
                xc = xcd.next()
                if not samp:
                    ur = ub.next()
                    ACPY(ur.a[:, 3:3 + N], zx.a[:, 0:N], [zx.T], [ur.T])
                    ACPY(ur.a[:, 0:3], crg[l].a[:, j, :], [crg[l].T], [ur.T])
                    ACPY(crg[l].a[:, j, :], ur.a[:, N:N + 3], [ur.T], [crg[l].T])
                    ACT(xc.a[:, 0:N], ur.a[:, 0:N], AF.Identity, [ur.T, lv.T], [xc.T], bias=cb, scale=cw[0])
                    for r in range(1, 4):
                        STT(xc.a[:, 0:N], ur.a[:, r:r + N], cw[r], xc.a[:, 0:N], MUL, ADD, [ur.T, lv.T, xc.T], [xc.T])
                else:
                    ACPY(nrgc.a[:, j, 2, :], zx.a[:, 0:N], [zx.T], [nrgc.T])
                    VCPY(nrgc.a[:, j, 0:2, :], srgc.a[:, j, 1:3, :], [srgc.T], [nrgc.T])
                    TS(xc.a[:, 0:N], srgc.a[:, j, 0, :], cw[0], cb, MUL, ADD, [srgc.T, lv.T], [xc.T])
                    for r in range(1, 3):
                        STT(xc.a[:, 0:N], srgc.a[:, j, r, :], cw[r], xc.a[:, 0:N], MUL, ADD, [srgc.T, lv.T, xc.T], [xc.T])
                    STT(xc.a[:, 0:N], nrgc.a[:, j, 2, :], cw[3], xc.a[:, 0:N], MUL, ADD, [nrgc.T, lv.T, xc.T], [xc.T])
                if pending is not None:
                    rg2_tail(*pending)
                    pending = None
                ACPY(xcb.a[:, 0:N], xc.a[:, 0:N], [xc.T], [xcb.T])

                slot = slotA
                w0 = lv.a[:, VSCW + 3 * j + 0:VSCW + 3 * j + 1]
                w1 = lv.a[:, VSCW + 3 * j + 1:VSCW + 3 * j + 2]
                w2 = lv.a[:, VSCW + 3 * j + 2:VSCW + 3 * j + 3]
                zc = proj(slot, 128, 128, hT, N)
                tcb = wk.next()
                ACPY(tcb.a[:, 0:N], zc.a[:, 0:N], [zc.T], [tcb.T])
                zh = proj(slot, 256, 128, hT, N)
                uc = wk.next()
                if not samp:
                    u = ub.next()
                    TTo(u.a[:, 2:2 + N], zh.a[:, 0:N], tcb.a[:, 0:N], MUL, [zh.T, tcb.T], [u.T])
                    ACPY(u.a[:, 0:2], csc[l].a[:, j, :], [csc[l].T], [u.T])
                    ACPY(csc[l].a[:, j, :], u.a[:, N:N + 2], [u.T], [csc[l].T])
                    ACT(uc.a[:, 0:N], u.a[:, 0:N], AF.Identity, [u.T, lv.T], [uc.T], scale=w0)
                    STT(uc.a[:, 0:N], u.a[:, 1:1 + N], w1, uc.a[:, 0:N], MUL, ADD, [u.T, lv.T, uc.T], [uc.T])
                    STT(uc.a[:, 0:N], u.a[:, 2:2 + N], w2, uc.a[:, 0:N], MUL, ADD, [u.T, lv.T, uc.T], [uc.T])
                else:
                    TTo(nsc.a[:, j, 1, :], zh.a[:, 0:N], tcb.a[:, 0:N], MUL, [zh.T, tcb.T], [nsc.T])
                    VCPY(nsc.a[:, j, 0, :], ssc.a[:, j, 1, :], [ssc.T], [nsc.T])
                    TS(uc.a[:, 0:N], ssc.a[:, j, 0, :], w0, None, MUL, None, [ssc.T, lv.T], [uc.T])
                    STT(uc.a[:, 0:N], ssc.a[:, j, 1, :], w1, uc.a[:, 0:N], MUL, ADD, [ssc.T, lv.T, uc.T], [uc.T])
                    STT(uc.a[:, 0:N], nsc.a[:, j, 1, :], w2, uc.a[:, 0:N], MUL, ADD, [nsc.T, lv.T, uc.T], [uc.T])
                zb_ = proj(slot, 384, 128, hT, N)
                TTo(uc.a[:, 0:N], zb_.a[:, 0:N], uc.a[:, 0:N], MUL, [zb_.T, uc.T], [uc.T])
                slotB = yield
                if samp:
                    sstep(2 * j + 1)
                zg = proj(slotB, 0, 128, hT, N)
                sg = wk.next()
                ACT(sg.a[:, 0:N], zg.a[:, 0:N], AF.Tanh, [zg.T], [sg.T], scale=0.5)
                STT(sg.a[:, 0:N], sg.a[:, 0:N], 1.0, zg.a[:, 0:N], ADD, MUL, [sg.T, zg.T], [sg.T])
                TTo(pbr[1].a[:, j, 0:N], uc.a[:, 0:N], sg.a[:, 0:N], MUL, [uc.T, sg.T], [pbr[1].T])

                slot = slotB
                zrr = zr.next()
                MM(zrr.a[:, 0:N], rgwa[l].a[:, j, :], xcb.a[:, 0:N], True, True, [rgwa[l].T, xcb.T], [zrr.T])
                zi = zr.next()
                MM(zi.a[:, 0:N], rgwx[l].a[:, j, :], xcb.a[:, 0:N], True, True, [rgwx[l].T, xcb.T], [zi.T])
                zg = proj(slot, 128, 128, hT, N)
                if samp or not cx.defer:
                    rg2_tail(j, zrr, zi, zg, xc)
                else:
                    pending = (j, zrr, zi, zg, xc)
            if pending is not None:
                rg2_tail(*pending)
                pending = None
            if samp:
                sstep(NS)
                for h in range(4):
                    rms_stats(lambda vc: oTs.a[:, 2 * h + vc, :], 2, N, 1.0 / 256, [oTs.T], rsb[h])
                for h in range(4):
                    for vc in range(2):
                        t1 = wk.next()
                        STT(t1.a[:, 0:N], oTs.a[:, 2 * h + vc, :], lv.a[:, VGNG + vc:VGNG + vc + 1], rsb[h].a[:, 0:N], MUL, MUL,
                            [oTs.T, lv.T, rsb[h].T], [t1.T])
                        TTo(pbr[0].a[:, 2 * h + vc, 0:N], t1.a[:, 0:N], sgs.a[:, 2 * h + vc, :], MUL, [t1.T, sgs.T], [pbr[0].T])
            if last_tile:
                DMA("sp", f"csc{l}", [(osc_p[l], csc[l].a)], [csc[l].T], [])
                DMA("sp", f"crg{l}", [(orgc_p[l], crg[l].a)], [crg[l].T], [])
                DMA("sp", f"chh{l}", [(orgh_p[l], chh[l].a)], [chh[l].T], [])
            if samp:
                DMA("sp", "nsc", [(osc_s[l], nsc.a)], [nsc.T], [])
                DMA("sp", "nrgc", [(orgc_s[l], nrgc.a)], [nrgc.T], [])
                DMA("sp", "nrgh", [(orgh_s[l], nrgh.a)], [nrgh.T], [])

            if PB <= 7:
                return
            for i in range(8):
                s1 = yield
                for br in range(3):
                    zm = proj(s1, 128 * br, 128, hT, N)
                    ACT(gbuf[br].a[:, 0:N], zm.a[:, 0:N], AF.Tanh, [zm.T, hv.T], [gbuf[br].T],
                        bias=hv.a[:, VBM + 8 * br + i:VBM + 8 * br + i + 1], scale=0.5)
                acc = wk.next()
                s2 = yield
                for br in range(3):
                    zy = proj(s2, 128 * br, 128, pbr[br], N)
                    if br == 0:
                        STT(acc.a[:, 0:N], gbuf[0].a[:, 0:N], 1.0, zy.a[:, 0:N], ADD, MUL, [zy.T, gbuf[0].T], [acc.T])
                    else:
                        STT(gbuf[br].a[:, 0:N], gbuf[br].a[:, 0:N], 1.0, zy.a[:, 0:N], ADD, MUL, [zy.T, gbuf[br].T], [gbuf[br].T])
                        if br == 1:
                            TTo(acc.a[:, 0:N], acc.a[:, 0:N], gbuf[1].a[:, 0:N], ADD, [acc.T, gbuf[1].T], [acc.T])
                        else:
                            TTo(mg.a[:, i, 0:N], acc.a[:, 0:N], gbuf[2].a[:, 0:N], ADD, [acc.T, gbuf[2].T], [mg.T])

            for i in range(8):
                if i % 4 == 0:
                    so_ = yield
                zo = proj(so_, (i % 4) * 128, 128, mg, N)
                if not samp:
                    STT(x.a[:, i, 0:N], zo.a[:, 0:N], gqp[l].a[:, i:i + 1], x.a[:, i, 0:N], MUL, ADD, [zo.T, gqp[l].T, x.T], [x.T])
                else:
                    t = wk.next()
                    TTo(t.a[:, 0:N], zo.a[:, 0:N], gqs[l].a[:, i, :], MUL, [zo.T, gqs[l].T], [t.T])
                    TTo(x.a[:, i, 0:N], x.a[:, i, 0:N], t.a[:, 0:N], ADD, [x.T, t.T], [x.T])

        def final_norm(x, N, dst, dst_dram, key):
            rms_stats(lambda k: x.a[:, k, 0:N], 8, N, 1.0 / D, [x.T])
            for k in range(8):
                STT(dst.a[:, k, 0:N], x.a[:, k, 0:N], lvec[0].a[:, VFG + k:VFG + k + 1], rstd.a[:, 0:N], MUL, MUL,
                    [x.T, lvec[0].T, rstd.T], [dst.T])
            DMA("sp", key, [(dst_dram, dst.a[:, :, 0:N])], [dst.T], [])

        def run_blocks(l, ctxs):
            gens = [block(l, c) for c in ctxs]
            active = []
            for g in gens:
                try:
                    next(g)
                    active.append(g)
                except StopIteration:
                    pass
            while active:
                slot = wq.get()
                nxt = []
                for g in active:
                    try:
                        g.send(slot)
                        nxt.append(g)
                    except StopIteration:
                        pass
                active = nxt

        cs_ctx = Ctx()
        cs_ctx.x, cs_ctx.N, cs_ctx.samp, cs_ctx.tile_idx = xs, NS, True, 0
        cs_ctx.hT, cs_ctx.pbr, cs_ctx.mg, cs_ctx.gb, cs_ctx.wk, cs_ctx.xcd, cs_ctx.xcb, cs_ctx.rsb = (
            hT_s, pbr_s, mg_s, gb_s, wk_s, xcd_s, xcb_s, rsb_s)

        def prompt_ctx(xbuf, tt):
            c = Ctx()
            c.x, c.N, c.samp, c.tile_idx = xbuf, TT, False, tt
            c.hT, c.pbr, c.mg, c.gb, c.wk, c.xcd, c.xcb, c.rsb = hT, pbr, mg, gbuf, wk, xcd, xcb, [rstd, gbuf[0], gbuf[1], gbuf[2]]
            return c

        DMA("sp", "xs", [(xs.a, xsT[:, :, :])], [], [xs.T])
        xcur = xt.next()
        DMA("sp", xcur.T.name, [(xcur.a, xT[:, :, 0:TT])], [], [xcur.T])
        for tt in range(NT):
            if tt + 1 < NT:
                xnext = xt.next()
                DMA("sp", xnext.T.name, [(xnext.a, xT[:, :, (tt + 1) * TT:(tt + 2) * TT])], [], [xnext.T])
            for l in range(L):
                ctxs = [prompt_ctx(xcur, tt)]
                if tt == 0:
                    DMA("sp", "ssc", [(ssc.a, sscf[l])], [], [ssc.T])
                    DMA("sp", "srgc", [(srgc.a, srgcf[l])], [], [srgc.T])
                    DMA("sp", "srgh", [(srgh.a, srghf[l])], [], [srgh.T])
                    ctxs.append(cs_ctx)
                for c_ in ctxs:
                    c_.defer = (len(ctxs) == 1)
                run_blocks(l, ctxs)
            final_norm(xcur, TT, xcur, yT[:, :, tt * TT:(tt + 1) * TT], xcur.T.name)
            if tt == 0:
                final_norm(xs, NS, xs, ysT[:, :, :], "xs")
            if tt + 1 < NT:
                xcur = xnext
        assert STAGE < 9 or wq.pos == len(wq.plan), (wq.pos, len(wq.plan))
        outs = [b.T for b in xt.bufs] + [xs.T, nsc.T, nrgc.T, nrgh.T] + [b.T for b in Snw]
        for l in range(L):
            outs += [csc[l].T, crg[l].T, chh[l].T] + [S[l][h].T for h in range(4)]
        P.wait_all("sp", outs)
        P.emit(st)
    return nc


_CACHE = {}


def _fm(v):
    return np.ascontiguousarray(np.swapaxes(v.reshape(v.shape[:-1] + (8, 128)), -1, -2))


def kernel(x_prompt, x_sample, c_prompt, c_sample, state_gla, state_sc_conv, state_rg_conv, state_rg_h,
           w_ada, b_ada, norm_gain, w_in, gla_w_alpha, gla_b_alpha, gla_norm_gain, gla_w_branch, sc_conv_w,
           sc_w_branch, rg_conv_w, rg_conv_b, rg_w_a, rg_b_a, rg_w_x, rg_b_x, rg_lambda, rg_w_branch,
           b_merge, w_out, final_gain):
    f = lambda a: np.ascontiguousarray(np.asarray(a, dtype=np.float32))
    x_prompt, x_sample, c_prompt, c_sample = f(x_prompt), f(x_sample), f(c_prompt), f(c_sample)
    state_gla, state_sc_conv, state_rg_conv, state_rg_h = f(state_gla), f(state_sc_conv), f(state_rg_conv), f(state_rg_h)
    n = 8
    if "nc" not in _CACHE:
        _CACHE["nc"] = build_program()
    nc = _CACHE["nc"]
    lvec = np.zeros((L, 128, NV), np.float32)
    for l in range(L):
        lvec[l, :, VNG:VNG + 8] = _fm(f(norm_gain)[l])
        lvec[l, :, VBADA:VBADA + 24] = f(b_ada)[l].reshape(24, 128).T
        lvec[l, :, VGNG:VGNG + 2] = f(gla_norm_gain)[l].reshape(2, 128).T
        lvec[l, :, VSCW:VSCW + 24] = f(sc_conv_w)[l].reshape(3, 8, 128).transpose(2, 1, 0).reshape(128, 24)
        lvec[l, :, VRGW:VRGW + 32] = f(rg_conv_w)[l].reshape(4, 8, 128).transpose(2, 1, 0).reshape(128, 32)
        lvec[l, :, VRGCB:VRGCB + 8] = _fm(f(rg_conv_b)[l])
        lvec[l, :, VRGBA:VRGBA + 8] = _fm(f(rg_b_a)[l])
        lvec[l, :, VRGBX:VRGBX + 8] = _fm(f(rg_b_x)[l])
        lvec[l, :, VLAM:VLAM + 8] = _fm(f(rg_lambda)[l])
        lvec[l, :, VBM:VBM + 24] = f(b_merge)[l].reshape(24, 128).T
        lvec[l, :, VFG:VFG + 8] = _fm(f(final_gain))
    gwa = np.concatenate([f(gla_w_alpha), f(gla_b_alpha)[:, None, :]], axis=1)
    s_idx = np.arange(128)[:, None]
    t_idx = np.arange(128)[None, :]
    consts = np.concatenate([
        np.eye(128, dtype=np.float32),
        np.ones((128, 128), np.float32),
        np.where(s_idx <= t_idx, -1.0 / 16.0, 0.0).astype(np.float32),
        np.where(s_idx > t_idx, -1.0 / 16.0, 0.0).astype(np.float32),
        np.where(s_idx <= t_idx, 1.0, 0.0).astype(np.float32),
    ], axis=1)
    shared = {
        "w_ada": f(w_ada), "w_in": f(w_in), "wgla": f(gla_w_branch), "wsc": f(sc_w_branch), "wrg": f(rg_w_branch),
        "w_out": f(w_out), "rgwa": f(rg_w_a), "rgwx": f(rg_w_x), "gwa": np.ascontiguousarray(gwa),
        "lvec": lvec, "consts": np.ascontiguousarray(consts),
    }
    in_maps = []
    for c in range(n):
        sl = slice(c * NS, (c + 1) * NS)
        m = dict(shared)
        m["xT"] = np.ascontiguousarray(x_prompt[c].reshape(2048, 8, 128).transpose(2, 1, 0))
        m["xsT"] = np.ascontiguousarray(x_sample[sl, 0, :].reshape(NS, 8, 128).transpose(2, 1, 0))
        cc = np.concatenate([c_prompt[c:c + 1], c_sample[sl]], axis=0)
        m["cT"] = np.ascontiguousarray(cc.reshape(1 + NS, 8, 128).transpose(2, 1, 0))
        m["sgla"] = np.ascontiguousarray(state_gla[:, sl])
        m["sscf"] = np.ascontiguousarray(state_sc_conv[:, sl].reshape(L, NS, 2, 8, 128).transpose(0, 4, 3, 2, 1))
        m["srgcf"] = np.ascontiguousarray(state_rg_conv[:, sl].reshape(L, NS, 3, 8, 128).transpose(0, 4, 3, 2, 1))
        m["srghf"] = np.ascontiguousarray(state_rg_h[:, sl].reshape(L, NS, 8, 128).transpose(0, 3, 2, 1))
        in_maps.append(m)
    if os.environ.get("KCORES"):
        n1 = int(os.environ["KCORES"])
        res = run_bass_kernel_spmd(nc, in_maps[:n1], core_ids=list(range(n1)), trace=bool(os.environ.get("KTRACE")))
        return res
    res = run_bass_kernel_spmd(nc, in_maps, core_ids=list(range(n)))
    R = res.results
    B = 8
    y_prompt = np.stack([R[c]["yT"].transpose(2, 1, 0).reshape(2048, D) for c in range(n)], axis=0)
    y_sample = np.concatenate([R[c]["ysT"].transpose(2, 1, 0).reshape(NS, 1, D) for c in range(n)], axis=0)
    gla_p = np.stack([R[c]["ogla_p"] for c in range(n)], axis=1)
    sc_p = np.stack([R[c]["osc_p"].transpose(0, 3, 2, 1).reshape(L, 2, D) for c in range(n)], axis=1)
    rgc_p = np.stack([R[c]["orgc_p"].transpose(0, 3, 2, 1).reshape(L, 3, D) for c in range(n)], axis=1)
    rgh_p = np.stack([R[c]["orgh_p"].transpose(0, 2, 1).reshape(L, D) for c in range(n)], axis=1)
    gla_s = np.concatenate([R[c]["ogla_s"] for c in range(n)], axis=1)
    sc_s = np.concatenate([R[c]["osc_s"].transpose(0, 4, 3, 2, 1).reshape(L, NS, 2, D) for c in range(n)], axis=1)
    rgc_s = np.concatenate([R[c]["orgc_s"].transpose(0, 4, 3, 2, 1).reshape(L, NS, 3, D) for c in range(n)], axis=1)
    rgh_s = np.concatenate([R[c]["orgh_s"].transpose(0, 3, 2, 1).reshape(L, NS, D) for c in range(n)], axis=1)
    outs = (y_prompt, y_sample, gla_p, sc_p, rgc_p, rgh_p, gla_s, sc_s, rgc_s, rgh_s)
    return tuple(np.ascontiguousarray(o.astype(np.float32)) for o in outs)
```

```python
from contextlib import ExitStack
import os
import numpy as np
import concourse.bass as bass
import concourse.mybir as mybir
from concourse.bass_utils import run_bass_kernel_spmd

F32 = mybir.dt.float32
BF16 = mybir.dt.bfloat16
AF = mybir.ActivationFunctionType
ALU = mybir.AluOpType

D = 1024
L = 2
TT = 512
NT = 4
NS = 16
NV = 160
N_IN = 12304
OQ, OK_, OV, OGG, OLR = 0, 512, 1024, 2048, 3072
OB, OC, OH, OGS, OX, OGR, OM = 3088, 4112, 5136, 6160, 7184, 8208, 9232
VNG, VBADA, VGNG, VSCW, VRGW, VRGCB, VRGBA, VRGBX, VLAM, VBM, VFG = 0, 8, 32, 34, 58, 90, 98, 106, 114, 122, 146
EPS = 1e-6


class Tile:
    __slots__ = ("name", "w", "r")

    def __init__(self, name):
        self.name = name
        self.w = None
        self.r = []


class Buf:
    __slots__ = ("a", "T")

    def __init__(self, a, T):
        self.a = a
        self.T = T


class Prog:
    ENGS = ("pe", "act", "dve", "pool", "sp")

    def __init__(self, nc):
        self.nc = nc
        self.streams = {e: [] for e in self.ENGS}
        self.dma_cnt = {}
        self.sig = {e: set() for e in self.ENGS}
        self.window = {"act": 1 << 30, "dve": 1 << 30, "pool": 1 << 30}

    def _collect(self, eng, idx, reads, writes):
        deps = set()
        for t in reads:
            if t.w is not None:
                deps.add(t.w)
        for t in writes:
            if t.w is not None:
                deps.add(t.w)
            deps.update(t.r)
        out = []
        for d in deps:
            if d[0] == "e" and d[1] == eng:
                if eng in ("pe", "sp"):
                    continue
                if idx - d[2] > self.window[eng]:
                    continue
            out.append(d)
        for d in out:
            if d[0] == "e":
                self.sig[d[1]].add(d[2])
        return out

    def _mark(self, me, reads, writes):
        for t in reads:
            t.r.append(me)
            if len(t.r) > 64:
                t.r = t.r[-48:]
        for t in writes:
            t.w = me
            t.r = []

    def op(self, eng, fn, reads=(), writes=()):
        idx = len(self.streams[eng])
        deps = self._collect(eng, idx, reads, writes)
        self.streams[eng].append((fn, deps, None))
        self._mark(("e", eng, idx), reads, writes)

    def dma(self, eng, fns, key, reads=(), writes=()):
        idx = len(self.streams[eng])
        deps = self._collect(eng, idx, reads, writes)
        cnt = self.dma_cnt.get(key, 0)
        for i, fn in enumerate(fns):
            cnt += 16
            self.streams[eng].append((fn, deps if i == 0 else [], key))
        self.dma_cnt[key] = cnt
        self._mark(("s", key, cnt), reads, writes)

    def wait_all(self, eng, tiles):
        idx = len(self.streams[eng])
        deps = self._collect(eng, idx, (), tiles)
        self.streams[eng].append((None, deps, None))

    def emit(self, stack):
        nc = self.nc
        esem = {e: stack.enter_context(nc.semaphore("S_" + e)) for e in self.ENGS if e != "sp"}
        dsem = {k: stack.enter_context(nc.semaphore("D_" + str(k))) for k in self.dma_cnt}
        cnt_at = {}
        for e in self.ENGS:
            c = 0
            m = {}
            for i in range(len(self.streams[e])):
                if i in self.sig[e]:
                    c += 1
                    m[i] = c
            cnt_at[e] = m
        block = stack.enter_context(nc.Block())

        def run(eng_name):
            def body(eng):
                waited = {}
                for i, (fn, deps, key) in enumerate(self.streams[eng_name]):
                    for d in deps:
                        if d[0] == "e":
                            sem, val, k = esem[d[1]], cnt_at[d[1]][d[2]], ("e", d[1])
                        else:
                            sem, val, k = dsem[d[1]], d[2], ("s", d[1])
                        if waited.get(k, 0) >= val:
                            continue
                        waited[k] = val
                        eng.wait_ge(sem, val)
                    if fn is None:
                        continue
                    ins = fn(eng)
                    if key is not None:
                        ins.then_inc(dsem[key], 16)
                    elif i in self.sig[eng_name]:
                        ins.then_inc(esem[eng_name], 1)
            return body

        block.tensor(run("pe"))
        block.scalar(run("act"))
        block.vector(run("dve"))
        block.gpsimd(run("pool"))
        block.sync(run("sp"))


class Ring:
    def __init__(self, bufs):
        self.bufs = bufs
        self.i = 0

    def next(self):
        b = self.bufs[self.i % len(self.bufs)]
        self.i += 1
        return b


def build_program():
    nc = bass.Bass("TRN2", target_bir_lowering=False)

    def din(name, shape):
        return nc.dram_tensor(name, list(shape), F32, kind="ExternalInput").ap()

    def dout(name, shape):
        return nc.dram_tensor(name, list(shape), F32, kind="ExternalOutput").ap()

    xT = din("xT", [128, 8, NT * TT])
    xsT = din("xsT", [128, 8, NS])
    cT = din("cT", [128, 8, 1 + NS])
    sgla = din("sgla", [L, NS, 4, 128, 256])
    sscf = din("sscf", [L, 128, 8, 2, NS])
    srgcf = din("srgcf", [L, 128, 8, 3, NS])
    srghf = din("srghf", [L, 128, 8, NS])
    w_ada = din("w_ada", [L, D, 3 * D])
    w_in = din("w_in", [L, D, N_IN])
    wbr = [din("wgla", [L, D, D]), din("wsc", [L, D, D]), din("wrg", [L, D, D])]
    w_out = din("w_out", [L, D, D])
    rgwa_d = din("rgwa", [L, 8, 128, 128])
    rgwx_d = din("rgwx", [L, 8, 128, 128])
    gwa_d = din("gwa", [L, 17, 512])
    lvec_d = din("lvec", [L, 128, NV])
    consts_d = din("consts", [128, 5 * 128])

    yT = dout("yT", [128, 8, NT * TT])
    ysT = dout("ysT", [128, 8, NS])
    ogla_p = dout("ogla_p", [L, 4, 128, 256])
    osc_p = dout("osc_p", [L, 128, 8, 2])
    orgc_p = dout("orgc_p", [L, 128, 8, 3])
    orgh_p = dout("orgh_p", [L, 128, 8])
    ogla_s = dout("ogla_s", [L, NS, 4, 128, 256])
    osc_s = dout("osc_s", [L, 128, 8, 2, NS])
    orgc_s = dout("orgc_s", [L, 128, 8, 3, NS])
    orgh_s = dout("orgh_s", [L, 128, 8, NS])

    with ExitStack() as st:
        P = Prog(nc)
        STAGE = int(os.environ.get("KSTAGE", "9"))

        def sb(name, shape, dt=F32):
            t = st.enter_context(nc.sbuf_tensor("s_" + name, list(shape), dt))
            return Buf(t[:], Tile(name))

        def ps(name, shape):
            t = st.enter_context(nc.psum_tensor("p_" + name, list(shape), F32))
            return Buf(t[:], Tile(name))

        def MM(out, lhsT, rhs, start, stop, R, W):
            P.op("pe", lambda e: e.matmul(out, lhsT=lhsT, rhs=rhs, start=start, stop=stop), R, W)

        def ACT(out, in_, func, R, W, bias=None, scale=None):
            kw = {}
            if bias is not None:
                kw["bias"] = bias
            if scale is not None:
                kw["scale"] = scale
            P.op("act", lambda e: e.activation(out=out, in_=in_, func=func, **kw), R, W)

        def ACPY(out, in_, R, W):
            P.op("act", lambda e: e.copy(out=out, in_=in_), R, W)

        def VCPY(out, in_, R, W):
            P.op("dve", lambda e: e.tensor_copy(out=out, in_=in_), R, W)

        def TTo(out, a, b, op, R, W):
            P.op("dve", lambda e: e.tensor_tensor(out=out, in0=a, in1=b, op=op), R, W)

        def TS(out, a, s1, s2, op0, op1, R, W):
            if op1 is None:
                P.op("dve", lambda e: e.tensor_scalar(out=out, in0=a, scalar1=s1, scalar2=None, op0=op0), R, W)
            else:
                P.op("dve", lambda e: e.tensor_scalar(out=out, in0=a, scalar1=s1, scalar2=s2, op0=op0, op1=op1), R, W)

        def STT(out, a, s, b, op0, op1, R, W):
            P.op("dve", lambda e: e.scalar_tensor_tensor(out=out, in0=a, scalar=s, in1=b, op0=op0, op1=op1), R, W)

        def PTS(out, a, s1, s2, op0, op1, R, W):
            P.op("pool", lambda e: e.tensor_scalar(out=out, in0=a, scalar1=s1, scalar2=s2, op0=op0, op1=op1), R, W)

        def PSTT(out, a, s, b, op0, op1, R, W):
            P.op("pool", lambda e: e.scalar_tensor_tensor(out=out, in0=a, scalar=s, in1=b, op0=op0, op1=op1), R, W)

        def PCPY(out, in_, R, W):
            P.op("pool", lambda e: e.tensor_copy(out=out, in_=in_), R, W)

        def DMA(eng, key, pairs, R, W):
            fns = []
            for (o, i) in pairs:
                fns.append((lambda o, i: (lambda e: e.dma_start(out=o, in_=i)))(o, i))
            P.dma(eng, fns, key, R, W)

        MUL, ADD, MIN = ALU.mult, ALU.add, ALU.min

        consts = sb("consts", [128, 5 * 128])
        ident = consts.a[:, 0:128]
        onesf = consts.a[:, 128:256]
        triNeg = consts.a[:, 256:384]
        triUpNeg = consts.a[:, 384:512]
        maskT = consts.a[:, 512:640]
        onesb = sb("onesb", [128, 128], BF16)
        lvec = [sb(f"lvec{l}", [128, NV]) for l in range(L)]
        cneg = [sb(f"cneg{l}", [128, 8]) for l in range(L)]
        gwa = [sb(f"gwa{l}", [17, 512]) for l in range(L)]
        rgwa = [sb(f"rgwa{l}", [128, 8, 128], BF16) for l in range(L)]
        rgwx = [sb(f"rgwx{l}", [128, 8, 128], BF16) for l in range(L)]
        modT = [sb(f"modT{l}", [128, 24, 1 + NS]) for l in range(L)]
        Ap = [sb(f"Ap{l}", [128, 8]) for l in range(L)]
        As = [sb(f"As{l}", [128, 8, NS]) for l in range(L)]
        cs = sb("cs", [128, 8, 1 + NS])
        scb = sb("scb", [128, 8, 1 + NS], BF16)

        wslots = Ring([sb(f"ws{i}", [128, 8, 512], BF16) for i in range(3)])
        xt = Ring([sb(f"xt{i}", [128, 8, TT]) for i in range(2)])
        hT = sb("hT", [128, 8, TT], BF16)
        pbr0 = sb("pbr0", [128, 8, TT], BF16)
        raw1 = sb("raw1", [128, 4 * 512])
        raw2 = sb("raw2", [128, 4 * 512])
        pbr = [pbr0,
               Buf(raw1.a.bitcast(BF16).rearrange("p (k n) -> p k n", k=8), raw1.T),
               Buf(raw2.a.bitcast(BF16).rearrange("p (k n) -> p k n", k=8), raw2.T)]
        mg = sb("mg", [128, 8, TT], BF16)
        sqb = sb("sqb", [128, 4, TT], BF16)
        rstd = sb("rstd", [128, TT])
        wk = Ring([sb(f"wk{i}", [128, TT]) for i in range(4)])
        xcd = Ring([sb(f"xcd{i}", [128, TT]) for i in range(2)])
        gbuf = [sb(f"gb{i}", [128, TT]) for i in range(3)]
        zlr = sb("zlr", [17, TT])
        sp_tok = Buf(raw1.a.rearrange("p (c n) -> p c n", c=4), raw1.T)
        ekd = Buf(raw2.a.rearrange("p (c n) -> p c n", c=4), raw2.T)
        eb = [sb(f"eb{h}", [128, 4]) for h in range(4)]
        qk = sb("qk", [128, 8, TT], BF16)
        qe = [Buf(qk.a[:, h, :], qk.T) for h in range(4)]
        ke = [Buf(qk.a[:, 4 + h, :], qk.T) for h in range(4)]
        hvec = [sb(f"hvec{l}", [128, NV]) for l in range(L)]
        hc = [sb(f"hc{l}", [128, 8]) for l in range(L)]
        gqp = [sb(f"gqp{l}", [128, 8]) for l in range(L)]
        gqs = [sb(f"gqs{l}", [128, 8, NS]) for l in range(L)]
        vtok = [Buf(None, mg.T) for h in range(4)]

        def vt(h, c):
            return mg.a[:, 2 * h + c // 2, (c % 2) * 256:(c % 2) * 256 + 256]
        kdtok = [sb(f"kdtok{h}", [128, 4, 128], BF16) for h in range(4)]
        attm = Ring([sb(f"attm{i}", [128, 128], BF16) for i in range(4)])
        oT = [sb(f"oT{h}", [128, 2, TT]) for h in range(4)]
        S = [[sb(f"S{l}_{h}", [128, 256]) for h in range(4)] for l in range(L)]
        Sb = [[sb(f"Sb{l}_{h}", [128, 256], BF16) for h in range(4)] for l in range(L)]
        csc = [sb(f"csc{l}", [128, 8, 2]) for l in range(L)]
        crg = [sb(f"crg{l}", [128, 8, 3]) for l in range(L)]
        chh = [sb(f"chh{l}", [128, 8]) for l in range(L)]
        ub = Ring([sb(f"ub{i}", [128, TT + 3]) for i in range(2)])
        xcb = sb("xcb", [128, TT], BF16)
        xs = sb("xs", [128, 8, NS])
        ssc = sb("ssc", [128, 8, 2, NS])
        srgc = sb("srgc", [128, 8, 3, NS])
        srgh = sb("srgh", [128, 8, NS])
        nsc = sb("nsc", [128, 8, 2, NS])
        nrgc = sb("nrgc", [128, 8, 3, NS])
        nrgh = sb("nrgh", [128, 8, NS])
        aTs = sb("aTs", [128, 4, NS])
        qss = sb("qss", [128, 4, NS], BF16)
        kvs = sb("kvs", [NS, 4, 384], BF16)
        km4 = Ring([Buf(sqb.a[0:NS, i, :].rearrange("p (h d) -> p h d", h=4), Tile(f"km4_{i}")) for i in range(2)])
        Sin = [Buf(oT[i].a.rearrange("p a (b c) -> p (a b) c", b=2), oT[i].T) for i in range(2)]
        Snw = [Buf(oT[i].a.rearrange("p a (b c) -> p (a b) c", b=2), oT[i].T) for i in range(2, 4)]
        snbq = [Buf(qk.a[:, 2 * i:2 * i + 2, :].rearrange("p k (a v) -> p (k a) v", a=2), Tile(f"snbq{i}")) for i in range(2)]
        sgs = sb("sgs", [128, 8, NS])
        snb4 = Ring([Buf(ub.bufs[i].a[:, 0:512].bitcast(BF16).rearrange("p (h v) -> p h v", h=4), ub.bufs[i].T) for i in range(2)])
        oTs = sb("oTs", [128, 8, NS])
        hT_s = sb("hT_s", [128, 8, NS], BF16)
        pbr_s = [sb(f"pbr_s{i}", [128, 8, NS], BF16) for i in range(3)]
        mg_s = sb("mg_s", [128, 8, NS], BF16)
        rsb_s = [sb(f"rsb_s{i}", [128, NS]) for i in range(4)]
        gb_s = [sb(f"gb_s{i}", [128, NS]) for i in range(3)]
        wk_s = Ring([sb(f"wk_s{i}", [128, NS]) for i in range(4)])
        xcd_s = Ring([sb(f"xcd_s{i}", [128, NS]) for i in range(2)])
        xcb_s = sb("xcb_s", [128, NS], BF16)

        zr = Ring([ps(f"z{i}", [128, 512]) for i in range(8)])

        class _SmView:
            def next(self):
                b = zr.next()
                return Buf(b.a[:, 0:256], b.T)
        smr = _SmView()

        def wview(w, l):
            return w[l].rearrange("(k p) n -> p k n", p=128)

        NGRP = 43
        wscr = nc.dram_tensor("wscr", [L * NGRP, 128, 8 * 512], BF16).ap()
        wscr_T = [Tile(f"wscr{g}") for g in range(L * NGRP)]

        def wload(entry):
            pieces, gid, mode = entry
            slot = wslots.next()
            ntot = sum(n for (_, _, n) in pieces)
            flat = slot.a.rearrange("p k n -> p (k n)")[:, 0:8 * ntot]
            sv = Buf(flat.rearrange("p (k n) -> p k n", k=8), slot.T)
            if mode == "ld":
                DMA("sp", slot.T.name, [(flat, wscr[gid][:, 0:8 * ntot])], [wscr_T[gid]], [slot.T])
                return sv
            pairs = [(sv.a[:, :, d0:d0 + n], src) for (src, d0, n) in pieces]
            DMA("pool", slot.T.name, pairs, [], [slot.T])
            if mode == "cast+st":
                DMA("sp", "wst_" + slot.T.name, [(wscr[gid][:, 0:8 * ntot], flat)], [slot.T], [wscr_T[gid]])
            return sv

        class WQ:
            def __init__(self):
                self.plan = []
                self.issued = []
                self.pos = 0

            def add(self, pieces):
                self.plan.append([pieces, None, "cast"])

            def get(self):
                while len(self.issued) < min(len(self.plan), self.pos + 3):
                    self.issued.append(wload(self.plan[len(self.issued)]))
                s = self.issued[self.pos]
                self.pos += 1
                return s

        wq = WQ()

        def plan_layer(l):
            wi = wview(w_in, l)
            wq.add([(wi[:, :, OLR:OLR + 16], 0, 16)])
            for h in range(4):
                wq.add([(wi[:, :, OQ + 128 * h:OQ + 128 * h + 128], 0, 128),
                        (wi[:, :, OK_ + 128 * h:OK_ + 128 * h + 128], 128, 128),
                        (wi[:, :, OV + 256 * h:OV + 256 * h + 256], 256, 256)])
            for h in range(4):
                wq.add([(wi[:, :, OGG + 256 * h:OGG + 256 * h + 256], 0, 256)])
            for j in range(8):
                wq.add([(wi[:, :, o + 128 * j:o + 128 * j + 128], 128 * q, 128) for q, o in enumerate((OX, OC, OH, OB))])
                wq.add([(wi[:, :, o + 128 * j:o + 128 * j + 128], 128 * q, 128) for q, o in enumerate((OGS, OGR))])
            for i in range(8):
                wq.add([(wi[:, :, OM + 1024 * br + 128 * i:OM + 1024 * br + 128 * i + 128], 128 * br, 128) for br in range(3)])
                wq.add([(wview(wbr[br], l)[:, :, 128 * i:128 * i + 128], 128 * br, 128) for br in range(3)])
            wo = wview(w_out, l)
            for g in range(2):
                wq.add([(wo[:, :, 512 * g:512 * g + 512], 0, 512)])

        for l in range(L):
            wa = wview(w_ada, l)
            for g in range(6):
                wq.add([(wa[:, :, 512 * g:512 * g + 512], 0, 512)])
        for _pass in range(NT):
            for l in range(L):
                n0 = len(wq.plan)
                plan_layer(l)
                assert len(wq.plan) - n0 == NGRP
                for g in range(NGRP):
                    wq.plan[n0 + g][1] = l * NGRP + g
                    wq.plan[n0 + g][2] = "cast+st" if _pass == 0 else "ld"

        DMA("sp", "consts", [(consts.a, consts_d[:, :])], [], [consts.T])
        for l in range(L):
            DMA("sp", f"lvec{l}", [(lvec[l].a, lvec_d[l])], [], [lvec[l].T])
            DMA("sp", f"gwa{l}", [(gwa[l].a, gwa_d[l])], [], [gwa[l].T])
            DMA("pool", f"rgwa{l}", [(rgwa[l].a, rgwa_d[l].rearrange("n i j -> i n j"))], [], [rgwa[l].T])
            DMA("pool", f"rgwx{l}", [(rgwx[l].a, rgwx_d[l].rearrange("n i j -> i n j"))], [], [rgwx[l].T])
        DMA("sp", "cs", [(cs.a, cT[:, :, :])], [], [cs.T])
        VCPY(onesb.a, onesf, [consts.T], [onesb.T])
        P.op("dve", lambda e: e.memset(zlr.a, 1.0), [], [zlr.T])
        for l in range(L if STAGE >= -2 else 0):
            for h in range(4):
                P.op("dve", (lambda a: (lambda e: e.memset(a, 0.0)))(S[l][h].a), [], [S[l][h].T])
                P.op("dve", (lambda a: (lambda e: e.memset(a, 0.0)))(Sb[l][h].a), [], [Sb[l][h].T])
            P.op("dve", (lambda a: (lambda e: e.memset(a, 0.0)))(csc[l].a), [], [csc[l].T])
            P.op("dve", (lambda a: (lambda e: e.memset(a, 0.0)))(crg[l].a), [], [crg[l].T])
            P.op("dve", (lambda a: (lambda e: e.memset(a, 0.0)))(chh[l].a), [], [chh[l].T])
        ACT(scb.a, cs.a, AF.Silu, [cs.T], [scb.T])
        for l in range(L if STAGE >= -1 else 0):
            lv = lvec[l]
            ACT(cneg[l].a, lv.a[:, VLAM:VLAM + 8], AF.Exp, [lv.T], [cneg[l].T], scale=-1.0)
            ACT(cneg[l].a, cneg[l].a, AF.Ln, [cneg[l].T], [cneg[l].T], bias=1.0)
            TS(cneg[l].a, cneg[l].a, -8.0, None, MUL, None, [cneg[l].T], [cneg[l].T])
            TS(hc[l].a, cneg[l].a, 0.5, None, MUL, None, [cneg[l].T], [hc[l].T])
            TS(hvec[l].a, lv.a, 0.5, None, MUL, None, [lv.T], [hvec[l].T])
            for g in range(6):
                slot = wq.get()
                for cc in range(4):
                    ch = g * 4 + cc
                    z = zr.next()
                    for k in range(8):
                        MM(z.a[:, 0:1 + NS], slot.a[:, k, cc * 128:cc * 128 + 128], scb.a[:, k, :], k == 0, k == 7,
                           [slot.T, scb.T], [z.T])
                    ACT(modT[l].a[:, ch, :], z.a[:, 0:1 + NS], AF.Identity, [z.T, lv.T], [modT[l].T],
                        bias=lv.a[:, VBADA + ch:VBADA + ch + 1])
            TS(Ap[l].a, modT[l].a[:, 8:16, 0], 1.0, None, ADD, None, [modT[l].T], [Ap[l].T])
            TTo(Ap[l].a, Ap[l].a, lv.a[:, VNG:VNG + 8], MUL, [Ap[l].T, lv.T], [Ap[l].T])
            for k in range(8):
                TS(As[l].a[:, k, :], modT[l].a[:, 8 + k, 1:1 + NS], 1.0, lv.a[:, VNG + k:VNG + k + 1], ADD, MUL,
                   [modT[l].T, lv.T], [As[l].T])
            TS(gqp[l].a, modT[l].a[:, 16:24, 0], 0.25, None, MUL, None, [modT[l].T], [gqp[l].T])
            TS(gqs[l].a, modT[l].a[:, 16:24, 1:1 + NS], 0.25, None, MUL, None, [modT[l].T], [gqs[l].T])

        def proj(slot, c0, M, rhs, N, extra_R=()):
            z = zr.next()
            for k in range(8):
                MM(z.a[0:M, 0:N], slot.a[:, k, c0:c0 + M], rhs.a[:, k, 0:N], k == 0, k == 7,
                   [slot.T, rhs.T] + list(extra_R), [z.T])
            return z

        def rms_stats(src_fn, nk, N, scale, Rsrc, rstd=rstd):
            z = zr.next()
            for k0 in range(0, nk, 4):
                for k in range(k0, min(nk, k0 + 4)):
                    ACT(sqb.a[:, k % 4, 0:N], src_fn(k), AF.Square, Rsrc, [sqb.T])
                for k in range(k0, min(nk, k0 + 4)):
                    MM(z.a[:, 0:N], onesb.a, sqb.a[:, k % 4, 0:N], k == 0, k == nk - 1, [onesb.T, sqb.T], [z.T])
            ACT(rstd.a[:, 0:N], z.a[:, 0:N], AF.Ln, [z.T], [rstd.T], bias=EPS, scale=scale)
            ACT(rstd.a[:, 0:N], rstd.a[:, 0:N], AF.Exp, [rstd.T], [rstd.T], scale=-0.5)

        class Ctx:
            pass

        def block(l, cx):
            x, N, samp, tile_idx = cx.x, cx.N, cx.samp, cx.tile_idx
            hT, pbr, mg, gbuf, wk, xcd, xcb = cx.hT, cx.pbr, cx.mg, cx.gb, cx.wk, cx.xcd, cx.xcb
            lv = lvec[l]
            md = modT[l]
            last_tile = (not samp) and tile_idx == NT - 1
            PB = int(os.environ.get("KPB", "99")) if not samp else 99
            rms_stats(lambda k: x.a[:, k, 0:N], 8, N, 1.0 / D, [x.T])
            for k in range(8):
                t = wk.next()
                if not samp:
                    STT(t.a[:, 0:N], x.a[:, k, 0:N], Ap[l].a[:, k:k + 1], rstd.a[:, 0:N], MUL, MUL,
                        [x.T, Ap[l].T, rstd.T], [t.T])
                    ACT(hT.a[:, k, 0:N], t.a[:, 0:N], AF.Identity, [t.T, md.T], [hT.T], bias=md.a[:, k, 0:1])
                else:
                    TTo(t.a[:, 0:N], x.a[:, k, 0:N], rstd.a[:, 0:N], MUL, [x.T, rstd.T], [t.T])
                    TTo(t.a[:, 0:N], t.a[:, 0:N], As[l].a[:, k, :], MUL, [t.T, As[l].T], [t.T])
                    TTo(hT.a[:, k, 0:N], t.a[:, 0:N], md.a[:, k, 1:1 + NS], ADD, [t.T, md.T], [hT.T])

            if PB <= 1:
                return
            slot = yield
            z = proj(slot, 0, 16, hT, N)
            ACPY(zlr.a[0:16, 0:N], z.a[0:16, 0:N], [z.T], [zlr.T])
            if not samp:
                for c in range(4):
                    z = zr.next()
                    MM(z.a[:, :], zlr.a[0:17, c * 128:c * 128 + 128], gwa[l].a[0:17, :], True, True, [zlr.T, gwa[l].T], [z.T])
                    ACT(sp_tok.a[:, c, :], z.a[:, :], AF.Exp, [z.T], [sp_tok.T], scale=-1.0)
                    ACT(sp_tok.a[:, c, :], sp_tok.a[:, c, :], AF.Ln, [sp_tok.T], [sp_tok.T], bias=1.0)
                for c in range(4):
                    z = zr.next()
                    MM(z.a[:, :], triUpNeg, sp_tok.a[:, c, :], True, True, [consts.T, sp_tok.T], [z.T])
                    ACT(ekd.a[:, c, :], z.a[:, :], AF.Exp, [z.T], [ekd.T])
            else:
                z = zr.next()
                for h in range(4):
                    MM(z.a[:, h * NS:(h + 1) * NS], gwa[l].a[0:17, h * 128:h * 128 + 128], zlr.a[0:17, 0:N], True, True,
                       [zlr.T, gwa[l].T], [z.T])
                av = aTs.a.rearrange("p h s -> p (h s)")
                ACT(av, z.a[:, 0:4 * NS], AF.Exp, [z.T], [aTs.T], scale=-1.0)
                ACT(av, av, AF.Ln, [aTs.T], [aTs.T], bias=1.0)
                ACT(av, av, AF.Exp, [aTs.T], [aTs.T], scale=-1.0 / 16.0)

            if PB <= 2:
                return
            for h in range(4):
                slot = yield
                if not samp:
                    zb = zr.next()
                    for c in range(4):
                        MM(zb.a[:, c * 128:c * 128 + 128], sp_tok.a[:, c, h * 128:h * 128 + 128], triNeg, True, True,
                           [sp_tok.T, consts.T], [zb.T])
                    E2 = wk.next()
                    E1 = wk.next()
                    ACT(E1.a, zb.a, AF.Exp, [zb.T], [E1.T])
                    ACT(E2.a, zb.a, AF.Exp, [zb.T], [E2.T], scale=-1.0)
                    ACPY(eb[h].a, E1.a.rearrange("p (c n) -> p c n", c=4)[:, :, 127], [E1.T], [eb[h].T])
                    KA = int(os.environ.get("KA", "9"))
                    if KA <= 1:
                        continue
                    zq = proj(slot, 0, 128, hT, N)
                    STT(qe[h].a, zq.a, float(128 ** -0.5), E1.a, MUL, MUL, [zq.T, E1.T], [qe[h].T])
                    zk = proj(slot, 128, 128, hT, N)
                    TTo(ke[h].a, zk.a, E2.a, MUL, [zk.T, E2.T], [ke[h].T])
                    if KA <= 2:
                        continue
                    for c in range(4):
                        z = zr.next()
                        for k in range(8):
                            MM(z.a[:, 0:384], hT.a[:, k, c * 128:c * 128 + 128], slot.a[:, k, 128:512], k == 0, k == 7,
                               [hT.T, slot.T], [z.T])
                        if KA <= 3:
                            continue
                        ACPY(vt(h, c), z.a[:, 128:384], [z.T], [vtok[h].T])
                        if KA <= 4:
                            continue
                        TTo(kdtok[h].a[:, c, :], z.a[:, 0:128], ekd.a[:, c, h * 128:h * 128 + 128], MUL, [z.T, ekd.T, vtok[h].T], [kdtok[h].T])
                else:
                    zq = proj(slot, 0, 128, hT, N)
                    TS(qss.a[:, h, :], zq.a[:, 0:N], float(128 ** -0.5), None, MUL, None, [zq.T], [qss.T])
                    z = zr.next()
                    for k in range(8):
                        MM(z.a[0:NS, 0:384], hT.a[:, k, 0:N], slot.a[:, k, 128:512], k == 0, k == 7, [hT.T, slot.T], [z.T])
                    ACPY(kvs.a[:, h, :], z.a[0:NS, 0:384], [z.T], [kvs.T])

            if PB <= 3:
                return
            if not samp:
                for c in range(4):
                    cs_ = slice(c * 128, c * 128 + 128)
                    sa4 = zr.next()
                    for h in range(4):
                        MM(sa4.a[:, h * 128:h * 128 + 128], ke[h].a[:, cs_], qe[h].a[:, cs_], True, True, [qk.T], [sa4.T])
                    ams = []
                    for h in range(4):
                        am = attm.next()
                        TTo(am.a, sa4.a[:, h * 128:h * 128 + 128], maskT, MUL, [sa4.T, consts.T], [am.T])
                        ams.append(am)
                    sos = [zr.next(), zr.next()]
                    for h in range(4):
                        so = sos[h // 2]
                        for vc in range(2):
                            col = (h % 2) * 256 + vc * 128
                            MM(so.a[:, col:col + 128], vt(h, c)[:, vc * 128:vc * 128 + 128], ams[h].a, True, False,
                               [vtok[h].T, ams[h].T], [so.T])
                            MM(so.a[:, col:col + 128], Sb[l][h].a[:, vc * 128:vc * 128 + 128], qe[h].a[:, cs_], False, True,
                               [Sb[l][h].T, qk.T], [so.T])
                    sus = [zr.next(), zr.next()]
                    for h in range(4):
                        su = sus[h // 2]
                        MM(su.a[:, (h % 2) * 256:(h % 2) * 256 + 256], kdtok[h].a[:, c, :], vt(h, c), True, True,
                           [kdtok[h].T, vtok[h].T], [su.T])
                    for h in range(4):
                        so = sos[h // 2]
                        ACPY(oT[h].a[:, :, cs_], so.a[:, (h % 2) * 256:(h % 2) * 256 + 256].rearrange("p (a b) -> p a b", a=2),
                             [so.T], [oT[h].T])
                    for h in range(4):
                        su = sus[h // 2]
                        STT(S[l][h].a, S[l][h].a, eb[h].a[:, c:c + 1], su.a[:, (h % 2) * 256:(h % 2) * 256 + 256], MUL, ADD,
                            [S[l][h].T, eb[h].T, su.T], [S[l][h].T])
                        ACPY(Sb[l][h].a, S[l][h].a, [S[l][h].T], [Sb[l][h].T])
                if last_tile:
                    for h in range(4):
                        DMA("sp", f"S{l}_{h}", [(ogla_p[l, h], S[l][h].a)], [S[l][h].T], [])
            else:
                pass

            def s_A(s_):
                si = Sin[s_ % 2]
                DMA("sp", "ld_" + si.T.name + "s", [(si.a, sgla[l, s_].rearrange("h d v -> d h v"))], [], [si.T])
                km = km4.bufs[s_ % 2]
                TS(km.a, kvs.a[:, :, 0:128], ident[0:NS, s_:s_ + 1], None, MUL, None, [kvs.T, consts.T, sqb.T], [km.T])

            def s_B(s_):
                si, sn, km, sbf = Sin[s_ % 2], Snw[s_ % 2], km4.bufs[s_ % 2], snbq[s_ % 2]
                sus = [zr.next(), zr.next()]
                for h in range(4):
                    su = sus[h // 2]
                    MM(su.a[:, (h % 2) * 256:(h % 2) * 256 + 256], km.a[:, h, :], kvs.a[:, h, 128:384], True, True,
                       [km.T, kvs.T, sqb.T], [su.T])
                for h in range(4):
                    su = sus[h // 2]
                    STT(sn.a[:, h, :], si.a[:, h, :], aTs.a[:, h, s_:s_ + 1], su.a[:, (h % 2) * 256:(h % 2) * 256 + 256], MUL, ADD,
                        [si.T, aTs.T, su.T], [sn.T])
                ACPY(sbf.a, sn.a, [sn.T, qk.T], [sbf.T])
                DMA("sp", "st_" + sn.T.name + "s", [(ogla_s[l, s_].rearrange("h d v -> d h v"), sn.a)], [sn.T], [])

            def s_C(s_):
                sbf = snbq[s_ % 2]
                so = zr.next()
                for h in range(4):
                    for vc in range(2):
                        MM(so.a[:, 2 * h + vc:2 * h + vc + 1], sbf.a[:, h, vc * 128:vc * 128 + 128], qss.a[:, h, s_:s_ + 1], True, True,
                           [sbf.T, qss.T, qk.T], [so.T])
                ACPY(oTs.a[:, :, s_:s_ + 1], so.a[:, 0:8].rearrange("p (a b) -> p a b", b=1), [so.T], [oTs.T])

            def sstep(n):
                if n == 0:
                    ACPY(qk.a[0:1, 0, 0:1], qk.a[0:1, 0, 0:1], [], [qk.T])
                    ACPY(sqb.a[0:1, 3, 0:1], sqb.a[0:1, 3, 0:1], [], [sqb.T])
                    s_A(0)
                if n + 1 < NS:
                    s_A(n + 1)
                if 0 <= n - 1 < NS:
                    s_C(n - 1)
                if n < NS:
                    s_B(n)

            if PB <= 4:
                return
            rsb = cx.rsb
            for h in range(4):
                if not samp:
                    rms_stats(lambda vc: oT[h].a[:, vc, :], 2, N, 1.0 / 256, [oT[h].T], rsb[h])
            for h in range(4):
                slot = yield
                for vc in range(2):
                    zg = proj(slot, vc * 128, 128, hT, N)
                    if samp:
                        ACT(sgs.a[:, 2 * h + vc, :], zg.a[:, 0:N], AF.Tanh, [zg.T], [sgs.T], scale=0.5)
                        STT(sgs.a[:, 2 * h + vc, :], sgs.a[:, 2 * h + vc, :], 1.0, zg.a[:, 0:N], ADD, MUL, [sgs.T, zg.T], [sgs.T])
                        continue
                    sg = wk.next()
                    ACT(sg.a[:, 0:N], zg.a[:, 0:N], AF.Tanh, [zg.T], [sg.T], scale=0.5)
                    STT(sg.a[:, 0:N], sg.a[:, 0:N], 1.0, zg.a[:, 0:N], ADD, MUL, [sg.T, zg.T], [sg.T])
                    t1 = wk.next()
                    osrc = oT[h].a[:, vc, :] if not samp else oTs.a[:, 2 * h + vc, :]
                    oTt = oT[h].T if not samp else oTs.T
                    STT(t1.a[:, 0:N], osrc, lv.a[:, VGNG + vc:VGNG + vc + 1], rsb[h].a[:, 0:N], MUL, MUL,
                        [oTt, lv.T, rsb[h].T], [t1.T])
                    TTo(pbr[0].a[:, 2 * h + vc, 0:N], t1.a[:, 0:N], sg.a[:, 0:N], MUL, [t1.T, sg.T], [pbr[0].T])

            if PB <= 5:
                return
            hv = hvec[l]
            def rg2_tail(j, zrr, zi, zg, xc):
                a_ = wk.next()
                ACT(a_.a[:, 0:N], zrr.a[:, 0:N], AF.Tanh, [zrr.T, hv.T], [a_.T], bias=hv.a[:, VRGBA + j:VRGBA + j + 1], scale=0.5)
                ig = wk.next()
                ACT(ig.a[:, 0:N], zi.a[:, 0:N], AF.Tanh, [zi.T, hv.T], [ig.T], bias=hv.a[:, VRGBX + j:VRGBX + j + 1], scale=0.5)
                sg = wk.next()
                ACT(sg.a[:, 0:N], zg.a[:, 0:N], AF.Tanh, [zg.T], [sg.T], scale=0.5)
                ACT(a_.a[:, 0:N], a_.a[:, 0:N], AF.Exp, [a_.T, hc[l].T], [a_.T], bias=hc[l].a[:, j:j + 1], scale=hc[l].a[:, j:j + 1])
                sq_ = wk.next()
                STT(sq_.a[:, 0:N], a_.a[:, 0:N], 0.9999999, a_.a[:, 0:N], MIN, MUL, [a_.T], [sq_.T])
                ACT(sq_.a[:, 0:N], sq_.a[:, 0:N], AF.Sqrt, [sq_.T], [sq_.T], bias=0.25, scale=-0.25)
                STT(ig.a[:, 0:N], ig.a[:, 0:N], 1.0, xc.a[:, 0:N], ADD, MUL, [ig.T, xc.T], [ig.T])
                STT(sg.a[:, 0:N], sg.a[:, 0:N], 1.0, zg.a[:, 0:N], ADD, MUL, [sg.T, zg.T], [sg.T])
                TTo(ig.a[:, 0:N], ig.a[:, 0:N], sq_.a[:, 0:N], MUL, [ig.T, sq_.T], [ig.T])
                if not samp:
                    P.op("dve", (lambda o, d0, d1, ini: (lambda e: e.tensor_tensor_scan(out=o, data0=d0, data1=d1, initial=ini,
                                                                                        op0=MUL, op1=ADD)))(
                        xc.a[:, 0:N], a_.a[:, 0:N], ig.a[:, 0:N], chh[l].a[:, j:j + 1]),
                        [a_.T, ig.T, chh[l].T], [xc.T])
                    VCPY(chh[l].a[:, j:j + 1], xc.a[:, N - 1:N], [xc.T], [chh[l].T])
                    hsrc = xc.a[:, 0:N]
                    hT_ = xc.T
                else:
                    TTo(a_.a[:, 0:N], a_.a[:, 0:N], srgh.a[:, j, :], MUL, [a_.T, srgh.T], [a_.T])
                    TTo(nrgh.a[:, j, :], a_.a[:, 0:N], ig.a[:, 0:N], ADD, [a_.T, ig.T], [nrgh.T])
                    hsrc = nrgh.a[:, j, :]
                    hT_ = nrgh.T
                TTo(pbr[2].a[:, j, 0:N], hsrc, sg.a[:, 0:N], MUL, [hT_, sg.T], [pbr[2].T])

            pending = None
            for j in range(8):
                slotA = yield
                if samp:
                    sstep(2 * j)
                cw = [lv.a[:, VRGW + 4 * j + r:VRGW + 4 * j + r + 1] for r in range(4)]
                cb = lv.a[:, VRGCB + j:VRGCB + j + 1]
                zx = proj(slotA, 0, 128, hT, N)
                xc = xcd.next()
                if not samp:
                    ur = ub.next()
                    ACPY(ur.a[:, 3:3 + N], zx.a[:, 0:N], [zx.T], [ur.T])
                    ACPY(ur.a[:, 0:3], crg[l].a[:, j, :], [crg[l].T], [ur.T])
                    ACPY(crg[l].a[:, j, :], ur.a[:, N:N + 3], [ur.T], [crg[l].T])
                    ACT(xc.a[:, 0:N], ur.a[:, 0:N], AF.Identity, [ur.T, lv.T], [xc.T], bias=cb, scale=cw[0])
                    for r in range(1, 4):
                        STT(xc.a[:, 0:N], ur.a[:, r:r + N], cw[r], xc.a[:, 0:N], MUL, ADD, [ur.T, lv.T, xc.T], [xc.T])
                else:
                    ACPY(nrgc.a[:, j, 2, :], zx.a[:, 0:N], [zx.T], [nrgc.T])
                    VCPY(nrgc.a[:, j, 0:2, :], srgc.a[:, j, 1:3, :], [srgc.T], [nrgc.T])
                    TS(xc.a[:, 0:N], srgc.a[:, j, 0, :], cw[0], cb, MUL, ADD, [srgc.T, lv.T], [xc.T])
                    for r in range(1, 3):
                        STT(xc.a[:, 0:N], srgc.a[:, j, r, :], cw[r], xc.a[:, 0:N], MUL, ADD, [srgc.T, lv.T, xc.T], [xc.T])
                    STT(xc.a[:, 0:N], nrgc.a[:, j, 2, :], cw[3], xc.a[:, 0:N], MUL, ADD, [nrgc.T, lv.T, xc.T], [xc.T])
                if pending is not None:
                    rg2_tail(*pending)
                    pending = None
                ACPY(xcb.a[:, 0:N], xc.a[:, 0:N], [xc.T], [xcb.T])

                slot = slotA
                w0 = lv.a[:, VSCW + 3 * j + 0:VSCW + 3 * j + 1]
                w1 = lv.a[:, VSCW + 3 * j + 1:VSCW + 3 * j + 2]
                w2 = lv.a[:, VSCW + 3 * j + 2:VSCW + 3 * j + 3]
                zc = proj(slot, 128, 128, hT, N)
                tcb = wk.next()
                ACPY(tcb.a[:, 0:N], zc.a[:, 0:N], [zc.T], [tcb.T])
                zh = proj(slot, 256, 128, hT, N)
                uc = wk.next()
                if not samp:
                    u = ub.next()
                    TTo(u.a[:, 2:2 + N], zh.a[:, 0:N], tcb.a[:, 0:N], MUL, [zh.T, tcb.T], [u.T])
                    ACPY(u.a[:, 0:2], csc[l].a[:, j, :], [csc[l].T], [u.T])
                    ACPY(csc[l].a[:, j, :], u.a[:, N:N + 2], [u.T], [csc[l].T])
                    ACT(uc.a[:, 0:N], u.a[:, 0:N], AF.Identity, [u.T, lv.T], [uc.T], scale=w0)
                    STT(uc.a[:, 0:N], u.a[:, 1:1 + N], w1, uc.a[:, 0:N], MUL, ADD, [u.T, lv.T, uc.T], [uc.T])
                    STT(uc.a[:, 0:N], u.a[:, 2:2 + N], w2, uc.a[:, 0:N], MUL, ADD, [u.T, lv.T, uc.T], [uc.T])
                else:
                    TTo(nsc.a[:, j, 1, :], zh.a[:, 0:N], tcb.a[:, 0:N], MUL, [zh.T, tcb.T], [nsc.T])
                    VCPY(nsc.a[:, j, 0, :], ssc.a[:, j, 1, :], [ssc.T], [nsc.T])
                    TS(uc.a[:, 0:N], ssc.a[:, j, 0, :], w0, None, MUL, None, [ssc.T, lv.T], [uc.T])
                    STT(uc.a[:, 0:N], ssc.a[:, j, 1, :], w1, uc.a[:, 0:N], MUL, ADD, [ssc.T, lv.T, uc.T], [uc.T])
                    STT(uc.a[:, 0:N], nsc.a[:, j, 1, :], w2, uc.a[:, 0:N], MUL, ADD, [nsc.T, lv.T, uc.T], [uc.T])
                zb_ = proj(slot, 384, 128, hT, N)
                TTo(uc.a[:, 0:N], zb_.a[:, 0:N], uc.a[:, 0:N], MUL, [zb_.T, uc.T], [uc.T])
                slotB = yield
                if samp:
                    sstep(2 * j + 1)
                zg = proj(slotB, 0, 128, hT, N)
                sg = wk.next()
                ACT(sg.a[:, 0:N], zg.a[:, 0:N], AF.Tanh, [zg.T], [sg.T], scale=0.5)
                STT(sg.a[:, 0:N], sg.a[:, 0:N], 1.0, zg.a[:, 0:N], ADD, MUL, [sg.T, zg.T], [sg.T])
                TTo(pbr[1].a[:, j, 0:N], uc.a[:, 0:N], sg.a[:, 0:N], MUL, [uc.T, sg.T], [pbr[1].T])

                slot = slotB
                zrr = zr.next()
                MM(zrr.a[:, 0:N], rgwa[l].a[:, j, :], xcb.a[:, 0:N], True, True, [rgwa[l].T, xcb.T], [zrr.T])
                zi = zr.next()
                MM(zi.a[:, 0:N], rgwx[l].a[:, j, :], xcb.a[:, 0:N], True, True, [rgwx[l].T, xcb.T], [zi.T])
                zg = proj(slot, 128, 128, hT, N)
                if samp or not cx.defer:
                    rg2_tail(j, zrr, zi, zg, xc)
                else:
                    pending = (j, zrr, zi, zg, xc)
            if pending is not None:
                rg2_tail(*pending)
                pending = None
            if samp:
                sstep(NS)
                for h in range(4):
                    rms_stats(lambda vc: oTs.a[:, 2 * h + vc, :], 2, N, 1.0 / 256, [oTs.T], rsb[h])
                for h in range(4):
                    for vc in range(2):
                        t1 = wk.next()
                        STT(t1.a[:, 0:N], oTs.a[:, 2 * h + vc, :], lv.a[:, VGNG + vc:VGNG + vc + 1], rsb[h].a[:, 0:N], MUL, MUL,
                            [oTs.T, lv.T, rsb[h].T], [t1.T])
                        TTo(pbr[0].a[:, 2 * h + vc, 0:N], t1.a[:, 0:N], sgs.a[:, 2 * h + vc, :], MUL, [t1.T, sgs.T], [pbr[0].T])
            if last_tile:
                DMA("sp", f"csc{l}", [(osc_p[l], csc[l].a)], [csc[l].T], [])
                DMA("sp", f"crg{l}", [(orgc_p[l], crg[l].a)], [crg[l].T], [])
                DMA("sp", f"chh{l}", [(orgh_p[l], chh[l].a)], [chh[l].T], [])
            if samp:
                DMA("sp", "nsc", [(osc_s[l], nsc.a)], [nsc.T], [])
                DMA("sp", "nrgc", [(orgc_s[l], nrgc.a)], [nrgc.T], [])
                DMA("sp", "nrgh", [(orgh_s[l], nrgh.a)], [nrgh.T], [])

            if PB <= 7:
                return
            for i in range(8):
                s1 = yield
                for br in range(3):
                    zm = proj(s1, 128 * br, 128, hT, N)
                    ACT(gbuf[br].a[:, 0:N], zm.a[:, 0:N], AF.Tanh, [zm.T, hv.T], [gbuf[br].T],
                        bias=hv.a[:, VBM + 8 * br + i:VBM + 8 * br + i + 1], scale=0.5)
                acc = wk.next()
                s2 = yield
                for br in range(3):
                    zy = proj(s2, 128 * br, 128, pbr[br], N)
                    if br == 0:
                        STT(acc.a[:, 0:N], gbuf[0].a[:, 0:N], 1.0, zy.a[:, 0:N], ADD, MUL, [zy.T, gbuf[0].T], [acc.T])
                    else:
                        STT(gbuf[br].a[:, 0:N], gbuf[br].a[:, 0:N], 1.0, zy.a[:, 0:N], ADD, MUL, [zy.T, gbuf[br].T], [gbuf[br].T])
                        if br == 1:
                            TTo(acc.a[:, 0:N], acc.a[:, 0:N], gbuf[1].a[:, 0:N], ADD, [acc.T, gbuf[1].T], [acc.T])
                        else:
                            TTo(mg.a[:, i, 0:N], acc.a[:, 0:N], gbuf[2].a[:, 0:N], ADD, [acc.T, gbuf[2].T], [mg.T])

            for i in range(8):
                if i % 4 == 0:
                    so_ = yield
                zo = proj(so_, (i % 4) * 128, 128, mg, N)
                if not samp:
                    STT(x.a[:, i, 0:N], zo.a[:, 0:N], gqp[l].a[:, i:i + 1], x.a[:, i, 0:N], MUL, ADD, [zo.T, gqp[l].T, x.T], [x.T])
                else:
                    t = wk.next()
                    TTo(t.a[:, 0:N], zo.a[:, 0:N], gqs[l].a[:, i, :], MUL, [zo.T, gqs[l].T], [t.T])
                    TTo(x.a[:, i, 0:N], x.a[:, i, 0:N], t.a[:, 0:N], ADD, [x.T, t.T], [x.T])

        def final_norm(x, N, dst, dst_dram, key):
            rms_stats(lambda k: x.a[:, k, 0:N], 8, N, 1.0 / D, [x.T])
            for k in range(8):
                STT(dst.a[:, k, 0:N], x.a[:, k, 0:N], lvec[0].a[:, VFG + k:VFG + k + 1], rstd.a[:, 0:N], MUL, MUL,
                    [x.T, lvec[0].T, rstd.T], [dst.T])
            DMA("sp", key, [(dst_dram, dst.a[:, :, 0:N])], [dst.T], [])

        def run_blocks(l, ctxs):
            gens = [block(l, c) for c in ctxs]
            active = []
            for g in gens:
                try:
                    next(g)
                    active.append(g)
                except StopIteration:
                    pass
            while active:
                slot = wq.get()
                nxt = []
                for g in active:
                    try:
                        g.send(slot)
                        nxt.append(g)
                    except StopIteration:
                        pass
                active = nxt

        cs_ctx = Ctx()
        cs_ctx.x, cs_ctx.N, cs_ctx.samp, cs_ctx.tile_idx = xs, NS, True, 0
        cs_ctx.hT, cs_ctx.pbr, cs_ctx.mg, cs_ctx.gb, cs_ctx.wk, cs_ctx.xcd, cs_ctx.xcb, cs_ctx.rsb = (
            hT_s, pbr_s, mg_s, gb_s, wk_s, xcd_s, xcb_s, rsb_s)

        def prompt_ctx(xbuf, tt):
            c = Ctx()
            c.x, c.N, c.samp, c.tile_idx = xbuf, TT, False, tt
            c.hT, c.pbr, c.mg, c.gb, c.wk, c.xcd, c.xcb, c.rsb = hT, pbr, mg, gbuf, wk, xcd, xcb, [rstd, gbuf[0], gbuf[1], gbuf[2]]
            return c

        DMA("sp", "xs", [(xs.a, xsT[:, :, :])], [], [xs.T])
        xcur = xt.next()
        DMA("sp", xcur.T.name, [(xcur.a, xT[:, :, 0:TT])], [], [xcur.T])
        for tt in range(NT):
            if tt + 1 < NT:
                xnext = xt.next()
                DMA("sp", xnext.T.name, [(xnext.a, xT[:, :, (tt + 1) * TT:(tt + 2) * TT])], [], [xnext.T])
            for l in range(L):
                ctxs = [prompt_ctx(xcur, tt)]
                if tt == 0:
                    DMA("sp", "ssc", [(ssc.a, sscf[l])], [], [ssc.T])
                    DMA("sp", "srgc", [(srgc.a, srgcf[l])], [], [srgc.T])
                    DMA("sp", "srgh", [(srgh.a, srghf[l])], [], [srgh.T])
                    ctxs.append(cs_ctx)
                for c_ in ctxs:
                    c_.defer = (len(ctxs) == 1)
                run_blocks(l, ctxs)
            final_norm(xcur, TT, xcur, yT[:, :, tt * TT:(tt + 1) * TT], xcur.T.name)
            if tt == 0:
                final_norm(xs, NS, xs, ysT[:, :, :], "xs")
            if tt + 1 < NT:
                xcur = xnext
        assert STAGE < 9 or wq.pos == len(wq.plan), (wq.pos, len(wq.plan))
        outs = [b.T for b in xt.bufs] + [xs.T, nsc.T, nrgc.T, nrgh.T] + [b.T for b in Snw]
        for l in range(L):
            outs += [csc[l].T, crg[l].T, chh[l].T] + [S[l][h].T for h in range(4)]
        P.wait_all("sp", outs)
        P.emit(st)
    return nc


_CACHE = {}


def _fm(v):
    return np.ascontiguousarray(np.swapaxes(v.reshape(v.shape[:-1] + (8, 128)), -1, -2))


def kernel(x_prompt, x_sample, c_prompt, c_sample, state_gla, state_sc_conv, state_rg_conv, state_rg_h,
           w_ada, b_ada, norm_gain, w_in, gla_w_alpha, gla_b_alpha, gla_norm_gain, gla_w_branch, sc_conv_w,
           sc_w_branch, rg_conv_w, rg_conv_b, rg_w_a, rg_b_a, rg_w_x, rg_b_x, rg_lambda, rg_w_branch,
           b_merge, w_out, final_gain):
    f = lambda a: np.ascontiguousarray(np.asarray(a, dtype=np.float32))
    x_prompt, x_sample, c_prompt, c_sample = f(x_prompt), f(x_sample), f(c_prompt), f(c_sample)
    state_gla, state_sc_conv, state_rg_conv, state_rg_h = f(state_gla), f(state_sc_conv), f(state_rg_conv), f(state_rg_h)
    n = 8
    if "nc" not in _CACHE:
        _CACHE["nc"] = build_program()
    nc = _CACHE["nc"]
    lvec = np.zeros((L, 128, NV), np.float32)
    for l in range(L):
        lvec[l, :, VNG:VNG + 8] = _fm(f(norm_gain)[l])
        lvec[l, :, VBADA:VBADA + 24] = f(b_ada)[l].reshape(24, 128).T
        lvec[l, :, VGNG:VGNG + 2] = f(gla_norm_gain)[l].reshape(2, 128).T
        lvec[l, :, VSCW:VSCW + 24] = f(sc_conv_w)[l].reshape(3, 8, 128).transpose(2, 1, 0).reshape(128, 24)
        lvec[l, :, VRGW:VRGW + 32] = f(rg_conv_w)[l].reshape(4, 8, 128).transpose(2, 1, 0).reshape(128, 32)
        lvec[l, :, VRGCB:VRGCB + 8] = _fm(f(rg_conv_b)[l])
        lvec[l, :, VRGBA:VRGBA + 8] = _fm(f(rg_b_a)[l])
        lvec[l, :, VRGBX:VRGBX + 8] = _fm(f(rg_b_x)[l])
        lvec[l, :, VLAM:VLAM + 8] = _fm(f(rg_lambda)[l])
        lvec[l, :, VBM:VBM + 24] = f(b_merge)[l].reshape(24, 128).T
        lvec[l, :, VFG:VFG + 8] = _fm(f(final_gain))
    gwa = np.concatenate([f(gla_w_alpha), f(gla_b_alpha)[:, None, :]], axis=1)
    s_idx = np.arange(128)[:, None]
    t_idx = np.arange(128)[None, :]
    consts = np.concatenate([
        np.eye(128, dtype=np.float32),
        np.ones((128, 128), np.float32),
        np.where(s_idx <= t_idx, -1.0 / 16.0, 0.0).astype(np.float32),
        np.where(s_idx > t_idx, -1.0 / 16.0, 0.0).astype(np.float32),
        np.where(s_idx <= t_idx, 1.0, 0.0).astype(np.float32),
    ], axis=1)
    shared = {
        "w_ada": f(w_ada), "w_in": f(w_in), "wgla": f(gla_w_branch), "wsc": f(sc_w_branch), "wrg": f(rg_w_branch),
        "w_out": f(w_out), "rgwa": f(rg_w_a), "rgwx": f(rg_w_x), "gwa": np.ascontiguousarray(gwa),
        "lvec": lvec, "consts": np.ascontiguousarray(consts),
    }
    in_maps = []
    for c in range(n):
        sl = slice(c * NS, (c + 1) * NS)
        m = dict(shared)
        m["xT"] = np.ascontiguousarray(x_prompt[c].reshape(2048, 8, 128).transpose(2, 1, 0))
        m["xsT"] = np.ascontiguousarray(x_sample[sl, 0, :].reshape(NS, 8, 128).transpose(2, 1, 0))
        cc = np.concatenate([c_prompt[c:c + 1], c_sample[sl]], axis=0)
        m["cT"] = np.ascontiguousarray(cc.reshape(1 + NS, 8, 128).transpose(2, 1, 0))
        m["sgla"] = np.ascontiguousarray(state_gla[:, sl])
        m["sscf"] = np.ascontiguousarray(state_sc_conv[:, sl].reshape(L, NS, 2, 8, 128).transpose(0, 4, 3, 2, 1))
        m["srgcf"] = np.ascontiguousarray(state_rg_conv[:, sl].reshape(L, NS, 3, 8, 128).transpose(0, 4, 3, 2, 1))
        m["srghf"] = np.ascontiguousarray(state_rg_h[:, sl].reshape(L, NS, 8, 128).transpose(0, 3, 2, 1))
        in_maps.append(m)
    if os.environ.get("KCORES"):
        n1 = int(os.environ["KCORES"])
        res = run_bass_kernel_spmd(nc, in_maps[:n1], core_ids=list(range(n1)), trace=bool(os.environ.get("KTRACE")))
        return res
    res = run_bass_kernel_spmd(nc, in_maps, core_ids=list(range(n)))
    R = res.results
    B = 8
    y_prompt = np.stack([R[c]["yT"].transpose(2, 1, 0).reshape(2048, D) for c in range(n)], axis=0)
    y_sample = np.concatenate([R[c]["ysT"].transpose(2, 1, 0).reshape(NS, 1, D) for c in range(n)], axis=0)
    gla_p = np.stack([R[c]["ogla_p"] for c in range(n)], axis=1)
    sc_p = np.stack([R[c]["osc_p"].transpose(0, 3, 2, 1).reshape(L, 2, D) for c in range(n)], axis=1)
    rgc_p = np.stack([R[c]["orgc_p"].transpose(0, 3, 2, 1).reshape(L, 3, D) for c in range(n)], axis=1)
    rgh_p = np.stack([R[c]["orgh_p"].transpose(0, 2, 1).reshape(L, D) for c in range(n)], axis=1)
    gla_s = np.concatenate([R[c]["ogla_s"] for c in range(n)], axis=1)
    sc_s = np.concatenate([R[c]["osc_s"].transpose(0, 4, 3, 2, 1).reshape(L, NS, 2, D) for c in range(n)], axis=1)
    rgc_s = np.concatenate([R[c]["orgc_s"].transpose(0, 4, 3, 2, 1).reshape(L, NS, 3, D) for c in range(n)], axis=1)
    rgh_s = np.concatenate([R[c]["orgh_s"].transpose(0, 3, 2, 1).reshape(L, NS, D) for c in range(n)], axis=1)
    outs = (y_prompt, y_sample, gla_p, sc_p, rgc_p, rgh_p, gla_s, sc_s, rgc_s, rgh_s)
    return tuple(np.ascontiguousarray(o.astype(np.float32)) for o in outs)
```
